# Optimizing a Trainium2 kernel written in Bass

```python
import jax, jax.numpy as jnp
from jax import lax
import numpy as np

D_MODEL = 4096
BATCH = 2
SEQ = 4096
DEPTH = 1

CTX_LEN = 256
GRID_W = 64
HG_HEADS = 16
HG_DIM = 128
HG_WIDTH = HG_HEADS * HG_DIM
RET_HEADS = 16
RET_DK = 128
RET_DV = 128
RET_QK_WIDTH = RET_HEADS * RET_DK
RET_V_WIDTH = RET_HEADS * RET_DV
RET_DECAY_EXP_MIN = 5.0
RET_DECAY_EXP_MAX = 12.0
D_FF = 4 * D_MODEL
CHUNK = 64
ROPE_BASE = 10000.0
EPS = 1e-6
ADALN_SCALE = 0.3
STATE_SPLITS = (HG_WIDTH, HG_WIDTH, HG_WIDTH, RET_QK_WIDTH, RET_V_WIDTH)
READ_SPLITS = (HG_WIDTH, HG_WIDTH, RET_QK_WIDTH, RET_V_WIDTH, D_MODEL, D_MODEL)
N_STATE_COLS = sum(STATE_SPLITS)
N_IN_COLS = N_STATE_COLS + sum(READ_SPLITS)

kernel_name = "hybrid_hgrn2_retention_dit_block"

F32 = jnp.float32


def _rmsnorm(x, gain):
    xf = x.astype(F32)
    y = xf * lax.rsqrt(jnp.mean(xf * xf, axis=-1, keepdims=True) + EPS)
    return (y * gain.astype(F32)).astype(x.dtype)


def _head_rmsnorm(o, gain):
    return o * lax.rsqrt(jnp.mean(o * o, axis=-1, keepdims=True) + EPS) * gain.astype(F32)


def _head_layernorm(o):
    mu = jnp.mean(o, axis=-1, keepdims=True)
    oc = o - mu
    return oc * lax.rsqrt(jnp.mean(oc * oc, axis=-1, keepdims=True) + EPS)


def _split_cols(a, sizes):
    out, off = [], 0
    for s in sizes:
        out.append(a[..., off:off + s])
        off += s
    return out


def _split_heads(a, n_heads):
    b, t, w = a.shape
    return a.reshape(b, t, n_heads, w // n_heads).transpose(0, 2, 1, 3)


def _merge_heads(a):
    b, h, t, d = a.shape
    return a.transpose(0, 2, 1, 3).reshape(b, t, h * d)


def _flip(a):
    return None if a is None else jnp.flip(a, axis=2)


def _rope_2d(t_len):
    rows = t_len // GRID_W
    pos = jnp.arange(rows * GRID_W)
    row = (pos // GRID_W).astype(F32)
    col = (pos % GRID_W).astype(F32)
    n_freq = RET_DK // 4
    inv_freq = ROPE_BASE ** (-jnp.arange(n_freq, dtype=F32) / n_freq)
    ang = jnp.concatenate([row[:, None] * inv_freq, col[:, None] * inv_freq], axis=-1)
    return jnp.cos(ang), jnp.sin(ang)


def _apply_rope(a, cos, sin):
    a1, a2 = a[..., 0::2], a[..., 1::2]
    return jnp.stack([a1 * cos - a2 * sin, a1 * sin + a2 * cos], axis=-1).reshape(a.shape)


def _chunk_scan(q, k, v, log_f, s0):
    b, h, t, _ = k.shape
    n = t // CHUNK

    def to_chunks(a):
        return jnp.moveaxis(a.astype(F32).reshape(b, h, n, CHUNK, a.shape[-1]), 2, 0)

    kc, vc, fc = to_chunks(k), to_chunks(v), to_chunks(log_f)

    def update(state, ki, vi, cum):
        last = cum[..., -1:, :]
        k_dec = ki * jnp.exp(last - cum)
        return jnp.exp(last[..., 0, :])[..., None] * state + jnp.einsum('bhck,bhcv->bhkv', k_dec, vi)

    if q is None:
        def step_state(state, inp):
            ki, vi, fi = inp
            return update(state, ki, vi, jnp.cumsum(fi, axis=-2)), None
        s_final, _ = lax.scan(step_state, s0, (kc, vc, fc))
        return None, s_final

    qc = to_chunks(q)
    mask = jnp.tril(jnp.ones((CHUNK, CHUNK), dtype=bool))[:, :, None]

    def step(state, inp):
        qi, ki, vi, fi = inp
        cum = jnp.cumsum(fi, axis=-2)
        diff = jnp.where(mask, cum[..., :, None, :] - cum[..., None, :, :], -jnp.inf)
        decay = jnp.exp(diff)
        if fi.shape[-1] == 1:
            scores = jnp.einsum('bhtk,bhsk->bhts', qi, ki) * decay[..., 0]
        else:
            scores = jnp.einsum('bhtk,bhsk,bhtsk->bhts', qi, ki, decay)
        o = (jnp.einsum('bhck,bhkv->bhcv', qi * jnp.exp(cum), state)
             + jnp.einsum('bhts,bhsv->bhtv', scores, vi))
        return update(state, ki, vi, cum), o

    s_final, oc = lax.scan(step, s0, (qc, kc, vc, fc))
    return jnp.moveaxis(oc, 0, 2).reshape(b, h, t, -1), s_final


def _prefix_bidirectional_scan(q_ctx, k_ctx, v_ctx, lf_ctx, q_lat, k_lat, v_lat, lf_lat):
    b, h, _, dk = k_lat[0].shape
    dv = v_lat.shape[-1]
    zero = jnp.zeros((b, h, dk, dv), F32)
    oc_f, sc_f = _chunk_scan(q_ctx, k_ctx[0], v_ctx, lf_ctx[0], zero)
    oc_b, sc_b = _chunk_scan(_flip(q_ctx), _flip(k_ctx[1]), _flip(v_ctx), _flip(lf_ctx[1]), zero)
    ox_f, _ = _chunk_scan(q_lat, k_lat[0], v_lat, lf_lat[0], sc_f)
    ox_b, _ = _chunk_scan(_flip(q_lat), _flip(k_lat[1]), _flip(v_lat), _flip(lf_lat[1]), sc_b)
    o_ctx = None if q_ctx is None else oc_f + _flip(oc_b)
    return o_ctx, ox_f + _flip(ox_b)


def _state_inputs(p, lb, ret_log_decay, rope):
    hg_i, hg_f_fwd, hg_f_bwd, ret_k, ret_v = _split_cols(p, STATE_SPLITS)
    b, t, _ = p.shape

    def hgrn_gates(z, lbd):
        f = lbd + (1.0 - lbd) * jax.nn.sigmoid(z.astype(F32))
        return _split_heads(1.0 - f, HG_HEADS), _split_heads(jnp.log(f), HG_HEADS)

    k_f, lf_f = hgrn_gates(hg_f_fwd, lb[0])
    k_b, lf_b = hgrn_gates(hg_f_bwd, lb[1])
    rk = _split_heads(ret_k.astype(F32), RET_HEADS) * RET_DK ** -0.5
    if rope is not None:
        rk = _apply_rope(rk, *rope)
    rlf = [jnp.broadcast_to(ret_log_decay[d][None, :, None, None], (b, RET_HEADS, t, 1)) for d in range(2)]
    return dict(hg_i=_split_heads(hg_i, HG_HEADS), hg_k=(k_f, k_b), hg_lf=(lf_f, lf_b),
                ret_k=(rk, rk), ret_v=_split_heads(ret_v, RET_HEADS), ret_lf=(rlf[0], rlf[1]))


def _read_inputs(p, rope):
    hg_q, hg_g, ret_q, ret_g, gate_hgrn, gate_ret = _split_cols(p, READ_SPLITS)
    q_h = _split_heads(jax.nn.silu(hg_q.astype(F32)) * HG_DIM ** -0.5, HG_HEADS)
    q_r = _split_heads(ret_q.astype(F32), RET_HEADS)
    if rope is not None:
        q_r = _apply_rope(q_r, *rope)
    return dict(hg_q=q_h, hg_g=hg_g, ret_q=q_r, ret_g=ret_g, gate_hgrn=gate_hgrn, gate_ret=gate_ret)


def _mixer_output(hg_o, ret_o, rd, hg_gain, w_bh, w_br, w_o, dtype):
    y_hg = _merge_heads(_head_rmsnorm(hg_o, hg_gain)).astype(dtype) * jax.nn.silu(rd['hg_g'])
    y_ret = _merge_heads(_head_layernorm(ret_o)).astype(dtype) * jax.nn.silu(rd['ret_g'])
    merged = (jax.nn.sigmoid(rd['gate_hgrn']) * (y_hg @ w_bh)
              + jax.nn.sigmoid(rd['gate_ret']) * (y_ret @ w_br))
    return merged @ w_o


def _sq_relu_mlp(h, w1, w2):
    return jnp.square(jax.nn.relu(h @ w1)) @ w2


def setup_inputs(seed: int = 0) -> dict:
    key = jax.random.key(seed)
    ks = jax.random.split(key, 18)

    def nrm(k, shape, scale):
        return jax.random.normal(k, shape, F32) * scale

    e = np.linspace(RET_DECAY_EXP_MIN, RET_DECAY_EXP_MAX, RET_HEADS).astype(np.float32)
    base_logit = jnp.asarray(np.log(np.exp2(e) - 1.0).astype(np.float32))
    return {
        "x": nrm(ks[0], (BATCH, SEQ, D_MODEL), 1.0),
        "c": nrm(ks[1], (BATCH, D_MODEL), 1.0),
        "ctx": nrm(ks[2], (BATCH, CTX_LEN, D_MODEL), 1.0),
        "c_ctx": nrm(ks[3], (D_MODEL,), 1.0),
        "w_mod": nrm(ks[4], (DEPTH, D_MODEL, 6 * D_MODEL), ADALN_SCALE * D_MODEL ** -0.5),
        "b_mod": nrm(ks[5], (DEPTH, 6 * D_MODEL), 0.02),
        "norm1_g": 1.0 + nrm(ks[6], (DEPTH, D_MODEL), 0.02),
        "norm2_g": 1.0 + nrm(ks[7], (DEPTH, D_MODEL), 0.02),
        "w_in": nrm(ks[8], (DEPTH, D_MODEL, N_IN_COLS), D_MODEL ** -0.5),
        "hg_lb_logits": nrm(ks[9], (DEPTH + 1, 2, HG_WIDTH), 0.1),
        "hg_norm_g": 1.0 + nrm(ks[10], (DEPTH, HG_DIM), 0.02),
        "ret_decay_logit": base_logit + nrm(ks[11], (DEPTH, 2, RET_HEADS), 0.1),
        "w_branch_hgrn": nrm(ks[12], (DEPTH, HG_WIDTH, D_MODEL), HG_WIDTH ** -0.5),
        "w_branch_ret": nrm(ks[13], (DEPTH, RET_V_WIDTH, D_MODEL), RET_V_WIDTH ** -0.5),
        "w_out": nrm(ks[14], (DEPTH, D_MODEL, D_MODEL), D_MODEL ** -0.5),
        "w_ff1": nrm(ks[15], (DEPTH, D_MODEL, D_FF), D_MODEL ** -0.5),
        "w_ff2": nrm(ks[16], (DEPTH, D_FF, D_MODEL), D_FF ** -0.5),
        "final_norm_g": 1.0 + nrm(ks[17], (D_MODEL,), 0.02),
    }


def reference(x, c, ctx, c_ctx, w_mod, b_mod, norm1_g, norm2_g, w_in, hg_lb_logits, hg_norm_g,
              ret_decay_logit, w_branch_hgrn, w_branch_ret, w_out, w_ff1, w_ff2, final_norm_g):
    rope = _rope_2d(x.shape[1])
    lower_bounds = jnp.cumsum(jax.nn.softmax(hg_lb_logits.astype(F32), axis=0), axis=0)[:DEPTH]
    ret_log_decay = jax.nn.log_sigmoid(ret_decay_logit.astype(F32))
    silu_c = jax.nn.silu(c)
    silu_cc = jax.nn.silu(c_ctx)

    for layer in range(DEPTH):
        keep_ctx = layer + 1 < DEPTH
        mod_x = (silu_c @ w_mod[layer] + b_mod[layer])[:, None, :]
        mod_c = silu_cc @ w_mod[layer] + b_mod[layer]
        shift1_x, scale1_x, gate1_x, shift2_x, scale2_x, gate2_x = jnp.split(mod_x, 6, axis=-1)
        shift1_c, scale1_c, gate1_c, shift2_c, scale2_c, gate2_c = jnp.split(mod_c, 6, axis=-1)

        h_x = _rmsnorm(x, norm1_g[layer]) * (1.0 + scale1_x) + shift1_x
        h_c = _rmsnorm(ctx, norm1_g[layer]) * (1.0 + scale1_c) + shift1_c
        p_x = h_x @ w_in[layer]
        p_c = h_c @ (w_in[layer] if keep_ctx else w_in[layer][:, :N_STATE_COLS])
        st_x = _state_inputs(p_x[..., :N_STATE_COLS], lower_bounds[layer], ret_log_decay[layer], rope)
        st_c = _state_inputs(p_c[..., :N_STATE_COLS], lower_bounds[layer], ret_log_decay[layer], None)
        rd_x = _read_inputs(p_x[..., N_STATE_COLS:], rope)
        rd_c = _read_inputs(p_c[..., N_STATE_COLS:], None) if keep_ctx else None

        hg_ctx, hg_lat = _prefix_bidirectional_scan(
            None if rd_c is None else rd_c['hg_q'], st_c['hg_k'], st_c['hg_i'], st_c['hg_lf'],
            rd_x['hg_q'], st_x['hg_k'], st_x['hg_i'], st_x['hg_lf'])
        ret_ctx, ret_lat = _prefix_bidirectional_scan(
            None if rd_c is None else rd_c['ret_q'], st_c['ret_k'], st_c['ret_v'], st_c['ret_lf'],
            rd_x['ret_q'], st_x['ret_k'], st_x['ret_v'], st_x['ret_lf'])

        out_x = _mixer_output(hg_lat, ret_lat, rd_x, hg_norm_g[layer], w_branch_hgrn[layer],
                              w_branch_ret[layer], w_out[layer], x.dtype)
        x = x + gate1_x * out_x
        h2_x = _rmsnorm(x, norm2_g[layer]) * (1.0 + scale2_x) + shift2_x
        x = x + gate2_x * _sq_relu_mlp(h2_x, w_ff1[layer], w_ff2[layer])

        if keep_ctx:
            out_c = _mixer_output(hg_ctx, ret_ctx, rd_c, hg_norm_g[layer], w_branch_hgrn[layer],
                                  w_branch_ret[layer], w_out[layer], ctx.dtype)
            ctx = ctx + gate1_c * out_c
            h2_c = _rmsnorm(ctx, norm2_g[layer]) * (1.0 + scale2_c) + shift2_c
            ctx = ctx + gate2_c * _sq_relu_mlp(h2_c, w_ff1[layer], w_ff2[layer])

    return _rmsnorm(x, final_norm_g)
```

```python
import numpy as np
from contextlib import ExitStack
import ml_dtypes
import concourse.bass as bass
import concourse.mybir as mybir
from concourse.bass_utils import run_bass_kernel_spmd

F32 = mybir.dt.float32
BF16 = mybir.dt.bfloat16
AF = mybir.ActivationFunctionType
ALU = mybir.AluOpType
AX = mybir.AxisListType

D = 4096
NT_ALL = 34
OWN0 = 26
EPS = 1e-6
DEBUG = False


class Buf:
    __slots__ = ("name", "w", "r", "dsem", "dcnt")

    def __init__(self, name):
        self.name = name
        self.w = None
        self.r = []
        self.dsem = None
        self.dcnt = 0


class Sched:
    ENG = ("pe", "act", "dve", "pool", "sp")

    def __init__(self, nc, es):
        self.nc = nc
        self.es = es
        self.streams = {e: [] for e in self.ENG}
        self.sems = {}
        self.cnt = {e: 0 for e in self.ENG}
        self.known = {e: {} for e in self.ENG}
        for e in self.ENG:
            self.sems[e] = es.enter_context(nc.semaphore("s_" + e))
        self.nd = 0
        self.dbufs = []

    def _dsem(self, buf):
        if buf.dsem is None:
            key = "d%d" % self.nd
            self.nd += 1
            self.sems[key] = self.es.enter_context(self.nc.semaphore(key))
            buf.dsem = key
            self.dbufs.append(buf)
        return buf.dsem

    def _need(self, eng, need):
        kn = self.known[eng]
        for k, v in need.items():
            if k == eng and v > self.cnt[eng]:
                continue
            if kn.get(k, 0) < v:
                kn[k] = v
                sem = self.sems[k]
                self.streams[eng].append(lambda e, sem=sem, v=v: e.wait_ge(sem, v))

    def _waits(self, eng, reads, writes):
        need = {}
        for b in reads:
            if b.w is not None:
                k, v = b.w
                need[k] = max(need.get(k, 0), v)
        for b in writes:
            if b.w is not None:
                k, v = b.w
                need[k] = max(need.get(k, 0), v)
            for (k, v) in b.r:
                need[k] = max(need.get(k, 0), v)
        self._need(eng, need)

    def _commit(self, ev, reads, writes):
        for b in reads:
            if len(b.r) > 64:
                m = {}
                for (k, v) in b.r:
                    m[k] = max(m.get(k, 0), v)
                b.r = list(m.items())
            b.r.append(ev)
        for b in writes:
            b.w = ev
            b.r = []

    def op(self, eng, fn, reads=(), writes=()):
        self._waits(eng, reads, writes)
        self.cnt[eng] += 1
        ev = (eng, self.cnt[eng])
        sem = self.sems[eng]
        self.streams[eng].append(lambda e, fn=fn, sem=sem: fn(e).then_inc(sem, 1))
        self._commit(ev, reads, writes)

    def op_noinc(self, eng, fn, reads=(), writes=()):
        self._waits(eng, reads, writes)
        ev = (eng, self.cnt[eng] + 1)
        self.streams[eng].append(lambda e, fn=fn: fn(e))
        self._commit(ev, reads, writes)

    def dma(self, q, out_ap, in_ap, sb, reads=(), writes=()):
        self._waits(q, reads, writes)
        key = self._dsem(sb)
        sb.dcnt += 16
        ev = (key, sb.dcnt)
        sem = self.sems[key]
        self.streams[q].append(
            lambda e, o=out_ap, i=in_ap, sem=sem: e.dma_start(out=o, in_=i).then_inc(sem, 16))
        self._commit(ev, reads, writes)

    def barrier(self):
        need = {e: self.cnt[e] for e in self.ENG if self.cnt[e] > 0}
        for b in self.dbufs:
            if b.dcnt > 0:
                need[b.dsem] = b.dcnt
        for e in self.ENG:
            self._need(e, dict(need))

    def emit(self):
        nc = self.nc
        st = self.streams
        with nc.Block() as block:
            @block.tensor
            def _(e):
                for f in st["pe"]:
                    f(e)

            @block.scalar
            def _(e):
                for f in st["act"]:
                    f(e)

            @block.vector
            def _(e):
                for f in st["dve"]:
                    f(e)

            @block.gpsimd
            def _(e):
                for f in st["pool"]:
                    f(e)

            @block.sync
            def _(e):
                for f in st["sp"]:
                    f(e)


def build_program():
    nc = bass.Bass("TRN2", target_bir_lowering=False)

    def din(name, shape, dt=F32):
        return nc.dram_tensor(name, list(shape), dt, kind="ExternalInput").ap()

    xo = din("xo", [1024, D])
    xf = din("xf", [3328, D])
    cc = din("cc", [128, 32, 2])
    w_mod = din("w_mod", [D, 6 * D])
    b_modT = din("b_modT", [128, 192])
    n1gT = din("n1gT", [128, 32])
    n2gT = din("n2gT", [128, 32])
    fg_bc = din("fg_bc", [128, D])
    w_in = din("w_in", [D, 26624])
    lbl = din("lbl", [128, 2, 2, 2048])
    hgain = din("hgain", [128, 128])
    rdl = din("rdl", [128, 2, 16])
    w_bh = din("w_bh", [2048, D])
    w_br = din("w_br", [2048, D])
    w_o = din("w_o", [D, D])
    w1 = din("w1", [D, 4 * D])
    w2 = din("w2", [4 * D, D])
    c_ident = din("c_ident", [128, 128])
    c_le = din("c_le", [128, 128])
    c_gt = din("c_gt", [128, 128])
    c_ge = din("c_ge", [128, 128])
    c_lt = din("c_lt", [128, 128])
    c_ones = din("c_ones", [128, 128])
    ropeK = din("ropeK", [128, NT_ALL, 2, 64])
    ropeQ = din("ropeQ", [128, 8, 2, 64])
    mk_d = din("mk", [128, NT_ALL, 2])
    selm_d = din("selm", [128, NT_ALL, 2, 2])
    out = nc.dram_tensor("out", [1024, D], F32, kind="ExternalOutput").ap()

    def dscr(name, shape, dt):
        if DEBUG and name in ("X1", "MT", "YT", "MODT"):
            return nc.dram_tensor(name, list(shape), dt, kind="ExternalOutput").ap()
        return nc.dram_tensor(name, list(shape), dt).ap()

    ZF = dscr("ZF", [16, 128, NT_ALL, 128], F32)
    ZB = dscr("ZB", [16, 128, NT_ALL, 128], F32)
    VH = dscr("VH", [16, 128, NT_ALL, 128], BF16)
    RK = dscr("RK", [16, 128, NT_ALL, 128], BF16)
    RV = dscr("RV", [16, 128, NT_ALL, 128], BF16)
    HQ = dscr("HQ", [16, 128, 8, 128], BF16)
    HG = dscr("HG", [16, 128, 8, 128], BF16)
    RQ = dscr("RQ", [16, 128, 8, 128], BF16)
    RG = dscr("RG", [16, 128, 8, 128], BF16)
    GT = dscr("GT", [64, 128, 1024], BF16)
    X1 = dscr("X1", [1024, D], F32)
    MT = dscr("MT", [32, 128, 1024], BF16)
    if DEBUG:
        YT = dscr("YT", [128, 32, 1024], BF16)
        MODT = dscr("MODT", [128, 192, 2], F32)
    scrB = {id(t): Buf("scr%d" % i) for i, t in enumerate([ZF, ZB, VH, RK, RV, HQ, HG, RQ, RG, GT, X1, MT])}

    with ExitStack() as es:
        S = Sched(nc, es)

        sbn = [0]

        def sb(name, shape, dt, stack=None):
            sbn[0] += 1
            t = (stack or es).enter_context(nc.sbuf_tensor("sb%d_%s" % (sbn[0], name), list(shape), dt))
            return t, Buf(name)

        banks = []
        for i in range(8):
            t = es.enter_context(nc.psum_tensor("pb%d" % i, [128, 512], F32))
            banks.append((t, Buf("pb%d" % i)))
        bank_i = [0]

        def nb():
            b = banks[bank_i[0] % 8]
            bank_i[0] += 1
            return b

        def mm(outap, lhsT, rhs, pbuf, reads, first, last):
            fn = lambda e: e.matmul(outap, lhsT=lhsT, rhs=rhs, start=first, stop=last)
            if last:
                S.op("pe", fn, reads=reads, writes=[pbuf])
            else:
                S.op_noinc("pe", fn, reads=reads, writes=[pbuf] if first else [])

        def tr(outap, inap, ident, pbuf, reads):
            S.op("pe", lambda e: e.transpose(outap, inap, ident), reads=reads, writes=[pbuf])

        ident, identB = sb("ident", [128, 128], F32)
        m_le, m_leB = sb("m_le", [128, 128], F32)
        m_gt, m_gtB = sb("m_gt", [128, 128], F32)
        m_ge, m_geB = sb("m_ge", [128, 128], F32)
        m_lt, m_ltB = sb("m_lt", [128, 128], F32)
        ones, onesB = sb("ones", [128, 128], F32)
        for t, b, src in ((ident, identB, c_ident), (m_le, m_leB, c_le), (m_gt, m_gtB, c_gt),
                          (m_ge, m_geB, c_ge), (m_lt, m_ltB, c_lt), (ones, onesB, c_ones)):
            S.dma("sp", t[:], src, b, writes=[b])
        modT, modTB = sb("modT", [128, 192, 2], F32)
        G1, G1B = sb("G1", [128, 32, 2], F32)
        G2, G2B = sb("G2", [128, 32], F32)
        n1g, n1gB = sb("n1g", [128, 32], F32)
        n2g, n2gB = sb("n2g", [128, 32], F32)
        bmod, bmodB = sb("bmod", [128, 192], F32)
        S.dma("sp", n1g[:], n1gT, n1gB, writes=[n1gB])
        S.dma("sp", n2g[:], n2gT, n2gB, writes=[n2gB])
        S.dma("sp", bmod[:], b_modT, bmodB, writes=[bmodB])
        mk, mkB = sb("mk", [128, NT_ALL, 2], F32)
        selm, selmB = sb("selm", [128, NT_ALL, 2, 2], F32)
        S.dma("sp", mk[:], mk_d, mkB, writes=[mkB])
        S.dma("sp", selm[:], selm_d, selmB, writes=[selmB])
        lg, lgB = sb("lg", [128, 2, 16], F32)
        S.dma("sp", lg[:], rdl, lgB, writes=[lgB])
        S.op("act", lambda e: e.activation(out=lg[:], in_=lg[:], func=AF.Sigmoid), reads=[lgB], writes=[lgB])
        S.op("act", lambda e: e.activation(out=lg[:], in_=lg[:], func=AF.Ln), reads=[lgB], writes=[lgB])

        with ExitStack() as ph:
            ccs, ccsB = sb("ccs", [128, 32, 2], F32, ph)
            ccb, ccbB = sb("ccb", [128, 32, 2], BF16, ph)
            sg0, sg0B = sb("sg0", [128, 32, 2], F32, ph)
            S.dma("sp", ccs[:], cc, ccsB, writes=[ccsB])
            S.op("act", lambda e: e.activation(out=sg0[:], in_=ccs[:], func=AF.Sigmoid), reads=[ccsB], writes=[sg0B])
            S.op("dve", lambda e: e.tensor_tensor(out=ccb[:], in0=ccs[:], in1=sg0[:], op=ALU.mult),
                 reads=[ccsB, sg0B], writes=[ccbB])
            wm = [sb("wm%d" % i, [128, 32, 512], BF16, ph) for i in range(2)]
            wv = w_mod.rearrange("(kc p) n -> p kc n", p=128)
            pm, pmB = nb()
            pmv = pm[:, 0:384].rearrange("p (n j) -> p n j", j=2)
            NB = 48
            S.dma("pool", wm[0][0][:], wv[:, :, 0:512], wm[0][1], writes=[wm[0][1]])
            for g in range(NB):
                wt, wB = wm[g % 2]
                if g + 1 < NB:
                    S.dma("pool", wm[(g + 1) % 2][0][:], wv[:, :, (g + 1) * 512:(g + 2) * 512],
                          wm[(g + 1) % 2][1], writes=[wm[(g + 1) % 2][1]])
                for n in range(4):
                    nn = g * 4 + n
                    for kc in range(32):
                        mm(pmv[:, nn, :], wt[:, kc, n * 128:(n + 1) * 128], ccb[:, kc, :], pmB,
                           [wB, ccbB], kc == 0, kc == 31)
            for j in range(2):
                S.op("dve", lambda e, j=j: e.tensor_tensor(out=modT[:, :, j], in0=pmv[:, :, j], in1=bmod[:], op=ALU.add),
                     reads=[pmB, bmodB], writes=[modTB])
            for j in range(2):
                S.op("dve", lambda e, j=j: e.scalar_tensor_tensor(
                    out=G1[:, :, j], in0=modT[:, 32:64, j], scalar=1.0, in1=n1g[:], op0=ALU.add, op1=ALU.mult),
                    reads=[modTB, n1gB], writes=[G1B])
            S.op("dve", lambda e: e.scalar_tensor_tensor(
                out=G2[:], in0=modT[:, 128:160, 0], scalar=1.0, in1=n2g[:], op0=ALU.add, op1=ALU.mult),
                reads=[modTB, n2gB], writes=[G2B])
        S.barrier()
        if DEBUG:
            S.dma("sp", MODT, modT[:], modTB, reads=[modTB], writes=[])

        def build_bc(dst, dstB, colap_fn, colB):
            with ExitStack() as ph2:
                dg, dgB = sb("dg_" + dstB.name, [128, 128], F32, ph2)
                for kc in range(32):
                    S.op("dve", lambda e, kc=kc: e.tensor_scalar(
                        out=dg[:], in0=ident[:], scalar1=colap_fn(kc), scalar2=None, op0=ALU.mult),
                        reads=[identB, colB], writes=[dgB])
                    pt, pB = nb()
                    mm(pt[:, 0:128], ones[:], dg[:], pB, [onesB, dgB], True, True)
                    S.op("act", lambda e, kc=kc, pt=pt: e.activation(
                        out=dst[:, kc * 128:(kc + 1) * 128], in_=pt[:, 0:128], func=AF.Copy),
                        reads=[pB], writes=[dstB])
            S.barrier()

        def norm_T(ph, xsrc, nt, hT, hTB, Gcol, GB, SHcol, SHB, tag):
            xt, xtB = sb("xt_" + tag, [128, D], F32, ph)
            jk, jkB = sb("jk_" + tag, [128, D], BF16, ph)
            st, stB = sb("st_" + tag, [128, 2], F32, ph)
            for t in range(nt):
                S.dma("sp", xt[:], xsrc[t * 128:(t + 1) * 128, :], xtB, writes=[xtB])
                S.op("act", lambda e: e.activation(out=jk[:], in_=xt[:], func=AF.Square, accum_out=st[:, 0:1]),
                     reads=[xtB], writes=[jkB, stB])
                S.op("dve", lambda e: e.tensor_scalar(out=st[:, 1:2], in0=st[:, 0:1], scalar1=1.0 / D, scalar2=EPS,
                                                      op0=ALU.mult, op1=ALU.add), reads=[stB], writes=[stB])
                S.op("act", lambda e: e.activation(out=st[:, 1:2], in_=st[:, 1:2], func=AF.Ln), reads=[stB], writes=[stB])
                S.op("act", lambda e: e.activation(out=st[:, 1:2], in_=st[:, 1:2], func=AF.Exp, scale=-0.5), reads=[stB], writes=[stB])
                S.op("dve", lambda e: e.tensor_scalar(out=xt[:], in0=xt[:], scalar1=st[:, 1:2], scalar2=None,
                                                      op0=ALU.mult), reads=[stB, xtB], writes=[xtB])
                for k4 in range(8):
                    pt, pB = nb()
                    for q in range(4):
                        kc = k4 * 4 + q
                        tr(pt[:, q * 128:(q + 1) * 128], xt[:, kc * 128:(kc + 1) * 128], ident[:], pB, [xtB, identB])
                    for q in range(4):
                        kc = k4 * 4 + q
                        S.op("act", lambda e, kc=kc, q=q, pt=pt, t=t: e.activation(
                            out=hT[:, kc, t * 128:(t + 1) * 128], in_=pt[:, q * 128:(q + 1) * 128],
                            func=AF.Identity, scale=Gcol(kc), bias=SHcol(kc)),
                            reads=[pB, GB, SHB], writes=[hTB])

        FAMS = [("vh", 0, "copy", VH, BF16), ("zf", 2048, "z", ZF, F32), ("zb", 4096, "z", ZB, F32),
                ("rk", 6144, "ropeK", RK, BF16), ("rv", 8192, "copy", RV, BF16),
                ("hq", 10240, "siluc", HQ, BF16), ("hg", 12288, "silu", HG, BF16),
                ("rq", 14336, "ropeQ", RQ, BF16), ("rg", 16384, "silu", RG, BF16)]
        winv = w_in.rearrange("(kc p) n -> p kc n", p=128)

        def seg_pass(xsrc, nt, t0, own, modj, tag):
            with ExitStack() as ph:
                hT, hTB = sb("hT_" + tag, [128, 32, nt * 128], BF16, ph)
                with ExitStack() as ph1:
                    norm_T(ph1, xsrc, nt, hT, hTB, lambda kc: G1[:, kc, modj:modj + 1], G1B,
                           lambda kc: modT[:, kc, modj:modj + 1], modTB, tag)
                S.barrier()
                wb = [sb("wi%d_%s" % (i, tag), [128, 32, 512], BF16, ph) for i in range(2)]
                stg32, stg32B = sb("s32_" + tag, [128, 4, nt, 128], F32, ph)
                stg16, stg16B = sb("s16_" + tag, [128, 4, nt, 128], BF16, ph)
                tmp, tmpB = sb("tmp_" + tag, [128, 512], F32, ph)
                tmp2, tmp2B = sb("tmp2_" + tag, [128, 512], F32, ph)
                rK, rKB = sb("rK_" + tag, [128, nt, 2, 64], F32, ph)
                S.dma("sp", rK[:], ropeK[:, t0:t0 + nt], rKB, writes=[rKB])
                if own:
                    rQ, rQB = sb("rQ_" + tag, [128, 8, 2, 64], F32, ph)
                    S.dma("sp", rQ[:], ropeQ, rQB, writes=[rQB])
                    gst, gstB = sb("gst_" + tag, [128, 1024], BF16, ph)
                fams = FAMS if own else FAMS[:5]
                groups = []
                for (fn_, off, kind, scr, dt) in fams:
                    for g4 in range(4):
                        groups.append((off + g4 * 512, kind, scr, dt, g4))
                if own:
                    for gg in range(16):
                        groups.append((18432 + gg * 512, "gate", GT, BF16, gg))
                ng = len(groups)
                S.dma("pool", wb[0][0][:], winv[:, :, groups[0][0]:groups[0][0] + 512], wb[0][1], writes=[wb[0][1]])
                for gi, (coff, kind, scr, dt, g4) in enumerate(groups):
                    wt, wB = wb[gi % 2]
                    if gi + 1 < ng:
                        c2 = groups[gi + 1][0]
                        S.dma("pool", wb[(gi + 1) % 2][0][:], winv[:, :, c2:c2 + 512], wb[(gi + 1) % 2][1],
                              writes=[wb[(gi + 1) % 2][1]])
                    if kind == "gate":
                        for n in range(4):
                            for half in range(2):
                                pt, pB = nb()
                                for kc in range(32):
                                    mm(pt[:], wt[:, kc, n * 128:(n + 1) * 128], hT[:, kc, half * 512:(half + 1) * 512],
                                       pB, [wB, hTB], kc == 0, kc == 31)
                                S.op("act", lambda e, pt=pt, half=half: e.activation(
                                    out=gst[:, half * 512:(half + 1) * 512], in_=pt[:], func=AF.Sigmoid),
                                    reads=[pB], writes=[gstB])
                            S.dma("sp", GT[g4 * 4 + n], gst[:], gstB, reads=[gstB], writes=[scrB[id(GT)]])
                        continue
                    stg, stgB = (stg32, stg32B) if dt == F32 else (stg16, stg16B)
                    for t in range(nt):
                        pt, pB = nb()
                        for kc in range(32):
                            mm(pt[:], hT[:, kc, t * 128:(t + 1) * 128], wt[:, kc, :], pB, [wB, hTB], kc == 0, kc == 31)
                        pv = pt[:].rearrange("p (h c) -> p h c", h=4)
                        if kind in ("copy", "z"):
                            S.op("act", lambda e, pv=pv, t=t, stg=stg: e.activation(out=stg[:, :, t, :], in_=pv, func=AF.Copy),
                                 reads=[pB], writes=[stgB])
                        elif kind in ("silu", "siluc"):
                            S.op("act", lambda e, pt=pt: e.activation(out=tmp[:], in_=pt[:], func=AF.Sigmoid),
                                 reads=[pB], writes=[tmpB])
                            cst = (128.0 ** -0.5) if kind == "siluc" else 1.0
                            S.op("dve", lambda e, pv=pv, t=t, cst=cst, stg=stg: e.scalar_tensor_tensor(
                                out=stg[:, :, t, :], in0=pv, scalar=cst, in1=tmp[:].rearrange("p (h c) -> p h c", h=4),
                                op0=ALU.mult, op1=ALU.mult), reads=[pB, tmpB], writes=[stgB])
                        else:
                            rt, rtB = (rK, rKB) if kind == "ropeK" else (rQ, rQB)
                            S.op("act", lambda e, pt=pt: e.activation(out=tmp[:], in_=pt[:], func=AF.Copy),
                                 reads=[pB], writes=[tmpB])
                            tv = tmp[:].rearrange("p (h c two) -> p h c two", h=4, two=2)
                            t2v = tmp2[:].rearrange("p (h c two) -> p h c two", h=4, two=2)
                            sv = stg[:, :, t, :].rearrange("p h (c two) -> p h c two", two=2)
                            for h in range(4):
                                a1, a2 = tv[:, h, :, 0], tv[:, h, :, 1]
                                cs, sn = rt[:, t, 0, :], rt[:, t, 1, :]
                                b1, b2 = t2v[:, h, :, 0], t2v[:, h, :, 1]
                                S.op("dve", lambda e, a1=a1, cs=cs, b1=b1: e.tensor_tensor(out=b1, in0=a1, in1=cs, op=ALU.mult),
                                     reads=[tmpB, rtB], writes=[tmp2B])
                                S.op("dve", lambda e, a2=a2, sn=sn, b2=b2: e.tensor_tensor(out=b2, in0=a2, in1=sn, op=ALU.mult),
                                     reads=[tmpB, rtB], writes=[tmp2B])
                                S.op("dve", lambda e, b1=b1, b2=b2, o=sv[:, h, :, 0]: e.tensor_tensor(out=o, in0=b1, in1=b2, op=ALU.subtract),
                                     reads=[tmp2B], writes=[stgB])
                                S.op("dve", lambda e, a1=a1, sn=sn, b1=b1: e.tensor_tensor(out=b1, in0=a1, in1=sn, op=ALU.mult),
                                     reads=[tmpB, rtB, stgB], writes=[tmp2B])
                                S.op("dve", lambda e, a2=a2, cs=cs, b2=b2: e.tensor_tensor(out=b2, in0=a2, in1=cs, op=ALU.mult),
                                     reads=[tmpB, rtB], writes=[tmp2B])
                                S.op("dve", lambda e, b1=b1, b2=b2, o=sv[:, h, :, 1]: e.tensor_tensor(out=o, in0=b1, in1=b2, op=ALU.add),
                                     reads=[tmp2B], writes=[stgB])
                    ts0 = t0 if scr.shape[2] == NT_ALL else 0
                    S.dma("sp", scr[g4 * 4:(g4 + 1) * 4, :, ts0:ts0 + nt, :].rearrange("h p t c -> p h t c"),
                          stg[:], stgB, reads=[stgB], writes=[scrB[id(scr)]])
            S.barrier()

        seg_pass(xf[0:256], 2, 0, False, 1, "c")
        for f in range(3):
            seg_pass(xf[256 + f * 1024:256 + (f + 1) * 1024], 8, 2 + f * 8, False, 0, "f%d" % f)
        seg_pass(xo, 8, OWN0, True, 0, "o")

        yT_stack = ExitStack()
        yT, yTB = sb("yT", [128, 32, 1024], BF16, yT_stack)
        with ExitStack() as ph:
            NTA = NT_ALL
            zt = [sb("zt%d" % d, [128, NTA, 128], F32, ph) for d in range(2)]
            kt = [sb("kt%d" % d, [128, NTA, 128], BF16, ph) for d in range(2)]
            kd = [sb("kd%d" % d, [128, NTA, 128], BF16, ph) for d in range(2)]
            vt, vtB = sb("vt", [128, NTA, 128], BF16, ph)
            qt, qtB = sb("qt", [128, 8, 128], BF16, ph)
            gt_, gtB = sb("gt", [128, 8, 128], BF16, ph)
            gg, ggB = sb("gg", [128, 8, 128], F32, ph)
            lbt, lbtB = sb("lbt", [128, 2, 2, 128], F32, ph)
            lb, lbB = sb("lb", [128, 2, 128], F32, ph)
            oml, omlB = sb("oml", [128, 2, 128], F32, ph)
            hgn, hgnB = sb("hgn", [128, 128], F32, ph)
            S.dma("sp", hgn[:], hgain, hgnB, writes=[hgnB])
            dd = [sb("dd%d" % d, [128, NTA, 2], F32, ph) for d in range(2)]
            Sst = [sb("Sst%d" % d, [128, 128], F32, ph) for d in range(2)]
            Ssn = [sb("Ssn%d" % d, [128, 16, 128], BF16, ph) for d in range(2)]
            qe = [sb("qe%d" % d, [128, 128], F32, ph) for d in range(2)]
            ke = [sb("ke%d" % d, [128, 128], F32, ph) for d in range(2)]
            ee, eeB = sb("ee", [128, 512], F32, ph)
            qT = [sb("qT%d" % d, [128, 8, 128], BF16, ph) for d in range(2)]
            kT, kTB = sb("kT", [128, 128], BF16, ph)
            scT = [sb("scT%d" % d, [128, 8, 128], BF16, ph) for d in range(2)]
            osb, osbB = sb("osb", [128, 128], F32, ph)
            ysb, ysbB = sb("ysb", [128, 128], F32, ph)
            stt, sttB = sb("stt", [128, 8], F32, ph)
            bst, bstB = sb("bst", [128, 6], F32, ph)
            lgc, lgcB = sb("lgc", [128, 2], F32, ph)
            TRI_INC = [(m_le, m_leB), (m_ge, m_geB)]
            TRI_DEC = [(m_gt, m_gtB), (m_lt, m_ltB)]

            ee2, ee2B = sb("ee2", [128, 512], F32, ph)
            qe4, qe4B = sb("qe4", [128, 4, 128], F32, ph)
            ke4, ke4B = sb("ke4", [128, 4, 128], F32, ph)
            kT4, kT4B = sb("kT4", [128, 4, 128], BF16, ph)
            oall, oallB = sb("oall", [128, 8, 128], F32, ph)
            stall, stallB = sb("stall", [128, 3, 8], F32, ph)
            bst8, bst8B = sb("bst8", [128, 8, 6], F32, ph)
            mv8, mv8B = sb("mv8", [128, 8, 2], F32, ph)

            def bc3(ap2d, n):
                return ap2d.rearrange("p (o c) -> p o c", o=1).to_broadcast([128, n, 128])

            def scan_head(kind, h):
                hg = kind == "hg"
                if hg:
                    S.dma("sp", zt[0][0][:], ZF[h], zt[0][1], reads=[scrB[id(ZF)]], writes=[zt[0][1]])
                    S.dma("sp", zt[1][0][:], ZB[h], zt[1][1], reads=[scrB[id(ZB)]], writes=[zt[1][1]])
                    S.dma("sp", vt[:], VH[h], vtB, reads=[scrB[id(VH)]], writes=[vtB])
                    S.dma("sp", qt[:], HQ[h], qtB, reads=[scrB[id(HQ)]], writes=[qtB])
                    S.dma("sp", gt_[:], HG[h], gtB, reads=[scrB[id(HG)]], writes=[gtB])
                    S.dma("sp", lbt[:], lbl[:, :, :, h * 128:(h + 1) * 128], lbtB, writes=[lbtB])
                    S.op("dve", lambda e: e.tensor_tensor(out=lb[:], in0=lbt[:, 0], in1=lbt[:, 1], op=ALU.subtract),
                         reads=[lbtB], writes=[lbB])
                    S.op("act", lambda e: e.activation(out=lb[:], in_=lb[:], func=AF.Sigmoid), reads=[lbB], writes=[lbB])
                    S.op("dve", lambda e: e.tensor_scalar(out=oml[:], in0=lb[:], scalar1=-1.0, scalar2=1.0,
                                                          op0=ALU.mult, op1=ALU.add), reads=[lbB], writes=[omlB])
                    for d in range(2):
                        z, zB = zt[d]
                        k_, kB_ = kt[d]
                        S.op("act", lambda e, z=z: e.activation(out=z[:], in_=z[:], func=AF.Sigmoid), reads=[zB], writes=[zB])
                        S.op("dve", lambda e, z=z, d=d: e.tensor_tensor(
                            out=z[:], in0=z[:], in1=oml[:, d:d + 1, :].to_broadcast([128, NTA, 128]), op=ALU.mult),
                            reads=[zB, omlB], writes=[zB])
                        S.op("dve", lambda e, z=z, d=d: e.tensor_tensor(
                            out=z[:], in0=z[:], in1=lb[:, d:d + 1, :].to_broadcast([128, NTA, 128]), op=ALU.add),
                            reads=[zB, lbB], writes=[zB])
                        S.op("dve", lambda e, z=z, k_=k_: e.tensor_scalar(out=k_[:], in0=z[:], scalar1=-1.0, scalar2=1.0,
                                                                         op0=ALU.mult, op1=ALU.add), reads=[zB], writes=[kB_])
                        S.op("pool", lambda e, k_=k_, d=d: e.tensor_tensor(
                            out=k_[:], in0=k_[:], in1=mk[:, :, d:d + 1].to_broadcast([128, NTA, 128]), op=ALU.mult),
                            reads=[kB_, mkB], writes=[kB_])
                        S.op("act", lambda e, z=z: e.activation(out=z[:], in_=z[:], func=AF.Ln), reads=[zB, kB_], writes=[zB])
                else:
                    S.dma("sp", kt[0][0][:], RK[h], kt[0][1], reads=[scrB[id(RK)]], writes=[kt[0][1]])
                    S.dma("sp", vt[:], RV[h], vtB, reads=[scrB[id(RV)]], writes=[vtB])
                    S.dma("sp", qt[:], RQ[h], qtB, reads=[scrB[id(RQ)]], writes=[qtB])
                    S.dma("sp", gt_[:], RG[h], gtB, reads=[scrB[id(RG)]], writes=[gtB])
                    for d in range(2):
                        z, zB = zt[d]
                        S.op("dve", lambda e, d=d: e.tensor_copy(out=lgc[:, d:d + 1], in_=lg[:, d, h:h + 1]), reads=[lgB], writes=[lgcB])
                        S.op("pool", lambda e, z=z: e.memset(z[:], 0.0), writes=[zB])
                        S.op("act", lambda e, z=z, d=d: e.activation(out=z[:], in_=z[:], func=AF.Identity, scale=1.0, bias=lgc[:, d:d + 1]),
                             reads=[zB, lgcB], writes=[zB])
                    S.op("pool", lambda e: e.tensor_tensor(
                        out=kt[1][0][:], in0=kt[0][0][:], in1=mk[:, :, 1:2].to_broadcast([128, NTA, 128]), op=ALU.mult),
                        reads=[kt[0][1], mkB], writes=[kt[1][1]])
                    S.op("pool", lambda e: e.tensor_tensor(
                        out=kt[0][0][:], in0=kt[0][0][:], in1=mk[:, :, 0:1].to_broadcast([128, NTA, 128]), op=ALU.mult),
                        reads=[kt[0][1], mkB], writes=[kt[0][1]])
                if hg:
                    S.op("dve", lambda e: e.tensor_tensor(out=gg[:], in0=gt_[:], in1=bc3(hgn[:], 8), op=ALU.mult),
                         reads=[gtB, hgnB], writes=[ggB])
                else:
                    S.op("dve", lambda e: e.tensor_copy(out=gg[:], in_=gt_[:]), reads=[gtB], writes=[ggB])
                for d in range(2):
                    z, zB = zt[d]
                    k_, kB_ = kt[d]
                    kdt, kdB = kd[d]
                    ddt, ddB = dd[d]
                    tri_d, tri_dB = TRI_DEC[d]
                    tri_i, tri_iB = TRI_INC[d]
                    pc, pcB = nb()
                    pcv = pc[:, 0:NTA * 2].rearrange("p (t c) -> p t c", c=2)
                    for t in range(NTA):
                        mm(pcv[:, t, :], z[:, t, :], selm[:, t, d, :], pcB, [zB, selmB], True, True)
                    S.op("act", lambda e, pcv=pcv, ddt=ddt: e.activation(out=ddt[:], in_=pcv, func=AF.Exp), reads=[pcB], writes=[ddB])
                    for t4 in range(0, NTA, 4):
                        n4 = min(4, NTA - t4)
                        pt, pB = nb()
                        for q in range(n4):
                            mm(pt[:, q * 128:(q + 1) * 128], tri_d[:], z[:, t4 + q, :], pB, [tri_dB, zB], True, True)
                        S.op("act", lambda e, pt=pt, n4=n4: e.activation(out=ee[:, 0:n4 * 128], in_=pt[:, 0:n4 * 128], func=AF.Exp),
                             reads=[pB], writes=[eeB])
                        S.op("dve", lambda e, t4=t4, n4=n4, k_=k_, kdt=kdt: e.tensor_tensor(
                            out=kdt[:, t4:t4 + n4, :], in0=k_[:, t4:t4 + n4, :],
                            in1=ee[:, 0:n4 * 128].rearrange("p (t c) -> p t c", c=128), op=ALU.mult),
                            reads=[kB_, eeB], writes=[kdB])
                    qTt, qTB = qT[d]
                    scTt, scTB = scT[d]
                    for t4 in range(0, 8, 4):
                        ta = OWN0 + t4
                        pt, pB = nb()
                        for q in range(4):
                            mm(pt[:, q * 128:(q + 1) * 128], tri_i[:], z[:, ta + q, :], pB, [tri_iB, zB], True, True)
                        S.op("act", lambda e, pt=pt: e.activation(out=ee[:], in_=pt[:], func=AF.Exp), reads=[pB], writes=[eeB])
                        S.op("act", lambda e, pt=pt: e.activation(out=ee2[:], in_=pt[:], func=AF.Exp, scale=-1.0), reads=[pB], writes=[ee2B])
                        S.op("dve", lambda e, t4=t4: e.tensor_tensor(
                            out=qe4[:], in0=qt[:, t4:t4 + 4, :], in1=ee[:].rearrange("p (t c) -> p t c", c=128), op=ALU.mult),
                            reads=[qtB, eeB], writes=[qe4B])
                        S.op("dve", lambda e, ta=ta, k_=k_: e.tensor_tensor(
                            out=ke4[:], in0=k_[:, ta:ta + 4, :], in1=ee2[:].rearrange("p (t c) -> p t c", c=128), op=ALU.mult),
                            reads=[kB_, ee2B], writes=[ke4B])
                        p2, p2B = nb()
                        p2k, p2kB = nb()
                        for q in range(4):
                            tr(p2[:, q * 128:(q + 1) * 128], qe4[:, q, :], ident[:], p2B, [qe4B, identB])
                        for q in range(4):
                            tr(p2k[:, q * 128:(q + 1) * 128], ke4[:, q, :], ident[:], p2kB, [ke4B, identB])
                        S.op("act", lambda e, p2=p2, t4=t4, qTt=qTt: e.activation(
                            out=qTt[:, t4:t4 + 4, :], in_=p2[:].rearrange("p (t c) -> p t c", c=128), func=AF.Copy),
                            reads=[p2B], writes=[qTB])
                        S.op("act", lambda e, p2k=p2k: e.activation(
                            out=kT4[:], in_=p2k[:].rearrange("p (t c) -> p t c", c=128), func=AF.Copy),
                            reads=[p2kB], writes=[kT4B])
                        p3, p3B = nb()
                        for q in range(4):
                            mm(p3[:, q * 128:(q + 1) * 128], kT4[:, q, :], qTt[:, t4 + q, :], p3B, [kT4B, qTB], True, True)
                        S.op("dve", lambda e, p3=p3, t4=t4, scTt=scTt, tri_i=tri_i: e.tensor_tensor(
                            out=scTt[:, t4:t4 + 4, :], in0=p3[:].rearrange("p (t c) -> p t c", c=128),
                            in1=bc3(tri_i[:], 4), op=ALU.mult), reads=[p3B, tri_iB], writes=[scTB])
                for d in range(2):
                    kdt, kdB = kd[d]
                    ddt, ddB = dd[d]
                    St, StB = Sst[d]
                    Snt, SnB = Ssn[d]
                    S.op("pool", lambda e, St=St: e.memset(St[:], 0.0), writes=[StB])
                    if d == 0:
                        order = [(t, c) for t in range(NTA) for c in (0, 1)]
                    else:
                        tl = [1, 0] + list(range(25, 1, -1)) + list(range(33, 25, -1))
                        order = [(t, c) for t in tl for c in (1, 0)]
                    for i0 in range(0, len(order), 4):
                        grp = order[i0:i0 + 4]
                        pt, pB = nb()
                        for q, (t, c) in enumerate(grp):
                            mm(pt[:, q * 128:(q + 1) * 128], kdt[c * 64:(c + 1) * 64, t, :], vt[c * 64:(c + 1) * 64, t, :],
                               pB, [kdB, vtB], True, True)
                        for q, (t, c) in enumerate(grp):
                            if t >= OWN0:
                                ci = (t - OWN0) * 2 + c
                                S.op("act", lambda e, ci=ci, Snt=Snt, St=St: e.activation(out=Snt[:, ci, :], in_=St[:], func=AF.Copy),
                                     reads=[StB], writes=[SnB])
                            S.op("dve", lambda e, pt=pt, q=q, t=t, c=c, St=St, ddt=ddt: e.scalar_tensor_tensor(
                                out=St[:], in0=St[:], scalar=ddt[:, t, c:c + 1], in1=pt[:, q * 128:(q + 1) * 128],
                                op0=ALU.mult, op1=ALU.add), reads=[StB, ddB, pB, SnB], writes=[StB])
                for t in range(8):
                    ta = OWN0 + t
                    po, poB = nb()
                    ov = po[:, 0:128]
                    for d in range(2):
                        qTt, qTB = qT[d]
                        scTt, scTB = scT[d]
                        Snt, SnB = Ssn[d]
                        mm(ov, scTt[:, t, :], vt[:, ta, :], poB, [scTB, vtB], d == 0, False)
                        for c in range(2):
                            mm(po[c * 64:(c + 1) * 64, 0:128], qTt[:, t, c * 64:(c + 1) * 64], Snt[:, t * 2 + c, :], poB,
                               [qTB, SnB], False, d == 1 and c == 1)
                    S.op("act", lambda e, ov=ov, t=t: e.activation(out=oall[:, t, :], in_=ov, func=AF.Copy), reads=[poB], writes=[oallB])
                    if hg:
                        S.op("act", lambda e, t=t: e.activation(out=osb[:], in_=oall[:, t, :], func=AF.Square, accum_out=stall[:, 0, t:t + 1]),
                             reads=[oallB], writes=[osbB, stallB])
                    else:
                        S.op("dve", lambda e, t=t: e.bn_stats(out=bst8[:, t, :], in_=oall[:, t, :]), reads=[oallB], writes=[bst8B])
                        S.op("dve", lambda e, t=t: e.bn_aggr(out=mv8[:, t, :], in_=bst8[:, t, :]), reads=[bst8B], writes=[mv8B])
                if hg:
                    S.op("dve", lambda e: e.tensor_scalar(out=stall[:, 1, :], in0=stall[:, 0, :], scalar1=1.0 / 128, scalar2=EPS,
                                                          op0=ALU.mult, op1=ALU.add), reads=[stallB], writes=[stallB])
                else:
                    S.op("dve", lambda e: e.tensor_scalar(out=stall[:, 1, :], in0=mv8[:, :, 1], scalar1=EPS, scalar2=None,
                                                          op0=ALU.add), reads=[mv8B], writes=[stallB])
                S.op("act", lambda e: e.activation(out=stall[:, 1, :], in_=stall[:, 1, :], func=AF.Ln), reads=[stallB], writes=[stallB])
                S.op("act", lambda e: e.activation(out=stall[:, 1, :], in_=stall[:, 1, :], func=AF.Exp, scale=-0.5), reads=[stallB], writes=[stallB])
                for t in range(8):
                    if hg:
                        S.op("dve", lambda e, t=t: e.scalar_tensor_tensor(
                            out=ysb[:], in0=oall[:, t, :], scalar=stall[:, 1, t:t + 1], in1=gg[:, t, :], op0=ALU.mult, op1=ALU.mult),
                            reads=[oallB, stallB, ggB], writes=[ysbB])
                    else:
                        S.op("dve", lambda e, t=t: e.tensor_scalar(out=osb[:], in0=oall[:, t, :], scalar1=mv8[:, t, 0:1],
                                                                   scalar2=stall[:, 1, t:t + 1], op0=ALU.subtract, op1=ALU.mult),
                             reads=[oallB, mv8B, stallB], writes=[osbB])
                        S.op("dve", lambda e, t=t: e.tensor_tensor(out=ysb[:], in0=osb[:], in1=gg[:, t, :], op=ALU.mult),
                             reads=[osbB, ggB], writes=[ysbB])
                    py, pyB = nb()
                    tr(py[:, 0:128], ysb[:], ident[:], pyB, [ysbB, identB])
                    hh = h if hg else 16 + h
                    S.op("act", lambda e, py=py, t=t, hh=hh: e.activation(out=yT[:, hh, t * 128:(t + 1) * 128], in_=py[:, 0:128], func=AF.Copy),
                         reads=[pyB], writes=[yTB])

            for h in range(16):
                scan_head("hg", h)
            for h in range(16):
                scan_head("ret", h)
        S.barrier()
        if DEBUG:
            S.dma("sp", YT, yT[:], yTB, reads=[yTB], writes=[])

        with ExitStack() as ph:
            mst, mstB = sb("mst", [128, 1024], BF16, ph)
            wbb = [sb("wbb%d" % i, [128, 2, 16, 256], BF16, ph) for i in range(2)]
            gsb = [sb("gsb%d" % i, [128, 1024], BF16, ph) for i in range(2)]
            t1, t1B = sb("t1", [128, 512], F32, ph)
            t2, t2B = sb("t2", [128, 512], F32, ph)
            wbhv = w_bh.rearrange("(kc p) n -> p kc n", p=128)
            wbrv = w_br.rearrange("(kc p) n -> p kc n", p=128)

            def ldw(g):
                wt, wB = wbb[g % 2]
                S.dma("pool", wt[:, 0], wbhv[:, :, g * 256:(g + 1) * 256], wB, writes=[wB])
                S.dma("pool", wt[:, 1], wbrv[:, :, g * 256:(g + 1) * 256], wB, writes=[wB])
            ldw(0)
            for g in range(16):
                wt, wB = wbb[g % 2]
                if g + 1 < 16:
                    ldw(g + 1)
                for n in range(2):
                    nn = g * 2 + n
                    S.dma("sp", gsb[0][0][:], GT[nn], gsb[0][1], reads=[scrB[id(GT)]], writes=[gsb[0][1]])
                    S.dma("sp", gsb[1][0][:], GT[32 + nn], gsb[1][1], reads=[scrB[id(GT)]], writes=[gsb[1][1]])
                    for half in range(2):
                        ps = []
                        for br in range(2):
                            pt, pB = nb()
                            for kc in range(16):
                                mm(pt[:], wt[:, br, kc, n * 128:(n + 1) * 128], yT[:, br * 16 + kc, half * 512:(half + 1) * 512],
                                   pB, [wB, yTB], kc == 0, kc == 15)
                            ps.append((pt, pB))
                        S.op("dve", lambda e, pt=ps[0][0], half=half: e.tensor_tensor(
                            out=t1[:], in0=pt[:], in1=gsb[0][0][:, half * 512:(half + 1) * 512], op=ALU.mult),
                            reads=[ps[0][1], gsb[0][1]], writes=[t1B])
                        S.op("dve", lambda e, pt=ps[1][0], half=half: e.tensor_tensor(
                            out=t2[:], in0=pt[:], in1=gsb[1][0][:, half * 512:(half + 1) * 512], op=ALU.mult),
                            reads=[ps[1][1], gsb[1][1]], writes=[t2B])
                        S.op("pool", lambda e, nn=nn, half=half: e.tensor_tensor(
                            out=mst[:, half * 512:(half + 1) * 512], in0=t1[:], in1=t2[:], op=ALU.add),
                            reads=[t1B, t2B], writes=[mstB])
                    S.dma("sp", MT[nn], mst[:], mstB, reads=[mstB], writes=[scrB[id(MT)]])
        S.barrier()
        yT_stack.close()

        with ExitStack() as ph:
            mT, mTB = sb("mT", [128, 32, 1024], BF16, ph)
            S.dma("sp", mT[:], MT.rearrange("n p t -> p n t"), mTB, reads=[scrB[id(MT)]], writes=[mTB])
            g1bc, g1bcB = sb("g1bc", [128, D], F32, ph)
            build_bc(g1bc, g1bcB, lambda kc: modT[:, 64 + kc, 0:1], modTB)
            wob = [sb("wob%d" % i, [128, 32, 512], BF16, ph) for i in range(2)]
            xp, xpB = sb("xp", [128, 512], F32, ph)
            xq, xqB = sb("xq", [128, 512], F32, ph)
            wov = w_o.rearrange("(kc p) n -> p kc n", p=128)
            S.dma("pool", wob[0][0][:], wov[:, :, 0:512], wob[0][1], writes=[wob[0][1]])
            for cg in range(8):
                wt, wB = wob[cg % 2]
                if cg + 1 < 8:
                    S.dma("pool", wob[(cg + 1) % 2][0][:], wov[:, :, (cg + 1) * 512:(cg + 2) * 512], wob[(cg + 1) % 2][1],
                          writes=[wob[(cg + 1) % 2][1]])
                for t in range(8):
                    pt, pB = nb()
                    for kc in range(32):
                        mm(pt[:], mT[:, kc, t * 128:(t + 1) * 128], wt[:, kc, :], pB, [wB, mTB], kc == 0, kc == 31)
                    S.dma("sp", xp[:], xo[t * 128:(t + 1) * 128, cg * 512:(cg + 1) * 512], xpB, writes=[xpB])
                    S.op("dve", lambda e, pt=pt, cg=cg: e.tensor_tensor(out=xq[:], in0=pt[:], in1=g1bc[:, cg * 512:(cg + 1) * 512], op=ALU.mult),
                         reads=[pB, g1bcB], writes=[xqB])
                    S.op("pool", lambda e: e.tensor_tensor(out=xp[:], in0=xq[:], in1=xp[:], op=ALU.add),
                         reads=[xqB, xpB], writes=[xpB])
                    S.dma("sp", X1[t * 128:(t + 1) * 128, cg * 512:(cg + 1) * 512], xp[:], xpB, reads=[xpB], writes=[scrB[id(X1)]])
        S.barrier()

        with ExitStack() as ph:
            acc, accB = sb("acc", [128, 4, D], F32, ph)
            h2T, h2TB = sb("h2T", [128, 32, 512], BF16, ph)
            st2, st2B = sb("st2", [128, 2], F32, ph)
            w1v = w1.rearrange("(kc p) n -> p kc n", p=128)
            w2v = w2.rearrange("(hc p) n -> p hc n", p=128)
            for half in range(2):
                with ExitStack() as ph1:
                    norm_T(ph1, X1[half * 512:(half + 1) * 512], 4, h2T, h2TB, lambda kc: G2[:, kc:kc + 1], G2B,
                           lambda kc: modT[:, 96 + kc, 0:1], modTB, "m%d" % half)
                S.barrier()
                NHB = 64
                with ExitStack() as phw:
                    w1b = [sb("w1b%d_%d" % (i, half), [128, 32, 256], BF16, phw) for i in range(2)]
                    w2b = [sb("w2b%d_%d" % (i, half), [128, 2, D], BF16, phw) for i in range(2)]
                    aT, aTB = sb("aT%d" % half, [128, 2, 512], BF16, phw)
                    rl, rlB = sb("rl%d" % half, [128, 512], F32, phw)

                    def ldw12(hb):
                        S.dma("pool", w1b[hb % 2][0][:], w1v[:, :, hb * 256:(hb + 1) * 256], w1b[hb % 2][1], writes=[w1b[hb % 2][1]])
                        S.dma("pool", w2b[hb % 2][0][:], w2v[:, hb * 2:(hb + 1) * 2, :], w2b[hb % 2][1], writes=[w2b[hb % 2][1]])
                    ldw12(0)
                    for hb in range(NHB):
                        if hb + 1 < NHB:
                            ldw12(hb + 1)
                        w1t, w1B = w1b[hb % 2]
                        w2t, w2B = w2b[hb % 2]
                        for hc in range(2):
                            pt, pB = nb()
                            for kc in range(32):
                                mm(pt[:], w1t[:, kc, hc * 128:(hc + 1) * 128], h2T[:, kc, :], pB, [w1B, h2TB], kc == 0, kc == 31)
                            S.op("act", lambda e, pt=pt: e.activation(out=rl[:], in_=pt[:], func=AF.Relu), reads=[pB], writes=[rlB])
                            S.op("pool", lambda e, hc=hc: e.tensor_tensor(out=aT[:, hc, :], in0=rl[:], in1=rl[:], op=ALU.mult),
                                 reads=[rlB], writes=[aTB])
                        for tt in range(4):
                            for cg in range(8):
                                pt, pB = nb()
                                for hc in range(2):
                                    mm(pt[:], aT[:, hc, tt * 128:(tt + 1) * 128], w2t[:, hc, cg * 512:(cg + 1) * 512], pB,
                                       [aTB, w2B], hc == 0, hc == 1)
                                if hb == 0:
                                    S.op("act", lambda e, pt=pt, tt=tt, cg=cg: e.activation(
                                        out=acc[:, tt, cg * 512:(cg + 1) * 512], in_=pt[:], func=AF.Copy), reads=[pB], writes=[accB])
                                else:
                                    S.op("dve", lambda e, pt=pt, tt=tt, cg=cg: e.tensor_tensor(
                                        out=acc[:, tt, cg * 512:(cg + 1) * 512], in0=pt[:], in1=acc[:, tt, cg * 512:(cg + 1) * 512], op=ALU.add),
                                        reads=[pB, accB], writes=[accB])
                S.barrier()
                with ExitStack() as ph2:
                    g2bc, g2bcB = sb("g2bc%d" % half, [128, D], F32, ph2)
                    build_bc(g2bc, g2bcB, lambda kc: modT[:, 160 + kc, 0:1], modTB)
                    fgt, fgtB = sb("fgt%d" % half, [128, D], F32, ph2)
                    S.dma("sp", fgt[:], fg_bc, fgtB, writes=[fgtB])
                    x1t, x1tB = sb("x1t%d" % half, [128, D], F32, ph2)
                    jk2, jk2B = sb("jk2%d" % half, [128, D], BF16, ph2)
                    for tt in range(4):
                        r0 = half * 512 + tt * 128
                        S.dma("sp", x1t[:], X1[r0:r0 + 128, :], x1tB, reads=[scrB[id(X1)]], writes=[x1tB])
                        S.op("dve", lambda e, tt=tt: e.tensor_tensor(out=acc[:, tt, :], in0=acc[:, tt, :], in1=g2bc[:], op=ALU.mult),
                             reads=[accB, g2bcB], writes=[accB])
                        S.op("pool", lambda e, tt=tt: e.tensor_tensor(out=acc[:, tt, :], in0=acc[:, tt, :], in1=x1t[:], op=ALU.add),
                             reads=[accB, x1tB], writes=[accB])
                        S.op("act", lambda e, tt=tt: e.activation(out=jk2[:], in_=acc[:, tt, :], func=AF.Square, accum_out=st2[:, 0:1]),
                             reads=[accB], writes=[jk2B, st2B])
                        S.op("dve", lambda e: e.tensor_scalar(out=st2[:, 1:2], in0=st2[:, 0:1], scalar1=1.0 / D, scalar2=EPS,
                                                              op0=ALU.mult, op1=ALU.add), reads=[st2B], writes=[st2B])
                        S.op("act", lambda e: e.activation(out=st2[:, 1:2], in_=st2[:, 1:2], func=AF.Ln), reads=[st2B], writes=[st2B])
                        S.op("act", lambda e: e.activation(out=st2[:, 1:2], in_=st2[:, 1:2], func=AF.Exp, scale=-0.5), reads=[st2B], writes=[st2B])
                        S.op("dve", lambda e, tt=tt: e.scalar_tensor_tensor(
                            out=acc[:, tt, :], in0=acc[:, tt, :], scalar=st2[:, 1:2], in1=fgt[:], op0=ALU.mult, op1=ALU.mult),
                            reads=[accB, st2B, fgtB], writes=[accB])
                        S.dma("sp", out[r0:r0 + 128, :], acc[:, tt, :], accB, reads=[accB], writes=[])
                S.barrier()
        S.barrier()
        S.emit()
    return nc


_NC_CACHE = {}


def _consts():
    p = np.arange(128)
    same = (p[:, None] // 64) == (p[None, :] // 64)
    le = (same & (p[:, None] <= p[None, :])).astype(np.float32)
    gt = (same & (p[:, None] > p[None, :])).astype(np.float32)
    ge = (same & (p[:, None] >= p[None, :])).astype(np.float32)
    lt = (same & (p[:, None] < p[None, :])).astype(np.float32)
    return dict(c_ident=np.eye(128, dtype=np.float32), c_le=le, c_gt=gt, c_ge=ge, c_lt=lt,
                c_ones=np.ones((128, 128), np.float32))


def _rope_tables(pos, scale):
    n_freq = 32
    inv_freq = (np.float32(10000.0) ** (-np.arange(n_freq, dtype=np.float32) / np.float32(n_freq))).astype(np.float32)
    row = (pos // 64).astype(np.float32)
    col = (pos % 64).astype(np.float32)
    ang = np.concatenate([row[:, None] * inv_freq, col[:, None] * inv_freq], axis=-1).astype(np.float32)
    return (np.cos(ang) * np.float32(scale)).astype(np.float32), (np.sin(ang) * np.float32(scale)).astype(np.float32)


def kernel(x, c, ctx, c_ctx, w_mod, b_mod, norm1_g, norm2_g, w_in, hg_lb_logits, hg_norm_g,
           ret_decay_logit, w_branch_hgrn, w_branch_ret, w_out, w_ff1, w_ff2, final_norm_g):
    f32 = lambda a: np.ascontiguousarray(np.asarray(a, dtype=np.float32))
    x, c, ctx, c_ctx = f32(x), f32(c), f32(ctx), f32(c_ctx)
    if "nc" not in _NC_CACHE:
        _NC_CACHE["nc"] = build_program()
    nc = _NC_CACHE["nc"]
    colT = lambda v: np.ascontiguousarray(f32(v).reshape(-1, 128).T)
    shared = dict(
        w_mod=f32(w_mod)[0], b_modT=colT(f32(b_mod)[0]), n1gT=colT(f32(norm1_g)[0]), n2gT=colT(f32(norm2_g)[0]),
        fg_bc=np.ascontiguousarray(np.broadcast_to(f32(final_norm_g)[None, :], (128, D))),
        w_in=f32(w_in)[0],
        lbl=np.ascontiguousarray(np.broadcast_to(f32(hg_lb_logits)[None], (128, 2, 2, 2048))),
        hgain=np.ascontiguousarray(np.broadcast_to(f32(hg_norm_g)[0][None, :], (128, 128))),
        rdl=np.ascontiguousarray(np.broadcast_to(f32(ret_decay_logit)[0][None], (128, 2, 16))),
        w_bh=f32(w_branch_hgrn)[0], w_br=f32(w_branch_ret)[0], w_o=f32(w_out)[0], w1=f32(w_ff1)[0], w2=f32(w_ff2)[0],
    )
    shared.update(_consts())
    sk = 128.0 ** -0.5
    in_maps = []
    for core in range(8):
        b, s = core // 4, core % 4
        others = [j for j in range(4) if j != s]
        xf = np.concatenate([ctx[b]] + [x[b, j * 1024:(j + 1) * 1024] for j in others], axis=0)
        ccm = np.stack([c[b], c_ctx], axis=-1).reshape(32, 128, 2).transpose(1, 0, 2)
        cosK = np.empty((NT_ALL * 128, 64), np.float32)
        sinK = np.empty((NT_ALL * 128, 64), np.float32)
        cosK[:256] = sk
        sinK[:256] = 0.0
        for fi, j in enumerate(others):
            cs, sn = _rope_tables(np.arange(j * 1024, (j + 1) * 1024), sk)
            cosK[256 + fi * 1024:256 + (fi + 1) * 1024] = cs
            sinK[256 + fi * 1024:256 + (fi + 1) * 1024] = sn
        cs, sn = _rope_tables(np.arange(s * 1024, (s + 1) * 1024), sk)
        cosK[3328:] = cs
        sinK[3328:] = sn
        rK = np.stack([cosK.reshape(NT_ALL, 128, 64), sinK.reshape(NT_ALL, 128, 64)], axis=2).transpose(1, 0, 2, 3)
        cq, sq = _rope_tables(np.arange(s * 1024, (s + 1) * 1024), 1.0)
        rQ = np.stack([cq.reshape(8, 128, 64), sq.reshape(8, 128, 64)], axis=2).transpose(1, 0, 2, 3)
        mkv = np.ones((NT_ALL, 2), np.float32)
        for fi, j in enumerate(others):
            mkv[2 + fi * 8:2 + (fi + 1) * 8, 0] = 1.0 if j < s else 0.0
            mkv[2 + fi * 8:2 + (fi + 1) * 8, 1] = 1.0 if j > s else 0.0
        mk = np.broadcast_to(mkv[None], (128, NT_ALL, 2))
        sel = np.zeros((128, 2), np.float32)
        sel[:64, 0] = 1.0
        sel[64:, 1] = 1.0
        selm = sel[:, None, None, :] * mkv[None, :, :, None]
        m = dict(shared)
        m.update(xo=np.ascontiguousarray(x[b, s * 1024:(s + 1) * 1024]), xf=np.ascontiguousarray(xf),
                 cc=np.ascontiguousarray(ccm), ropeK=np.ascontiguousarray(rK), ropeQ=np.ascontiguousarray(rQ),
                 mk=np.ascontiguousarray(mk), selm=np.ascontiguousarray(selm.astype(np.float32)))
        in_maps.append(m)
    res = run_bass_kernel_spmd(nc, in_maps, core_ids=list(range(8)))
    outp = np.empty((2, 4096, D), np.float32)
    for core in range(8):
        b, s = core // 4, core % 4
        outp[b, s * 1024:(s + 1) * 1024] = res.results[core]["out"]
    return outp
```

```python
import numpy as np
from contextlib import ExitStack
import ml_dtypes
import concourse.bass as bass
import concourse.mybir as mybir
from concourse.bass_utils import run_bass_kernel_spmd

F32 = mybir.dt.float32
BF16 = mybir.dt.bfloat16
AF = mybir.ActivationFunctionType
ALU = mybir.AluOpType
AX = mybir.AxisListType

D = 4096
NT_ALL = 34
OWN0 = 26
EPS = 1e-6
DEBUG = False


class Buf:
    __slots__ = ("name", "w", "r", "dsem", "dcnt")

    def __init__(self, name):
        self.name = name
        self.w = None
        self.r = []
        self.dsem = None
        self.dcnt = 0


class Sched:
    ENG = ("pe", "act", "dve", "pool", "sp")

    def __init__(self, nc, es):
        self.nc = nc
        self.es = es
        self.streams = {e: [] for e in self.ENG}
        self.sems = {}
        self.cnt = {e: 0 for e in self.ENG}
        self.known = {e: {} for e in self.ENG}
        for e in self.ENG:
            self.sems[e] = es.enter_context(nc.semaphore("s_" + e))
        self.nd = 0
        self.dbufs = []

    def _dsem(self, buf):
        if buf.dsem is None:
            key = "d%d" % self.nd
            self.nd += 1
            self.sems[key] = self.es.enter_context(self.nc.semaphore(key))
            buf.dsem = key
            self.dbufs.append(buf)
        return buf.dsem

    def _need(self, eng, need):
        kn = self.known[eng]
        for k, v in need.items():
            if k == eng and v > self.cnt[eng]:
                continue
            if kn.get(k, 0) < v:
                kn[k] = v
                sem = self.sems[k]
                self.streams[eng].append(lambda e, sem=sem, v=v: e.wait_ge(sem, v))

    def _waits(self, eng, reads, writes):
        need = {}
        for b in reads:
            if b.w is not None:
                k, v = b.w
                need[k] = max(need.get(k, 0), v)
        for b in writes:
            if b.w is not None:
                k, v = b.w
                need[k] = max(need.get(k, 0), v)
            for (k, v) in b.r:
                need[k] = max(need.get(k, 0), v)
        self._need(eng, need)

    def _commit(self, ev, reads, writes):
        for b in reads:
            if len(b.r) > 64:
                m = {}
                for (k, v) in b.r:
                    m[k] = max(m.get(k, 0), v)
                b.r = list(m.items())
            b.r.append(ev)
        for b in writes:
            b.w = ev
            b.r = []

    def op(self, eng, fn, reads=(), writes=()):
        self._waits(eng, reads, writes)
        self.cnt[eng] += 1
        ev = (eng, self.cnt[eng])
        sem = self.sems[eng]
        self.streams[eng].append(lambda e, fn=fn, sem=sem: fn(e).then_inc(sem, 1))
        self._commit(ev, reads, writes)

    def op_noinc(self, eng, fn, reads=(), writes=()):
        self._waits(eng, reads, writes)
        ev = (eng, self.cnt[eng] + 1)
        self.streams[eng].append(lambda e, fn=fn: fn(e))
        self._commit(ev, reads, writes)

    def dma(self, q, out_ap, in_ap, sb, reads=(), writes=()):
        self._waits(q, reads, writes)
        key = self._dsem(sb)
        sb.dcnt += 16
        ev = (key, sb.dcnt)
        sem = self.sems[key]
        self.streams[q].append(
            lambda e, o=out_ap, i=in_ap, sem=sem: e.dma_start(out=o, in_=i).then_inc(sem, 16))
        self._commit(ev, reads, writes)

    def barrier(self):
        need = {e: self.cnt[e] for e in self.ENG if self.cnt[e] > 0}
        for b in self.dbufs:
            if b.dcnt > 0:
                need[b.dsem] = b.dcnt
        for e in self.ENG:
            self._need(e, dict(need))

    def emit(self):
        nc = self.nc
        st = self.streams
        with nc.Block() as block:
            @block.tensor
            def _(e):
                for f in st["pe"]:
                    f(e)

            @block.scalar
            def _(e):
                for f in st["act"]:
                    f(e)

            @block.vector
            def _(e):
                for f in st["dve"]:
                    f(e)

            @block.gpsimd
            def _(e):
                for f in st["pool"]:
                    f(e)

            @block.sync
            def _(e):
                for f in st["sp"]:
                    f(e)


def build_program():
    nc = bass.Bass("TRN2", target_bir_lowering=False)

    def din(name, shape, dt=F32):
        return nc.dram_tensor(name, list(shape), dt, kind="ExternalInput").ap()

    xo = din("xo", [1024, D])
    xf = din("xf", [3328, D])
    cc = din("cc", [128, 32, 2])
    w_mod = din("w_mod", [D, 6 * D])
    b_modT = din("b_modT", [128, 192])
    n1gT = din("n1gT", [128, 32])
    n2gT = din("n2gT", [128, 32])
    fg_bc = din("fg_bc", [128, D])
    w_in = din("w_in", [D, 26624])
    lbl = din("lbl", [128, 2, 2, 2048])
    hgain = din("hgain", [128, 128])
    rdl = din("rdl", [128, 2, 16])
    w_bh = din("w_bh", [2048, D])
    w_br = din("w_br", [2048, D])
    w_o = din("w_o", [D, D])
    w1 = din("w1", [D, 4 * D])
    w2 = din("w2", [4 * D, D])
    c_ident = din("c_ident", [128, 128])
    c_le = din("c_le", [128, 128])
    c_gt = din("c_gt", [128, 128])
    c_ge = din("c_ge", [128, 128])
    c_lt = din("c_lt", [128, 128])
    c_ones = din("c_ones", [128, 128])
    ropeK = din("ropeK", [128, NT_ALL, 2, 64])
    ropeQ = din("ropeQ", [128, 8, 2, 64])
    mk_d = din("mk", [128, NT_ALL, 2])
    selm_d = din("selm", [128, NT_ALL, 2, 2])
    out = nc.dram_tensor("out", [1024, D], F32, kind="ExternalOutput").ap()

    def dscr(name, shape, dt):
        if DEBUG and name in ("X1", "MT", "YT", "MODT"):
            return nc.dram_tensor(name, list(shape), dt, kind="ExternalOutput").ap()
        return nc.dram_tensor(name, list(shape), dt).ap()

    ZF = dscr("ZF", [16, 128, NT_ALL, 128], F32)
    ZB = dscr("ZB", [16, 128, NT_ALL, 128], F32)
    VH = dscr("VH", [16, 128, NT_ALL, 128], BF16)
    RK = dscr("RK", [16, 128, NT_ALL, 128], BF16)
    RV = dscr("RV", [16, 128, NT_ALL, 128], BF16)
    HQ = dscr("HQ", [16, 128, 8, 128], BF16)
    HG = dscr("HG", [16, 128, 8, 128], BF16)
    RQ = dscr("RQ", [16, 128, 8, 128], BF16)
    RG = dscr("RG", [16, 128, 8, 128], BF16)
    GT = dscr("GT", [64, 128, 1024], BF16)
    X1 = dscr("X1", [1024, D], F32)
    MT = dscr("MT", [32, 128, 1024], BF16)
    if DEBUG:
        YT = dscr("YT", [128, 32, 1024], BF16)
        MODT = dscr("MODT", [128, 192, 2], F32)
    scrB = {id(t): Buf("scr%d" % i) for i, t in enumerate([ZF, ZB, VH, RK, RV, HQ, HG, RQ, RG, GT, X1, MT])}

    with ExitStack() as es:
        S = Sched(nc, es)

        sbn = [0]

        def sb(name, shape, dt, stack=None):
            sbn[0] += 1
            t = (stack or es).enter_context(nc.sbuf_tensor("sb%d_%s" % (sbn[0], name), list(shape), dt))
            return t, Buf(name)

        banks = []
        for i in range(8):
            t = es.enter_context(nc.psum_tensor("pb%d" % i, [128, 512], F32))
            banks.append((t, Buf("pb%d" % i)))
        bank_i = [0]

        def nb():
            b = banks[bank_i[0] % 8]
            bank_i[0] += 1
            return b

        def mm(outap, lhsT, rhs, pbuf, reads, first, last):
            fn = lambda e: e.matmul(outap, lhsT=lhsT, rhs=rhs, start=first, stop=last)
            if last:
                S.op("pe", fn, reads=reads, writes=[pbuf])
            else:
                S.op_noinc("pe", fn, reads=reads, writes=[pbuf] if first else [])

        def tr(outap, inap, ident, pbuf, reads):
            S.op("pe", lambda e: e.transpose(outap, inap, ident), reads=reads, writes=[pbuf])

        ident, identB = sb("ident", [128, 128], F32)
        m_le, m_leB = sb("m_le", [128, 128], F32)
        m_gt, m_gtB = sb("m_gt", [128, 128], F32)
        m_ge, m_geB = sb("m_ge", [128, 128], F32)
        m_lt, m_ltB = sb("m_lt", [128, 128], F32)
        ones, onesB = sb("ones", [128, 128], F32)
        for t, b, src in ((ident, identB, c_ident), (m_le, m_leB, c_le), (m_gt, m_gtB, c_gt),
                          (m_ge, m_geB, c_ge), (m_lt, m_ltB, c_lt), (ones, onesB, c_ones)):
            S.dma("sp", t[:], src, b, writes=[b])
        modT, modTB = sb("modT", [128, 192, 2], F32)
        G1, G1B = sb("G1", [128, 32, 2], F32)
        G2, G2B = sb("G2", [128, 32], F32)
        n1g, n1gB = sb("n1g", [128, 32], F32)
        n2g, n2gB = sb("n2g", [128, 32], F32)
        bmod, bmodB = sb("bmod", [128, 192], F32)
        S.dma("sp", n1g[:], n1gT, n1gB, writes=[n1gB])
        S.dma("sp", n2g[:], n2gT, n2gB, writes=[n2gB])
        S.dma("sp", bmod[:], b_modT, bmodB, writes=[bmodB])
        mk, mkB = sb("mk", [128, NT_ALL, 2], F32)
        selm, selmB = sb("selm", [128, NT_ALL, 2, 2], F32)
        S.dma("sp", mk[:], mk_d, mkB, writes=[mkB])
        S.dma("sp", selm[:], selm_d, selmB, writes=[selmB])
        lg, lgB = sb("lg", [128, 2, 16], F32)
        S.dma("sp", lg[:], rdl, lgB, writes=[lgB])
        S.op("act", lambda e: e.activation(out=lg[:], in_=lg[:], func=AF.Sigmoid), reads=[lgB], writes=[lgB])
        S.op("act", lambda e: e.activation(out=lg[:], in_=lg[:], func=AF.Ln), reads=[lgB], writes=[lgB])

        with ExitStack() as ph:
            ccs, ccsB = sb("ccs", [128, 32, 2], F32, ph)
            ccb, ccbB = sb("ccb", [128, 32, 2], BF16, ph)
            sg0, sg0B = sb("sg0", [128, 32, 2], F32, ph)
            S.dma("sp", ccs[:], cc, ccsB, writes=[ccsB])
            S.op("act", lambda e: e.activation(out=sg0[:], in_=ccs[:], func=AF.Sigmoid), reads=[ccsB], writes=[sg0B])
            S.op("dve", lambda e: e.tensor_tensor(out=ccb[:], in0=ccs[:], in1=sg0[:], op=ALU.mult),
                 reads=[ccsB, sg0B], writes=[ccbB])
            wm = [sb("wm%d" % i, [128, 32, 512], BF16, ph) for i in range(2)]
            wv = w_mod.rearrange("(kc p) n -> p kc n", p=128)
            pm, pmB = nb()
            pmv = pm[:, 0:384].rearrange("p (n j) -> p n j", j=2)
            NB = 48
            S.dma("pool", wm[0][0][:], wv[:, :, 0:512], wm[0][1], writes=[wm[0][1]])
            for g in range(NB):
                wt, wB = wm[g % 2]
                if g + 1 < NB:
                    S.dma("pool", wm[(g + 1) % 2][0][:], wv[:, :, (g + 1) * 512:(g + 2) * 512],
                          wm[(g + 1) % 2][1], writes=[wm[(g + 1) % 2][1]])
                for n in range(4):
                    nn = g * 4 + n
                    for kc in range(32):
                        mm(pmv[:, nn, :], wt[:, kc, n * 128:(n + 1) * 128], ccb[:, kc, :], pmB,
                           [wB, ccbB], kc == 0, kc == 31)
            for j in range(2):
                S.op("dve", lambda e, j=j: e.tensor_tensor(out=modT[:, :, j], in0=pmv[:, :, j], in1=bmod[:], op=ALU.add),
                     reads=[pmB, bmodB], writes=[modTB])
            for j in range(2):
                S.op("dve", lambda e, j=j: e.scalar_tensor_tensor(
                    out=G1[:, :, j], in0=modT[:, 32:64, j], scalar=1.0, in1=n1g[:], op0=ALU.add, op1=ALU.mult),
                    reads=[modTB, n1gB], writes=[G1B])
            S.op("dve", lambda e: e.scalar_tensor_tensor(
                out=G2[:], in0=modT[:, 128:160, 0], scalar=1.0, in1=n2g[:], op0=ALU.add, op1=ALU.mult),
                reads=[modTB, n2gB], writes=[G2B])
        S.barrier()
        if DEBUG:
            S.dma("sp", MODT, modT[:], modTB, reads=[modTB], writes=[])

        def build_bc(dst, dstB, colap_fn, colB):
            with ExitStack() as ph2:
                dg, dgB = sb("dg_" + dstB.name, [128, 128], F32, ph2)
                for kc in range(32):
                    S.op("dve", lambda e, kc=kc: e.tensor_scalar(
                        out=dg[:], in0=ident[:], scalar1=colap_fn(kc), scalar2=None, op0=ALU.mult),
                        reads=[identB, colB], writes=[dgB])
                    pt, pB = nb()
                    mm(pt[:, 0:128], ones[:], dg[:], pB, [onesB, dgB], True, True)
                    S.op("act", lambda e, kc=kc, pt=pt: e.activation(
                        out=dst[:, kc * 128:(kc + 1) * 128], in_=pt[:, 0:128], func=AF.Copy),
                        reads=[pB], writes=[dstB])
            S.barrier()

        def norm_T(ph, xsrc, nt, hT, hTB, Gcol, GB, SHcol, SHB, tag):
            xt, xtB = sb("xt_" + tag, [128, D], F32, ph)
            jk, jkB = sb("jk_" + tag, [128, D], BF16, ph)
            st, stB = sb("st_" + tag, [128, 2], F32, ph)
            for t in range(nt):
                S.dma("sp", xt[:], xsrc[t * 128:(t + 1) * 128, :], xtB, writes=[xtB])
                S.op("act", lambda e: e.activation(out=jk[:], in_=xt[:], func=AF.Square, accum_out=st[:, 0:1]),
                     reads=[xtB], writes=[jkB, stB])
                S.op("dve", lambda e: e.tensor_scalar(out=st[:, 1:2], in0=st[:, 0:1], scalar1=1.0 / D, scalar2=EPS,
                                                      op0=ALU.mult, op1=ALU.add), reads=[stB], writes=[stB])
                S.op("act", lambda e: e.activation(out=st[:, 1:2], in_=st[:, 1:2], func=AF.Ln), reads=[stB], writes=[stB])
                S.op("act", lambda e: e.activation(out=st[:, 1:2], in_=st[:, 1:2], func=AF.Exp, scale=-0.5), reads=[stB], writes=[stB])
                S.op("dve", lambda e: e.tensor_scalar(out=xt[:], in0=xt[:], scalar1=st[:, 1:2], scalar2=None,
                                                      op0=ALU.mult), reads=[stB, xtB], writes=[xtB])
                for k4 in range(8):
                    pt, pB = nb()
                    for q in range(4):
                        kc = k4 * 4 + q
                        tr(pt[:, q * 128:(q + 1) * 128], xt[:, kc * 128:(kc + 1) * 128], ident[:], pB, [xtB, identB])
                    for q in range(4):
                        kc = k4 * 4 + q
                        S.op("act", lambda e, kc=kc, q=q, pt=pt, t=t: e.activation(
                            out=hT[:, kc, t * 128:(t + 1) * 128], in_=pt[:, q * 128:(q + 1) * 128],
                            func=AF.Identity, scale=Gcol(kc), bias=SHcol(kc)),
                            reads=[pB, GB, SHB], writes=[hTB])

        FAMS = [("vh", 0, "copy", VH, BF16), ("zf", 2048, "z", ZF, F32), ("zb", 4096, "z", ZB, F32),
                ("rk", 6144, "ropeK", RK, BF16), ("rv", 8192, "copy", RV, BF16),
                ("hq", 10240, "siluc", HQ, BF16), ("hg", 12288, "silu", HG, BF16),
                ("rq", 14336, "ropeQ", RQ, BF16), ("rg", 16384, "silu", RG, BF16)]
        winv = w_in.rearrange("(kc p) n -> p kc n", p=128)

        def seg_pass(xsrc, nt, t0, own, modj, tag):
            with ExitStack() as ph:
                hT, hTB = sb("hT_" + tag, [128, 32, nt * 128], BF16, ph)
                wb = [sb("wi%d_%s" % (i, tag), [128, 32, 512], BF16, ph) for i in range(2)]
                S.dma("pool", wb[0][0][:], winv[:, :, 0:512], wb[0][1], writes=[wb[0][1]])
                with ExitStack() as ph1:
                    norm_T(ph1, xsrc, nt, hT, hTB, lambda kc: G1[:, kc, modj:modj + 1], G1B,
                           lambda kc: modT[:, kc, modj:modj + 1], modTB, tag)
                S.barrier()
                stg32, stg32B = sb("s32_" + tag, [128, 4, nt, 128], F32, ph)
                stg16, stg16B = sb("s16_" + tag, [128, 4, nt, 128], BF16, ph)
                tmp, tmpB = sb("tmp_" + tag, [128, 512], F32, ph)
                tmp2, tmp2B = sb("tmp2_" + tag, [128, 512], F32, ph)
                rK, rKB = sb("rK_" + tag, [128, nt, 2, 64], F32, ph)
                S.dma("sp", rK[:], ropeK[:, t0:t0 + nt], rKB, writes=[rKB])
                if own:
                    rQ, rQB = sb("rQ_" + tag, [128, 8, 2, 64], F32, ph)
                    S.dma("sp", rQ[:], ropeQ, rQB, writes=[rQB])
                    gst, gstB = sb("gst_" + tag, [128, 1024], BF16, ph)
                fams = FAMS if own else FAMS[:5]
                groups = []
                for (fn_, off, kind, scr, dt) in fams:
                    for g4 in range(4):
                        groups.append((off + g4 * 512, kind, scr, dt, g4))
                if own:
                    for gg in range(16):
                        groups.append((18432 + gg * 512, "gate", GT, BF16, gg))
                ng = len(groups)
                assert groups[0][0] == 0
                for gi, (coff, kind, scr, dt, g4) in enumerate(groups):
                    wt, wB = wb[gi % 2]
                    if gi + 1 < ng:
                        c2 = groups[gi + 1][0]
                        S.dma("pool", wb[(gi + 1) % 2][0][:], winv[:, :, c2:c2 + 512], wb[(gi + 1) % 2][1],
                              writes=[wb[(gi + 1) % 2][1]])
                    if kind == "gate":
                        for n in range(4):
                            for half in range(2):
                                pt, pB = nb()
                                for kc in range(32):
                                    mm(pt[:], wt[:, kc, n * 128:(n + 1) * 128], hT[:, kc, half * 512:(half + 1) * 512],
                                       pB, [wB, hTB], kc == 0, kc == 31)
                                S.op("act", lambda e, pt=pt, half=half: e.activation(
                                    out=gst[:, half * 512:(half + 1) * 512], in_=pt[:], func=AF.Sigmoid),
                                    reads=[pB], writes=[gstB])
                            S.dma("sp", GT[g4 * 4 + n], gst[:], gstB, reads=[gstB], writes=[scrB[id(GT)]])
                        continue
                    stg, stgB = (stg32, stg32B) if dt == F32 else (stg16, stg16B)
                    for t in range(nt):
                        pt, pB = nb()
                        for kc in range(32):
                            mm(pt[:], hT[:, kc, t * 128:(t + 1) * 128], wt[:, kc, :], pB, [wB, hTB], kc == 0, kc == 31)
                        pv = pt[:].rearrange("p (h c) -> p h c", h=4)
                        if kind in ("copy", "z"):
                            S.op("act", lambda e, pv=pv, t=t, stg=stg: e.activation(out=stg[:, :, t, :], in_=pv, func=AF.Copy),
                                 reads=[pB], writes=[stgB])
                        elif kind in ("silu", "siluc"):
                            S.op("act", lambda e, pt=pt: e.activation(out=tmp[:], in_=pt[:], func=AF.Sigmoid),
                                 reads=[pB], writes=[tmpB])
                            cst = (128.0 ** -0.5) if kind == "siluc" else 1.0
                            S.op("dve", lambda e, pv=pv, t=t, cst=cst, stg=stg: e.scalar_tensor_tensor(
                                out=stg[:, :, t, :], in0=pv, scalar=cst, in1=tmp[:].rearrange("p (h c) -> p h c", h=4),
                                op0=ALU.mult, op1=ALU.mult), reads=[pB, tmpB], writes=[stgB])
                        else:
                            rt, rtB = (rK, rKB) if kind == "ropeK" else (rQ, rQB)
                            S.op("act", lambda e, pt=pt: e.activation(out=tmp[:], in_=pt[:], func=AF.Copy),
                                 reads=[pB], writes=[tmpB])
                            tv = tmp[:].rearrange("p (h c two) -> p h c two", h=4, two=2)
                            t2v = tmp2[:].rearrange("p (h c two) -> p h c two", h=4, two=2)
                            sv = stg[:, :, t, :].rearrange("p h (c two) -> p h c two", two=2)
                            for h in range(4):
                                a1, a2 = tv[:, h, :, 0], tv[:, h, :, 1]
                                cs, sn = rt[:, t, 0, :], rt[:, t, 1, :]
                                b1, b2 = t2v[:, h, :, 0], t2v[:, h, :, 1]
                                S.op("dve", lambda e, a1=a1, cs=cs, b1=b1: e.tensor_tensor(out=b1, in0=a1, in1=cs, op=ALU.mult),
                                     reads=[tmpB, rtB], writes=[tmp2B])
                                S.op("dve", lambda e, a2=a2, sn=sn, b2=b2: e.tensor_tensor(out=b2, in0=a2, in1=sn, op=ALU.mult),
                                     reads=[tmpB, rtB], writes=[tmp2B])
                                S.op("dve", lambda e, b1=b1, b2=b2, o=sv[:, h, :, 0]: e.tensor_tensor(out=o, in0=b1, in1=b2, op=ALU.subtract),
                                     reads=[tmp2B], writes=[stgB])
                                S.op("dve", lambda e, a1=a1, sn=sn, b1=b1: e.tensor_tensor(out=b1, in0=a1, in1=sn, op=ALU.mult),
                                     reads=[tmpB, rtB, stgB], writes=[tmp2B])
                                S.op("dve", lambda e, a2=a2, cs=cs, b2=b2: e.tensor_tensor(out=b2, in0=a2, in1=cs, op=ALU.mult),
                                     reads=[tmpB, rtB], writes=[tmp2B])
                                S.op("dve", lambda e, b1=b1, b2=b2, o=sv[:, h, :, 1]: e.tensor_tensor(out=o, in0=b1, in1=b2, op=ALU.add),
                                     reads=[tmp2B], writes=[stgB])
                    ts0 = t0 if scr.shape[2] == NT_ALL else 0
                    S.dma("sp", scr[g4 * 4:(g4 + 1) * 4, :, ts0:ts0 + nt, :].rearrange("h p t c -> p h t c"),
                          stg[:], stgB, reads=[stgB], writes=[scrB[id(scr)]])
            S.barrier()

        seg_pass(xf[0:256], 2, 0, False, 1, "c")
        for f in range(3):
            seg_pass(xf[256 + f * 1024:256 + (f + 1) * 1024], 8, 2 + f * 8, False, 0, "f%d" % f)
        seg_pass(xo, 8, OWN0, True, 0, "o")

        yT_stack = ExitStack()
        yT, yTB = sb("yT", [128, 32, 1024], BF16, yT_stack)
        with ExitStack() as ph:
            NTA = NT_ALL
            zt = [sb("zt%d" % d, [128, NTA, 128], F32, ph) for d in range(2)]
            kt = [sb("kt%d" % d, [128, NTA, 128], BF16, ph) for d in range(2)]
            kd = [sb("kd%d" % d, [128, NTA, 128], BF16, ph) for d in range(2)]
            vt, vtB = sb("vt", [128, NTA, 128], BF16, ph)
            qt, qtB = sb("qt", [128, 8, 128], BF16, ph)
            gt_, gtB = sb("gt", [128, 8, 128], BF16, ph)
            gg, ggB = sb("gg", [128, 8, 128], F32, ph)
            lbt, lbtB = sb("lbt", [128, 2, 2, 128], F32, ph)
            lb, lbB = sb("lb", [128, 2, 128], F32, ph)
            oml, omlB = sb("oml", [128, 2, 128], F32, ph)
            hgn, hgnB = sb("hgn", [128, 128], F32, ph)
            S.dma("sp", hgn[:], hgain, hgnB, writes=[hgnB])
            dd = [sb("dd%d" % d, [128, NTA, 2], F32, ph) for d in range(2)]
            Sst = [sb("Sst%d" % d, [128, 128], F32, ph) for d in range(2)]
            Ssn = [sb("Ssn%d" % d, [128, 16, 128], BF16, ph) for d in range(2)]
            qe = [sb("qe%d" % d, [128, 128], F32, ph) for d in range(2)]
            ke = [sb("ke%d" % d, [128, 128], F32, ph) for d in range(2)]
            ee, eeB = sb("ee", [128, 512], F32, ph)
            qT = [sb("qT%d" % d, [128, 8, 128], BF16, ph) for d in range(2)]
            kT, kTB = sb("kT", [128, 128], BF16, ph)
            scT = [sb("scT%d" % d, [128, 8, 128], BF16, ph) for d in range(2)]
            osb, osbB = sb("osb", [128, 128], F32, ph)
            ysb, ysbB = sb("ysb", [128, 128], F32, ph)
            stt, sttB = sb("stt", [128, 8], F32, ph)
            bst, bstB = sb("bst", [128, 6], F32, ph)
            lgc, lgcB = sb("lgc", [128, 2], F32, ph)
            TRI_INC = [(m_le, m_leB), (m_ge, m_geB)]
            TRI_DEC = [(m_gt, m_gtB), (m_lt, m_ltB)]

            ee2, ee2B = sb("ee2", [128, 512], F32, ph)
            qe4, qe4B = sb("qe4", [128, 4, 128], F32, ph)
            ke4, ke4B = sb("ke4", [128, 4, 128], F32, ph)
            kT4, kT4B = sb("kT4", [128, 4, 128], BF16, ph)
            oall, oallB = sb("oall", [128, 8, 128], F32, ph)
            stall, stallB = sb("stall", [128, 3, 8], F32, ph)
            bst8, bst8B = sb("bst8", [128, 8, 6], F32, ph)
            mv8, mv8B = sb("mv8", [128, 8, 2], F32, ph)

            def bc3(ap2d, n):
                return ap2d.rearrange("p (o c) -> p o c", o=1).to_broadcast([128, n, 128])

            def loads_z(kind, h):
                if kind == "hg":
                    S.dma("sp", zt[0][0][:], ZF[h], zt[0][1], reads=[scrB[id(ZF)]], writes=[zt[0][1]])
                    S.dma("sp", zt[1][0][:], ZB[h], zt[1][1], reads=[scrB[id(ZB)]], writes=[zt[1][1]])
                else:
                    S.dma("sp", kt[0][0][:], RK[h], kt[0][1], reads=[scrB[id(RK)]], writes=[kt[0][1]])

            def scan_head(kind, h, nxt):
                hg = kind == "hg"
                if hg:
                    S.dma("sp", vt[:], VH[h], vtB, reads=[scrB[id(VH)]], writes=[vtB])
                    S.dma("sp", qt[:], HQ[h], qtB, reads=[scrB[id(HQ)]], writes=[qtB])
                    S.dma("sp", gt_[:], HG[h], gtB, reads=[scrB[id(HG)]], writes=[gtB])
                    S.dma("sp", lbt[:], lbl[:, :, :, h * 128:(h + 1) * 128], lbtB, writes=[lbtB])
                    S.op("dve", lambda e: e.tensor_tensor(out=lb[:], in0=lbt[:, 0], in1=lbt[:, 1], op=ALU.subtract),
                         reads=[lbtB], writes=[lbB])
                    S.op("act", lambda e: e.activation(out=lb[:], in_=lb[:], func=AF.Sigmoid), reads=[lbB], writes=[lbB])
                    S.op("dve", lambda e: e.tensor_scalar(out=oml[:], in0=lb[:], scalar1=-1.0, scalar2=1.0,
                                                          op0=ALU.mult, op1=ALU.add), reads=[lbB], writes=[omlB])
                    for d in range(2):
                        z, zB = zt[d]
                        k_, kB_ = kt[d]
                        S.op("act", lambda e, z=z: e.activation(out=z[:], in_=z[:], func=AF.Sigmoid), reads=[zB], writes=[zB])
                        S.op("dve", lambda e, z=z, d=d: e.tensor_tensor(
                            out=z[:], in0=z[:], in1=oml[:, d:d + 1, :].to_broadcast([128, NTA, 128]), op=ALU.mult),
                            reads=[zB, omlB], writes=[zB])
                        S.op("dve", lambda e, z=z, d=d: e.tensor_tensor(
                            out=z[:], in0=z[:], in1=lb[:, d:d + 1, :].to_broadcast([128, NTA, 128]), op=ALU.add),
                            reads=[zB, lbB], writes=[zB])
                        S.op("dve", lambda e, z=z, k_=k_: e.tensor_scalar(out=k_[:], in0=z[:], scalar1=-1.0, scalar2=1.0,
                                                                         op0=ALU.mult, op1=ALU.add), reads=[zB], writes=[kB_])
                        S.op("pool", lambda e, k_=k_, d=d: e.tensor_tensor(
                            out=k_[:], in0=k_[:], in1=mk[:, :, d:d + 1].to_broadcast([128, NTA, 128]), op=ALU.mult),
                            reads=[kB_, mkB], writes=[kB_])
                        S.op("act", lambda e, z=z: e.activation(out=z[:], in_=z[:], func=AF.Ln), reads=[zB, kB_], writes=[zB])
                else:
                    S.dma("sp", vt[:], RV[h], vtB, reads=[scrB[id(RV)]], writes=[vtB])
                    S.dma("sp", qt[:], RQ[h], qtB, reads=[scrB[id(RQ)]], writes=[qtB])
                    S.dma("sp", gt_[:], RG[h], gtB, reads=[scrB[id(RG)]], writes=[gtB])
                    for d in range(2):
                        z, zB = zt[d]
                        S.op("dve", lambda e, d=d: e.tensor_copy(out=lgc[:, d:d + 1], in_=lg[:, d, h:h + 1]), reads=[lgB], writes=[lgcB])
                        S.op("pool", lambda e, z=z: e.memset(z[:], 0.0), writes=[zB])
                        S.op("act", lambda e, z=z, d=d: e.activation(out=z[:], in_=z[:], func=AF.Identity, scale=1.0, bias=lgc[:, d:d + 1]),
                             reads=[zB, lgcB], writes=[zB])
                    S.op("pool", lambda e: e.tensor_tensor(
                        out=kt[1][0][:], in0=kt[0][0][:], in1=mk[:, :, 1:2].to_broadcast([128, NTA, 128]), op=ALU.mult),
                        reads=[kt[0][1], mkB], writes=[kt[1][1]])
                    S.op("pool", lambda e: e.tensor_tensor(
                        out=kt[0][0][:], in0=kt[0][0][:], in1=mk[:, :, 0:1].to_broadcast([128, NTA, 128]), op=ALU.mult),
                        reads=[kt[0][1], mkB], writes=[kt[0][1]])
                if hg:
                    S.op("dve", lambda e: e.tensor_tensor(out=gg[:], in0=gt_[:], in1=bc3(hgn[:], 8), op=ALU.mult),
                         reads=[gtB, hgnB], writes=[ggB])
                else:
                    S.op("dve", lambda e: e.tensor_copy(out=gg[:], in_=gt_[:]), reads=[gtB], writes=[ggB])
                for d in range(2):
                    z, zB = zt[d]
                    k_, kB_ = kt[d]
                    kdt, kdB = kd[d]
                    ddt, ddB = dd[d]
                    tri_d, tri_dB = TRI_DEC[d]
                    tri_i, tri_iB = TRI_INC[d]
                    pc, pcB = nb()
                    pcv = pc[:, 0:NTA * 2].rearrange("p (t c) -> p t c", c=2)
                    for t in range(NTA):
                        mm(pcv[:, t, :], z[:, t, :], selm[:, t, d, :], pcB, [zB, selmB], True, True)
                    S.op("act", lambda e, pcv=pcv, ddt=ddt: e.activation(out=ddt[:], in_=pcv, func=AF.Exp), reads=[pcB], writes=[ddB])
                    for t4 in range(0, NTA, 4):
                        n4 = min(4, NTA - t4)
                        pt, pB = nb()
                        for q in range(n4):
                            mm(pt[:, q * 128:(q + 1) * 128], tri_d[:], z[:, t4 + q, :], pB, [tri_dB, zB], True, True)
                        S.op("act", lambda e, pt=pt, n4=n4: e.activation(out=ee[:, 0:n4 * 128], in_=pt[:, 0:n4 * 128], func=AF.Exp),
                             reads=[pB], writes=[eeB])
                        S.op("dve", lambda e, t4=t4, n4=n4, k_=k_, kdt=kdt: e.tensor_tensor(
                            out=kdt[:, t4:t4 + n4, :], in0=k_[:, t4:t4 + n4, :],
                            in1=ee[:, 0:n4 * 128].rearrange("p (t c) -> p t c", c=128), op=ALU.mult),
                            reads=[kB_, eeB], writes=[kdB])
                    qTt, qTB = qT[d]
                    scTt, scTB = scT[d]
                    for t4 in range(0, 8, 4):
                        ta = OWN0 + t4
                        pt, pB = nb()
                        for q in range(4):
                            mm(pt[:, q * 128:(q + 1) * 128], tri_i[:], z[:, ta + q, :], pB, [tri_iB, zB], True, True)
                        S.op("act", lambda e, pt=pt: e.activation(out=ee[:], in_=pt[:], func=AF.Exp), reads=[pB], writes=[eeB])
                        S.op("act", lambda e, pt=pt: e.activation(out=ee2[:], in_=pt[:], func=AF.Exp, scale=-1.0), reads=[pB], writes=[ee2B])
                        S.op("dve", lambda e, t4=t4: e.tensor_tensor(
                            out=qe4[:], in0=qt[:, t4:t4 + 4, :], in1=ee[:].rearrange("p (t c) -> p t c", c=128), op=ALU.mult),
                            reads=[qtB, eeB], writes=[qe4B])
                        S.op("dve", lambda e, ta=ta, k_=k_: e.tensor_tensor(
                            out=ke4[:], in0=k_[:, ta:ta + 4, :], in1=ee2[:].rearrange("p (t c) -> p t c", c=128), op=ALU.mult),
                            reads=[kB_, ee2B], writes=[ke4B])
                        p2, p2B = nb()
                        p2k, p2kB = nb()
                        for q in range(4):
                            tr(p2[:, q * 128:(q + 1) * 128], qe4[:, q, :], ident[:], p2B, [qe4B, identB])
                        for q in range(4):
                            tr(p2k[:, q * 128:(q + 1) * 128], ke4[:, q, :], ident[:], p2kB, [ke4B, identB])
                        S.op("act", lambda e, p2=p2, t4=t4, qTt=qTt: e.activation(
                            out=qTt[:, t4:t4 + 4, :], in_=p2[:].rearrange("p (t c) -> p t c", c=128), func=AF.Copy),
                            reads=[p2B], writes=[qTB])
                        S.op("act", lambda e, p2k=p2k: e.activation(
                            out=kT4[:], in_=p2k[:].rearrange("p (t c) -> p t c", c=128), func=AF.Copy),
                            reads=[p2kB], writes=[kT4B])
                        p3, p3B = nb()
                        for q in range(4):
                            mm(p3[:, q * 128:(q + 1) * 128], kT4[:, q, :], qTt[:, t4 + q, :], p3B, [kT4B, qTB], True, True)
                        S.op("dve", lambda e, p3=p3, t4=t4, scTt=scTt, tri_i=tri_i: e.tensor_tensor(
                            out=scTt[:, t4:t4 + 4, :], in0=p3[:].rearrange("p (t c) -> p t c", c=128),
                            in1=bc3(tri_i[:], 4), op=ALU.mult), reads=[p3B, tri_iB], writes=[scTB])
                orders = []
                for d in range(2):
                    St, StB = Sst[d]
                    S.op("pool", lambda e, St=St: e.memset(St[:], 0.0), writes=[StB])
                    if d == 0:
                        orders.append([(t, c) for t in range(NTA) for c in (0, 1)])
                    else:
                        tl = [1, 0] + list(range(25, 1, -1)) + list(range(33, 25, -1))
                        orders.append([(t, c) for t in tl for c in (1, 0)])
                for i0 in range(0, NTA * 2, 4):
                    pend = []
                    for d in range(2):
                        kdt, kdB = kd[d]
                        grp = orders[d][i0:i0 + 4]
                        pt, pB = nb()
                        for q, (t, c) in enumerate(grp):
                            mm(pt[:, q * 128:(q + 1) * 128], kdt[c * 64:(c + 1) * 64, t, :], vt[c * 64:(c + 1) * 64, t, :],
                               pB, [kdB, vtB], True, True)
                        pend.append((grp, pt, pB))
                    for q in range(4):
                        for d in range(2):
                            grp, pt, pB = pend[d]
                            t, c = grp[q]
                            ddt, ddB = dd[d]
                            St, StB = Sst[d]
                            Snt, SnB = Ssn[d]
                            if t >= OWN0:
                                ci = (t - OWN0) * 2 + c
                                S.op("act", lambda e, ci=ci, Snt=Snt, St=St: e.activation(out=Snt[:, ci, :], in_=St[:], func=AF.Copy),
                                     reads=[StB], writes=[SnB])
                            S.op("dve", lambda e, pt=pt, q=q, t=t, c=c, St=St, ddt=ddt: e.scalar_tensor_tensor(
                                out=St[:], in0=St[:], scalar=ddt[:, t, c:c + 1], in1=pt[:, q * 128:(q + 1) * 128],
                                op0=ALU.mult, op1=ALU.add), reads=[StB, ddB, pB, SnB], writes=[StB])
                if nxt is not None:
                    loads_z(*nxt)
                for t in range(8):
                    ta = OWN0 + t
                    po, poB = nb()
                    ov = po[:, 0:128]
                    for d in range(2):
                        qTt, qTB = qT[d]
                        scTt, scTB = scT[d]
                        Snt, SnB = Ssn[d]
                        mm(ov, scTt[:, t, :], vt[:, ta, :], poB, [scTB, vtB], d == 0, False)
                        for c in range(2):
                            mm(po[c * 64:(c + 1) * 64, 0:128], qTt[:, t, c * 64:(c + 1) * 64], Snt[:, t * 2 + c, :], poB,
                               [qTB, SnB], False, d == 1 and c == 1)
                    S.op("act", lambda e, ov=ov, t=t: e.activation(out=oall[:, t, :], in_=ov, func=AF.Copy), reads=[poB], writes=[oallB])
                    if hg:
                        S.op("act", lambda e, t=t: e.activation(out=osb[:], in_=oall[:, t, :], func=AF.Square, accum_out=stall[:, 0, t:t + 1]),
                             reads=[oallB], writes=[osbB, stallB])
                    else:
                        S.op("dve", lambda e, t=t: e.bn_stats(out=bst8[:, t, :], in_=oall[:, t, :]), reads=[oallB], writes=[bst8B])
                        S.op("dve", lambda e, t=t: e.bn_aggr(out=mv8[:, t, :], in_=bst8[:, t, :]), reads=[bst8B], writes=[mv8B])
                if hg:
                    S.op("dve", lambda e: e.tensor_scalar(out=stall[:, 1, :], in0=stall[:, 0, :], scalar1=1.0 / 128, scalar2=EPS,
                                                          op0=ALU.mult, op1=ALU.add), reads=[stallB], writes=[stallB])
                else:
                    S.op("dve", lambda e: e.tensor_scalar(out=stall[:, 1, :], in0=mv8[:, :, 1], scalar1=EPS, scalar2=None,
                                                          op0=ALU.add), reads=[mv8B], writes=[stallB])
                S.op("act", lambda e: e.activation(out=stall[:, 1, :], in_=stall[:, 1, :], func=AF.Ln), reads=[stallB], writes=[stallB])
                S.op("act", lambda e: e.activation(out=stall[:, 1, :], in_=stall[:, 1, :], func=AF.Exp, scale=-0.5), reads=[stallB], writes=[stallB])
                for t in range(8):
                    if hg:
                        S.op("dve", lambda e, t=t: e.scalar_tensor_tensor(
                            out=ysb[:], in0=oall[:, t, :], scalar=stall[:, 1, t:t + 1], in1=gg[:, t, :], op0=ALU.mult, op1=ALU.mult),
                            reads=[oallB, stallB, ggB], writes=[ysbB])
                    else:
                        S.op("dve", lambda e, t=t: e.tensor_scalar(out=osb[:], in0=oall[:, t, :], scalar1=mv8[:, t, 0:1],
                                                                   scalar2=stall[:, 1, t:t + 1], op0=ALU.subtract, op1=ALU.mult),
                             reads=[oallB, mv8B, stallB], writes=[osbB])
                        S.op("dve", lambda e, t=t: e.tensor_tensor(out=ysb[:], in0=osb[:], in1=gg[:, t, :], op=ALU.mult),
                             reads=[osbB, ggB], writes=[ysbB])
                    py, pyB = nb()
                    tr(py[:, 0:128], ysb[:], ident[:], pyB, [ysbB, identB])
                    hh = h if hg else 16 + h
                    S.op("act", lambda e, py=py, t=t, hh=hh: e.activation(out=yT[:, hh, t * 128:(t + 1) * 128], in_=py[:, 0:128], func=AF.Copy),
                         reads=[pyB], writes=[yTB])

            heads = [("hg", h) for h in range(16)] + [("ret", h) for h in range(16)]
            loads_z(*heads[0])
            for i, (kind, h) in enumerate(heads):
                scan_head(kind, h, heads[i + 1] if i + 1 < len(heads) else None)
        S.barrier()
        if DEBUG:
            S.dma("sp", YT, yT[:], yTB, reads=[yTB], writes=[])

        with ExitStack() as ph:
            mst, mstB = sb("mst", [128, 1024], BF16, ph)
            wbb = [sb("wbb%d" % i, [128, 2, 16, 256], BF16, ph) for i in range(2)]
            gsb = [sb("gsb%d" % i, [128, 1024], BF16, ph) for i in range(2)]
            t1, t1B = sb("t1", [128, 512], F32, ph)
            t2, t2B = sb("t2", [128, 512], F32, ph)
            wbhv = w_bh.rearrange("(kc p) n -> p kc n", p=128)
            wbrv = w_br.rearrange("(kc p) n -> p kc n", p=128)

            def ldw(g):
                wt, wB = wbb[g % 2]
                S.dma("pool", wt[:, 0], wbhv[:, :, g * 256:(g + 1) * 256], wB, writes=[wB])
                S.dma("pool", wt[:, 1], wbrv[:, :, g * 256:(g + 1) * 256], wB, writes=[wB])
            ldw(0)
            for g in range(16):
                wt, wB = wbb[g % 2]
                if g + 1 < 16:
                    ldw(g + 1)
                for n in range(2):
                    nn = g * 2 + n
                    S.dma("sp", gsb[0][0][:], GT[nn], gsb[0][1], reads=[scrB[id(GT)]], writes=[gsb[0][1]])
                    S.dma("sp", gsb[1][0][:], GT[32 + nn], gsb[1][1], reads=[scrB[id(GT)]], writes=[gsb[1][1]])
                    for half in range(2):
                        ps = []
                        for br in range(2):
                            pt, pB = nb()
                            for kc in range(16):
                                mm(pt[:], wt[:, br, kc, n * 128:(n + 1) * 128], yT[:, br * 16 + kc, half * 512:(half + 1) * 512],
                                   pB, [wB, yTB], kc == 0, kc == 15)
                            ps.append((pt, pB))
                        S.op("dve", lambda e, pt=ps[0][0], half=half: e.tensor_tensor(
                            out=t1[:], in0=pt[:], in1=gsb[0][0][:, half * 512:(half + 1) * 512], op=ALU.mult),
                            reads=[ps[0][1], gsb[0][1]], writes=[t1B])
                        S.op("dve", lambda e, pt=ps[1][0], half=half: e.tensor_tensor(
                            out=t2[:], in0=pt[:], in1=gsb[1][0][:, half * 512:(half + 1) * 512], op=ALU.mult),
                            reads=[ps[1][1], gsb[1][1]], writes=[t2B])
                        S.op("pool", lambda e, nn=nn, half=half: e.tensor_tensor(
                            out=mst[:, half * 512:(half + 1) * 512], in0=t1[:], in1=t2[:], op=ALU.add),
                            reads=[t1B, t2B], writes=[mstB])
                    S.dma("sp", MT[nn], mst[:], mstB, reads=[mstB], writes=[scrB[id(MT)]])
        S.barrier()
        yT_stack.close()

        with ExitStack() as ph:
            mT, mTB = sb("mT", [128, 32, 1024], BF16, ph)
            S.dma("sp", mT[:], MT.rearrange("n p t -> p n t"), mTB, reads=[scrB[id(MT)]], writes=[mTB])
            g1bc, g1bcB = sb("g1bc", [128, D], F32, ph)
            build_bc(g1bc, g1bcB, lambda kc: modT[:, 64 + kc, 0:1], modTB)
            wob = [sb("wob%d" % i, [128, 32, 512], BF16, ph) for i in range(2)]
            xp, xpB = sb("xp", [128, 512], F32, ph)
            xq, xqB = sb("xq", [128, 512], F32, ph)
            wov = w_o.rearrange("(kc p) n -> p kc n", p=128)
            S.dma("pool", wob[0][0][:], wov[:, :, 0:512], wob[0][1], writes=[wob[0][1]])
            for cg in range(8):
                wt, wB = wob[cg % 2]
                if cg + 1 < 8:
                    S.dma("pool", wob[(cg + 1) % 2][0][:], wov[:, :, (cg + 1) * 512:(cg + 2) * 512], wob[(cg + 1) % 2][1],
                          writes=[wob[(cg + 1) % 2][1]])
                for t in range(8):
                    pt, pB = nb()
                    for kc in range(32):
                        mm(pt[:], mT[:, kc, t * 128:(t + 1) * 128], wt[:, kc, :], pB, [wB, mTB], kc == 0, kc == 31)
                    S.dma("sp", xp[:], xo[t * 128:(t + 1) * 128, cg * 512:(cg + 1) * 512], xpB, writes=[xpB])
                    S.op("dve", lambda e, pt=pt, cg=cg: e.tensor_tensor(out=xq[:], in0=pt[:], in1=g1bc[:, cg * 512:(cg + 1) * 512], op=ALU.mult),
                         reads=[pB, g1bcB], writes=[xqB])
                    S.op("pool", lambda e: e.tensor_tensor(out=xp[:], in0=xq[:], in1=xp[:], op=ALU.add),
                         reads=[xqB, xpB], writes=[xpB])
                    S.dma("sp", X1[t * 128:(t + 1) * 128, cg * 512:(cg + 1) * 512], xp[:], xpB, reads=[xpB], writes=[scrB[id(X1)]])
        S.barrier()

        with ExitStack() as ph:
            acc, accB = sb("acc", [128, 4, D], F32, ph)
            h2T, h2TB = sb("h2T", [128, 32, 512], BF16, ph)
            st2, st2B = sb("st2", [128, 2], F32, ph)
            w1v = w1.rearrange("(kc p) n -> p kc n", p=128)
            w2v = w2.rearrange("(hc p) n -> p hc n", p=128)
            for half in range(2):
                with ExitStack() as ph1:
                    norm_T(ph1, X1[half * 512:(half + 1) * 512], 4, h2T, h2TB, lambda kc: G2[:, kc:kc + 1], G2B,
                           lambda kc: modT[:, 96 + kc, 0:1], modTB, "m%d" % half)
                S.barrier()
                NHB = 64
                with ExitStack() as phw:
                    w1b = [sb("w1b%d_%d" % (i, half), [128, 32, 256], BF16, phw) for i in range(2)]
                    w2b = [sb("w2b%d_%d" % (i, half), [128, 2, D], BF16, phw) for i in range(2)]
                    aT, aTB = sb("aT%d" % half, [128, 2, 512], BF16, phw)
                    rl, rlB = sb("rl%d" % half, [128, 512], F32, phw)

                    def ldw12(hb):
                        S.dma("pool", w1b[hb % 2][0][:], w1v[:, :, hb * 256:(hb + 1) * 256], w1b[hb % 2][1], writes=[w1b[hb % 2][1]])
                        S.dma("pool", w2b[hb % 2][0][:], w2v[:, hb * 2:(hb + 1) * 2, :], w2b[hb % 2][1], writes=[w2b[hb % 2][1]])
                    ldw12(0)
                    for hb in range(NHB):
                        if hb + 1 < NHB:
                            ldw12(hb + 1)
                        w1t, w1B = w1b[hb % 2]
                        w2t, w2B = w2b[hb % 2]
                        for hc in range(2):
                            pt, pB = nb()
                            for kc in range(32):
                                mm(pt[:], w1t[:, kc, hc * 128:(hc + 1) * 128], h2T[:, kc, :], pB, [w1B, h2TB], kc == 0, kc == 31)
                            S.op("act", lambda e, pt=pt: e.activation(out=rl[:], in_=pt[:], func=AF.Relu), reads=[pB], writes=[rlB])
                            S.op("pool", lambda e, hc=hc: e.tensor_tensor(out=aT[:, hc, :], in0=rl[:], in1=rl[:], op=ALU.mult),
                                 reads=[rlB], writes=[aTB])
                        for tt in range(4):
                            for cg in range(8):
                                pt, pB = nb()
                                for hc in range(2):
                                    mm(pt[:], aT[:, hc, tt * 128:(tt + 1) * 128], w2t[:, hc, cg * 512:(cg + 1) * 512], pB,
                                       [aTB, w2B], hc == 0, hc == 1)
                                if hb == 0:
                                    S.op("act", lambda e, pt=pt, tt=tt, cg=cg: e.activation(
                                        out=acc[:, tt, cg * 512:(cg + 1) * 512], in_=pt[:], func=AF.Copy), reads=[pB], writes=[accB])
                                else:
                                    S.op("dve", lambda e, pt=pt, tt=tt, cg=cg: e.tensor_tensor(
                                        out=acc[:, tt, cg * 512:(cg + 1) * 512], in0=pt[:], in1=acc[:, tt, cg * 512:(cg + 1) * 512], op=ALU.add),
                                        reads=[pB, accB], writes=[accB])
                S.barrier()
                with ExitStack() as ph2:
                    g2bc, g2bcB = sb("g2bc%d" % half, [128, D], F32, ph2)
                    build_bc(g2bc, g2bcB, lambda kc: modT[:, 160 + kc, 0:1], modTB)
                    fgt, fgtB = sb("fgt%d" % half, [128, D], F32, ph2)
                    S.dma("sp", fgt[:], fg_bc, fgtB, writes=[fgtB])
                    x1t, x1tB = sb("x1t%d" % half, [128, D], F32, ph2)
                    jk2, jk2B = sb("jk2%d" % half, [128, D], BF16, ph2)
                    for tt in range(4):
                        r0 = half * 512 + tt * 128
                        S.dma("sp", x1t[:], X1[r0:r0 + 128, :], x1tB, reads=[scrB[id(X1)]], writes=[x1tB])
                        S.op("dve", lambda e, tt=tt: e.tensor_tensor(out=acc[:, tt, :], in0=acc[:, tt, :], in1=g2bc[:], op=ALU.mult),
                             reads=[accB, g2bcB], writes=[accB])
                        S.op("pool", lambda e, tt=tt: e.tensor_tensor(out=acc[:, tt, :], in0=acc[:, tt, :], in1=x1t[:], op=ALU.add),
                             reads=[accB, x1tB], writes=[accB])
                        S.op("act", lambda e, tt=tt: e.activation(out=jk2[:], in_=acc[:, tt, :], func=AF.Square, accum_out=st2[:, 0:1]),
                             reads=[accB], writes=[jk2B, st2B])
                        S.op("dve", lambda e: e.tensor_scalar(out=st2[:, 1:2], in0=st2[:, 0:1], scalar1=1.0 / D, scalar2=EPS,
                                                              op0=ALU.mult, op1=ALU.add), reads=[st2B], writes=[st2B])
                        S.op("act", lambda e: e.activation(out=st2[:, 1:2], in_=st2[:, 1:2], func=AF.Ln), reads=[st2B], writes=[st2B])
                        S.op("act", lambda e: e.activation(out=st2[:, 1:2], in_=st2[:, 1:2], func=AF.Exp, scale=-0.5), reads=[st2B], writes=[st2B])
                        S.op("dve", lambda e, tt=tt: e.scalar_tensor_tensor(
                            out=acc[:, tt, :], in0=acc[:, tt, :], scalar=st2[:, 1:2], in1=fgt[:], op0=ALU.mult, op1=ALU.mult),
                            reads=[accB, st2B, fgtB], writes=[accB])
                        S.dma("sp", out[r0:r0 + 128, :], acc[:, tt, :], accB, reads=[accB], writes=[])
                S.barrier()
        S.barrier()
        S.emit()
    return nc


_NC_CACHE = {}


def _consts():
    p = np.arange(128)
    same = (p[:, None] // 64) == (p[None, :] // 64)
    le = (same & (p[:, None] <= p[None, :])).astype(np.float32)
    gt = (same & (p[:, None] > p[None, :])).astype(np.float32)
    ge = (same & (p[:, None] >= p[None, :])).astype(np.float32)
    lt = (same & (p[:, None] < p[None, :])).astype(np.float32)
    return dict(c_ident=np.eye(128, dtype=np.float32), c_le=le, c_gt=gt, c_ge=ge, c_lt=lt,
                c_ones=np.ones((128, 128), np.float32))


def _rope_tables(pos, scale):
    n_freq = 32
    inv_freq = (np.float32(10000.0) ** (-np.arange(n_freq, dtype=np.float32) / np.float32(n_freq))).astype(np.float32)
    row = (pos // 64).astype(np.float32)
    col = (pos % 64).astype(np.float32)
    ang = np.concatenate([row[:, None] * inv_freq, col[:, None] * inv_freq], axis=-1).astype(np.float32)
    return (np.cos(ang) * np.float32(scale)).astype(np.float32), (np.sin(ang) * np.float32(scale)).astype(np.float32)


def kernel(x, c, ctx, c_ctx, w_mod, b_mod, norm1_g, norm2_g, w_in, hg_lb_logits, hg_norm_g,
           ret_decay_logit, w_branch_hgrn, w_branch_ret, w_out, w_ff1, w_ff2, final_norm_g):
    f32 = lambda a: np.ascontiguousarray(np.asarray(a, dtype=np.float32))
    x, c, ctx, c_ctx = f32(x), f32(c), f32(ctx), f32(c_ctx)
    if "nc" not in _NC_CACHE:
        _NC_CACHE["nc"] = build_program()
    nc = _NC_CACHE["nc"]
    colT = lambda v: np.ascontiguousarray(f32(v).reshape(-1, 128).T)
    shared = dict(
        w_mod=f32(w_mod)[0], b_modT=colT(f32(b_mod)[0]), n1gT=colT(f32(norm1_g)[0]), n2gT=colT(f32(norm2_g)[0]),
        fg_bc=np.ascontiguousarray(np.broadcast_to(f32(final_norm_g)[None, :], (128, D))),
        w_in=f32(w_in)[0],
        lbl=np.ascontiguousarray(np.broadcast_to(f32(hg_lb_logits)[None], (128, 2, 2, 2048))),
        hgain=np.ascontiguousarray(np.broadcast_to(f32(hg_norm_g)[0][None, :], (128, 128))),
        rdl=np.ascontiguousarray(np.broadcast_to(f32(ret_decay_logit)[0][None], (128, 2, 16))),
        w_bh=f32(w_branch_hgrn)[0], w_br=f32(w_branch_ret)[0], w_o=f32(w_out)[0], w1=f32(w_ff1)[0], w2=f32(w_ff2)[0],
    )
    shared.update(_consts())
    sk = 128.0 ** -0.5
    in_maps = []
    for core in range(8):
        b, s = core // 4, core % 4
        others = [j for j in range(4) if j != s]
        xf = np.concatenate([ctx[b]] + [x[b, j * 1024:(j + 1) * 1024] for j in others], axis=0)
        ccm = np.stack([c[b], c_ctx], axis=-1).reshape(32, 128, 2).transpose(1, 0, 2)
        cosK = np.empty((NT_ALL * 128, 64), np.float32)
        sinK = np.empty((NT_ALL * 128, 64), np.float32)
        cosK[:256] = sk
        sinK[:256] = 0.0
        for fi, j in enumerate(others):
            cs, sn = _rope_tables(np.arange(j * 1024, (j + 1) * 1024), sk)
            cosK[256 + fi * 1024:256 + (fi + 1) * 1024] = cs
            sinK[256 + fi * 1024:256 + (fi + 1) * 1024] = sn
        cs, sn = _rope_tables(np.arange(s * 1024, (s + 1) * 1024), sk)
        cosK[3328:] = cs
        sinK[3328:] = sn
        rK = np.stack([cosK.reshape(NT_ALL, 128, 64), sinK.reshape(NT_ALL, 128, 64)], axis=2).transpose(1, 0, 2, 3)
        cq, sq = _rope_tables(np.arange(s * 1024, (s + 1) * 1024), 1.0)
        rQ = np.stack([cq.reshape(8, 128, 64), sq.reshape(8, 128, 64)], axis=2).transpose(1, 0, 2, 3)
        mkv = np.ones((NT_ALL, 2), np.float32)
        for fi, j in enumerate(others):
            mkv[2 + fi * 8:2 + (fi + 1) * 8, 0] = 1.0 if j < s else 0.0
            mkv[2 + fi * 8:2 + (fi + 1) * 8, 1] = 1.0 if j > s else 0.0
        mk = np.broadcast_to(mkv[None], (128, NT_ALL, 2))
        sel = np.zeros((128, 2), np.float32)
        sel[:64, 0] = 1.0
        sel[64:, 1] = 1.0
        selm = sel[:, None, None, :] * mkv[None, :, :, None]
        m = dict(shared)
        m.update(xo=np.ascontiguousarray(x[b, s * 1024:(s + 1) * 1024]), xf=np.ascontiguousarray(xf),
                 cc=np.ascontiguousarray(ccm), ropeK=np.ascontiguousarray(rK), ropeQ=np.ascontiguousarray(rQ),
                 mk=np.ascontiguousarray(mk), selm=np.ascontiguousarray(selm.astype(np.float32)))
        in_maps.append(m)
    res = run_bass_kernel_spmd(nc, in_maps, core_ids=list(range(8)))
    outp = np.empty((2, 4096, D), np.float32)
    for core in range(8):
        b, s = core // 4, core % 4
        outp[b, s * 1024:(s + 1) * 1024] = res.results[core]["out"]
    return outp
```

```python
import numpy as np
from contextlib import ExitStack
import ml_dtypes
import concourse.bass as bass
import concourse.mybir as mybir
from concourse.bass_utils import run_bass_kernel_spmd

F32 = mybir.dt.float32
BF16 = mybir.dt.bfloat16
AF = mybir.ActivationFunctionType
ALU = mybir.AluOpType
AX = mybir.AxisListType

D = 4096
NT_ALL = 34
OWN0 = 26
EPS = 1e-6
DEBUG = False


class Buf:
    __slots__ = ("name", "w", "r", "dsem", "dcnt")

    def __init__(self, name):
        self.name = name
        self.w = None
        self.r = []
        self.dsem = None
        self.dcnt = 0


class Sched:
    ENG = ("pe", "act", "dve", "pool", "sp")

    def __init__(self, nc, es):
        self.nc = nc
        self.es = es
        self.streams = {e: [] for e in self.ENG}
        self.sems = {}
        self.cnt = {e: 0 for e in self.ENG}
        self.known = {e: {} for e in self.ENG}
        for e in self.ENG:
            self.sems[e] = es.enter_context(nc.semaphore("s_" + e))
        self.nd = 0
        self.dbufs = []

    def _dsem(self, buf):
        if buf.dsem is None:
            key = "d%d" % self.nd
            self.nd += 1
            self.sems[key] = self.es.enter_context(self.nc.semaphore(key))
            buf.dsem = key
            self.dbufs.append(buf)
        return buf.dsem

    def _need(self, eng, need):
        kn = self.known[eng]
        for k, v in need.items():
            if k == eng and v > self.cnt[eng]:
                continue
            if kn.get(k, 0) < v:
                kn[k] = v
                sem = self.sems[k]
                self.streams[eng].append(lambda e, sem=sem, v=v: e.wait_ge(sem, v))

    def _waits(self, eng, reads, writes):
        need = {}
        for b in reads:
            if b.w is not None:
                k, v = b.w
                need[k] = max(need.get(k, 0), v)
        for b in writes:
            if b.w is not None:
                k, v = b.w
                need[k] = max(need.get(k, 0), v)
            for (k, v) in b.r:
                need[k] = max(need.get(k, 0), v)
        self._need(eng, need)

    def _commit(self, ev, reads, writes):
        for b in reads:
            if len(b.r) > 64:
                m = {}
                for (k, v) in b.r:
                    m[k] = max(m.get(k, 0), v)
                b.r = list(m.items())
            b.r.append(ev)
        for b in writes:
            b.w = ev
            b.r = []

    def op(self, eng, fn, reads=(), writes=()):
        self._waits(eng, reads, writes)
        self.cnt[eng] += 1
        ev = (eng, self.cnt[eng])
        sem = self.sems[eng]
        self.streams[eng].append(lambda e, fn=fn, sem=sem: fn(e).then_inc(sem, 1))
        self._commit(ev, reads, writes)

    def op_noinc(self, eng, fn, reads=(), writes=()):
        self._waits(eng, reads, writes)
        ev = (eng, self.cnt[eng] + 1)
        self.streams[eng].append(lambda e, fn=fn: fn(e))
        self._commit(ev, reads, writes)

    def dma(self, q, out_ap, in_ap, sb, reads=(), writes=()):
        self._waits(q, reads, writes)
        key = self._dsem(sb)
        sb.dcnt += 16
        ev = (key, sb.dcnt)
        sem = self.sems[key]
        self.streams[q].append(
            lambda e, o=out_ap, i=in_ap, sem=sem: e.dma_start(out=o, in_=i).then_inc(sem, 16))
        self._commit(ev, reads, writes)

    def barrier(self):
        need = {e: self.cnt[e] for e in self.ENG if self.cnt[e] > 0}
        for b in self.dbufs:
            if b.dcnt > 0:
                need[b.dsem] = b.dcnt
        for e in self.ENG:
            self._need(e, dict(need))

    def emit(self):
        nc = self.nc
        st = self.streams
        with nc.Block() as block:
            @block.tensor
            def _(e):
                for f in st["pe"]:
                    f(e)

            @block.scalar
            def _(e):
                for f in st["act"]:
                    f(e)

            @block.vector
            def _(e):
                for f in st["dve"]:
                    f(e)

            @block.gpsimd
            def _(e):
                for f in st["pool"]:
                    f(e)

            @block.sync
            def _(e):
                for f in st["sp"]:
                    f(e)


def build_program():
    nc = bass.Bass("TRN2", target_bir_lowering=False)

    def din(name, shape, dt=F32):
        return nc.dram_tensor(name, list(shape), dt, kind="ExternalInput").ap()

    xo = din("xo", [1024, D])
    xf = din("xf", [3328, D])
    cc = din("cc", [128, 32, 2])
    w_mod = din("w_mod", [D, 6 * D])
    b_modT = din("b_modT", [128, 192])
    n1gT = din("n1gT", [128, 32])
    n2gT = din("n2gT", [128, 32])
    fg_bc = din("fg_bc", [128, D])
    w_in = din("w_in", [D, 26624])
    wz = din("wz", [3, D, 2048])
    lbl = din("lbl", [128, 2, 2, 2048])
    hgain = din("hgain", [128, 128])
    rdl = din("rdl", [128, 2, 16])
    w_bh = din("w_bh", [2048, D])
    w_br = din("w_br", [2048, D])
    w_o = din("w_o", [D, D])
    w1 = din("w1", [D, 4 * D])
    w2 = din("w2", [4 * D, D])
    c_ident = din("c_ident", [128, 128])
    c_le = din("c_le", [128, 128])
    c_gt = din("c_gt", [128, 128])
    c_ge = din("c_ge", [128, 128])
    c_lt = din("c_lt", [128, 128])
    c_ones = din("c_ones", [128, 128])
    ropeK = din("ropeK", [128, NT_ALL, 2, 64])
    ropeQ = din("ropeQ", [128, 8, 2, 64])
    mk_d = din("mk", [128, NT_ALL, 2])
    selm_d = din("selm", [128, NT_ALL, 2, 2])
    out = nc.dram_tensor("out", [1024, D], F32, kind="ExternalOutput").ap()

    def dscr(name, shape, dt):
        if DEBUG and name in ("X1", "MT", "YT", "MODT"):
            return nc.dram_tensor(name, list(shape), dt, kind="ExternalOutput").ap()
        return nc.dram_tensor(name, list(shape), dt).ap()

    ZF = dscr("ZF", [16, 128, NT_ALL, 128], F32)
    ZB = dscr("ZB", [16, 128, NT_ALL, 128], F32)
    VH = dscr("VH", [16, 128, NT_ALL, 128], BF16)
    RK = dscr("RK", [16, 128, NT_ALL, 128], BF16)
    RV = dscr("RV", [16, 128, NT_ALL, 128], BF16)
    HQ = dscr("HQ", [16, 128, 8, 128], BF16)
    HG = dscr("HG", [16, 128, 8, 128], BF16)
    RQ = dscr("RQ", [16, 128, 8, 128], BF16)
    RG = dscr("RG", [16, 128, 8, 128], BF16)
    GT = dscr("GT", [64, 128, 1024], BF16)
    X1 = dscr("X1", [1024, D], F32)
    MT = dscr("MT", [32, 128, 1024], BF16)
    if DEBUG:
        YT = dscr("YT", [128, 32, 1024], BF16)
        MODT = dscr("MODT", [128, 192, 2], F32)
    scrB = {id(t): Buf("scr%d" % i) for i, t in enumerate([ZF, ZB, VH, RK, RV, HQ, HG, RQ, RG, GT, X1, MT])}

    with ExitStack() as es:
        S = Sched(nc, es)

        sbn = [0]

        def sb(name, shape, dt, stack=None):
            sbn[0] += 1
            t = (stack or es).enter_context(nc.sbuf_tensor("sb%d_%s" % (sbn[0], name), list(shape), dt))
            return t, Buf(name)

        banks = []
        for i in range(8):
            t = es.enter_context(nc.psum_tensor("pb%d" % i, [128, 512], F32))
            banks.append((t, Buf("pb%d" % i)))
        bank_i = [0]

        def nb():
            b = banks[bank_i[0] % 8]
            bank_i[0] += 1
            return b

        def mm(outap, lhsT, rhs, pbuf, reads, first, last):
            fn = lambda e: e.matmul(outap, lhsT=lhsT, rhs=rhs, start=first, stop=last)
            if last:
                S.op("pe", fn, reads=reads, writes=[pbuf])
            else:
                S.op_noinc("pe", fn, reads=reads, writes=[pbuf] if first else [])

        def tr(outap, inap, ident, pbuf, reads):
            S.op("pe", lambda e: e.transpose(outap, inap, ident), reads=reads, writes=[pbuf])

        ident, identB = sb("ident", [128, 128], F32)
        m_le, m_leB = sb("m_le", [128, 128], F32)
        m_gt, m_gtB = sb("m_gt", [128, 128], F32)
        m_ge, m_geB = sb("m_ge", [128, 128], F32)
        m_lt, m_ltB = sb("m_lt", [128, 128], F32)
        ones, onesB = sb("ones", [128, 128], F32)
        for t, b, src in ((ident, identB, c_ident), (m_le, m_leB, c_le), (m_gt, m_gtB, c_gt),
                          (m_ge, m_geB, c_ge), (m_lt, m_ltB, c_lt), (ones, onesB, c_ones)):
            S.dma("sp", t[:], src, b, writes=[b])
        modT, modTB = sb("modT", [128, 192, 2], F32)
        G1, G1B = sb("G1", [128, 32, 2], F32)
        G2, G2B = sb("G2", [128, 32], F32)
        n1g, n1gB = sb("n1g", [128, 32], F32)
        n2g, n2gB = sb("n2g", [128, 32], F32)
        bmod, bmodB = sb("bmod", [128, 192], F32)
        S.dma("sp", n1g[:], n1gT, n1gB, writes=[n1gB])
        S.dma("sp", n2g[:], n2gT, n2gB, writes=[n2gB])
        S.dma("sp", bmod[:], b_modT, bmodB, writes=[bmodB])
        mk, mkB = sb("mk", [128, NT_ALL, 2], F32)
        selm, selmB = sb("selm", [128, NT_ALL, 2, 2], F32)
        S.dma("sp", mk[:], mk_d, mkB, writes=[mkB])
        S.dma("sp", selm[:], selm_d, selmB, writes=[selmB])
        lg, lgB = sb("lg", [128, 2, 16], F32)
        S.dma("sp", lg[:], rdl, lgB, writes=[lgB])
        S.op("act", lambda e: e.activation(out=lg[:], in_=lg[:], func=AF.Sigmoid), reads=[lgB], writes=[lgB])
        S.op("act", lambda e: e.activation(out=lg[:], in_=lg[:], func=AF.Ln), reads=[lgB], writes=[lgB])

        with ExitStack() as ph:
            ccs, ccsB = sb("ccs", [128, 32, 2], F32, ph)
            ccb, ccbB = sb("ccb", [128, 32, 2], BF16, ph)
            sg0, sg0B = sb("sg0", [128, 32, 2], F32, ph)
            S.dma("sp", ccs[:], cc, ccsB, writes=[ccsB])
            S.op("act", lambda e: e.activation(out=sg0[:], in_=ccs[:], func=AF.Sigmoid), reads=[ccsB], writes=[sg0B])
            S.op("dve", lambda e: e.tensor_tensor(out=ccb[:], in0=ccs[:], in1=sg0[:], op=ALU.mult),
                 reads=[ccsB, sg0B], writes=[ccbB])
            wm = [sb("wm%d" % i, [128, 32, 512], BF16, ph) for i in range(2)]
            wv = w_mod.rearrange("(kc p) n -> p kc n", p=128)
            pm, pmB = nb()
            pmv = pm[:, 0:384].rearrange("p (n j) -> p n j", j=2)
            NB = 48
            S.dma("pool", wm[0][0][:], wv[:, :, 0:512], wm[0][1], writes=[wm[0][1]])
            for g in range(NB):
                wt, wB = wm[g % 2]
                if g + 1 < NB:
                    S.dma("pool", wm[(g + 1) % 2][0][:], wv[:, :, (g + 1) * 512:(g + 2) * 512],
                          wm[(g + 1) % 2][1], writes=[wm[(g + 1) % 2][1]])
                for n in range(4):
                    nn = g * 4 + n
                    for kc in range(32):
                        mm(pmv[:, nn, :], wt[:, kc, n * 128:(n + 1) * 128], ccb[:, kc, :], pmB,
                           [wB, ccbB], kc == 0, kc == 31)
            for j in range(2):
                S.op("dve", lambda e, j=j: e.tensor_tensor(out=modT[:, :, j], in0=pmv[:, :, j], in1=bmod[:], op=ALU.add),
                     reads=[pmB, bmodB], writes=[modTB])
            for j in range(2):
                S.op("dve", lambda e, j=j: e.scalar_tensor_tensor(
                    out=G1[:, :, j], in0=modT[:, 32:64, j], scalar=1.0, in1=n1g[:], op0=ALU.add, op1=ALU.mult),
                    reads=[modTB, n1gB], writes=[G1B])
            S.op("dve", lambda e: e.scalar_tensor_tensor(
                out=G2[:], in0=modT[:, 128:160, 0], scalar=1.0, in1=n2g[:], op0=ALU.add, op1=ALU.mult),
                reads=[modTB, n2gB], writes=[G2B])
        S.barrier()
        if DEBUG:
            S.dma("sp", MODT, modT[:], modTB, reads=[modTB], writes=[])

        def build_bc(dst, dstB, colap_fn, colB):
            with ExitStack() as ph2:
                dg, dgB = sb("dg_" + dstB.name, [128, 128], F32, ph2)
                for kc in range(32):
                    S.op("dve", lambda e, kc=kc: e.tensor_scalar(
                        out=dg[:], in0=ident[:], scalar1=colap_fn(kc), scalar2=None, op0=ALU.mult),
                        reads=[identB, colB], writes=[dgB])
                    pt, pB = nb()
                    mm(pt[:, 0:128], ones[:], dg[:], pB, [onesB, dgB], True, True)
                    S.op("act", lambda e, kc=kc, pt=pt: e.activation(
                        out=dst[:, kc * 128:(kc + 1) * 128], in_=pt[:, 0:128], func=AF.Copy),
                        reads=[pB], writes=[dstB])
            S.barrier()

        def norm_T(ph, xsrc, nt, hT, hTB, Gcol, GB, SHcol, SHB, tag):
            xts = [sb("xt%d_%s" % (i, tag), [128, D], F32, ph) for i in range(2)]
            jk, jkB = sb("jk_" + tag, [128, D], BF16, ph)
            sts = [sb("st%d_%s" % (i, tag), [128, 2], F32, ph) for i in range(2)]
            for t in range(nt):
                xt, xtB = xts[t % 2]
                st, stB = sts[t % 2]
                S.dma("sp", xt[:], xsrc[t * 128:(t + 1) * 128, :], xtB, writes=[xtB])
                S.op("act", lambda e, xt=xt, st=st: e.activation(out=jk[:], in_=xt[:], func=AF.Square, accum_out=st[:, 0:1]),
                     reads=[xtB], writes=[jkB, stB])
                S.op("dve", lambda e, st=st: e.tensor_scalar(out=st[:, 1:2], in0=st[:, 0:1], scalar1=1.0 / D, scalar2=EPS,
                                                      op0=ALU.mult, op1=ALU.add), reads=[stB], writes=[stB])
                S.op("act", lambda e, st=st: e.activation(out=st[:, 1:2], in_=st[:, 1:2], func=AF.Ln), reads=[stB], writes=[stB])
                S.op("act", lambda e, st=st: e.activation(out=st[:, 1:2], in_=st[:, 1:2], func=AF.Exp, scale=-0.5), reads=[stB], writes=[stB])
                S.op("dve", lambda e, xt=xt, st=st: e.tensor_scalar(out=xt[:], in0=xt[:], scalar1=st[:, 1:2], scalar2=None,
                                                      op0=ALU.mult), reads=[stB, xtB], writes=[xtB])
                for k4 in range(8):
                    pt, pB = nb()
                    for q in range(4):
                        kc = k4 * 4 + q
                        tr(pt[:, q * 128:(q + 1) * 128], xt[:, kc * 128:(kc + 1) * 128], ident[:], pB, [xtB, identB])
                    for q in range(4):
                        kc = k4 * 4 + q
                        S.op("act", lambda e, kc=kc, q=q, pt=pt, t=t: e.activation(
                            out=hT[:, kc, t * 128:(t + 1) * 128], in_=pt[:, q * 128:(q + 1) * 128],
                            func=AF.Identity, scale=Gcol(kc), bias=SHcol(kc)),
                            reads=[pB, GB, SHB], writes=[hTB])

        FAMS = [("vh", 0, "copy", VH, BF16), ("zf", 2048, "z", ZF, F32), ("zb", 4096, "z", ZB, F32),
                ("rk", 6144, "ropeK", RK, BF16), ("rv", 8192, "copy", RV, BF16),
                ("hq", 10240, "siluc", HQ, BF16), ("hg", 12288, "silu", HG, BF16),
                ("rq", 14336, "ropeQ", RQ, BF16), ("rg", 16384, "silu", RG, BF16)]
        winv = w_in.rearrange("(kc p) n -> p kc n", p=128)

        def seg_pass(xsrc, nt, t0, own, modj, tag, fsel=None):
            with ExitStack() as ph:
                hT, hTB = sb("hT_" + tag, [128, 32, nt * 128], BF16, ph)
                wb = [sb("wi%d_%s" % (i, tag), [128, 32, 512], BF16, ph) for i in range(2)]
                S.dma("pool", wb[0][0][:], winv[:, :, 0:512], wb[0][1], writes=[wb[0][1]])
                with ExitStack() as ph1:
                    norm_T(ph1, xsrc, nt, hT, hTB, lambda kc: G1[:, kc, modj:modj + 1], G1B,
                           lambda kc: modT[:, kc, modj:modj + 1], modTB, tag)
                S.barrier()
                stg32, stg32B = sb("s32_" + tag, [128, 4, nt, 128], F32, ph)
                stg16, stg16B = sb("s16_" + tag, [128, 4, nt, 128], BF16, ph)
                tmp, tmpB = sb("tmp_" + tag, [128, 512], F32, ph)
                tmp2, tmp2B = sb("tmp2_" + tag, [128, 512], F32, ph)
                rK, rKB = sb("rK_" + tag, [128, nt, 2, 64], F32, ph)
                S.dma("sp", rK[:], ropeK[:, t0:t0 + nt], rKB, writes=[rKB])
                if own:
                    rQ, rQB = sb("rQ_" + tag, [128, 8, 2, 64], F32, ph)
                    S.dma("sp", rQ[:], ropeQ, rQB, writes=[rQB])
                    gst, gstB = sb("gst_" + tag, [128, 1024], BF16, ph)
                fams = FAMS if own else FAMS[:5]
                groups = []
                for (fn_, off, kind, scr, dt) in fams:
                    if fsel is not None and fn_ == "zb":
                        continue
                    for g4 in range(4):
                        if fsel is not None and fn_ == "zf":
                            groups.append((g4 * 512, kind, scr, dt, g4, wz[fsel].rearrange("(kc p) n -> p kc n", p=128)))
                        else:
                            groups.append((off + g4 * 512, kind, scr, dt, g4, winv))
                if own:
                    for gg in range(16):
                        groups.append((18432 + gg * 512, "gate", GT, BF16, gg, winv))
                ng = len(groups)
                assert groups[0][0] == 0
                for gi, (coff, kind, scr, dt, g4, wsrc) in enumerate(groups):
                    wt, wB = wb[gi % 2]
                    if gi + 1 < ng:
                        c2 = groups[gi + 1][0]
                        S.dma("pool", wb[(gi + 1) % 2][0][:], groups[gi + 1][5][:, :, c2:c2 + 512], wb[(gi + 1) % 2][1],
                              writes=[wb[(gi + 1) % 2][1]])
                    if kind == "gate":
                        for n in range(4):
                            for half in range(2):
                                pt, pB = nb()
                                for kc in range(32):
                                    mm(pt[:], wt[:, kc, n * 128:(n + 1) * 128], hT[:, kc, half * 512:(half + 1) * 512],
                                       pB, [wB, hTB], kc == 0, kc == 31)
                                S.op("act", lambda e, pt=pt, half=half: e.activation(
                                    out=gst[:, half * 512:(half + 1) * 512], in_=pt[:], func=AF.Sigmoid),
                                    reads=[pB], writes=[gstB])
                            S.dma("sp", GT[g4 * 4 + n], gst[:], gstB, reads=[gstB], writes=[scrB[id(GT)]])
                        continue
                    stg, stgB = (stg32, stg32B) if dt == F32 else (stg16, stg16B)
                    for t in range(nt):
                        pt, pB = nb()
                        for kc in range(32):
                            mm(pt[:], hT[:, kc, t * 128:(t + 1) * 128], wt[:, kc, :], pB, [wB, hTB], kc == 0, kc == 31)
                        pv = pt[:].rearrange("p (h c) -> p h c", h=4)
                        if kind in ("copy", "z"):
                            S.op("act", lambda e, pv=pv, t=t, stg=stg: e.activation(out=stg[:, :, t, :], in_=pv, func=AF.Copy),
                                 reads=[pB], writes=[stgB])
                        elif kind in ("silu", "siluc"):
                            S.op("act", lambda e, pt=pt: e.activation(out=tmp[:], in_=pt[:], func=AF.Sigmoid),
                                 reads=[pB], writes=[tmpB])
                            cst = (128.0 ** -0.5) if kind == "siluc" else 1.0
                            S.op("dve", lambda e, pv=pv, t=t, cst=cst, stg=stg: e.scalar_tensor_tensor(
                                out=stg[:, :, t, :], in0=pv, scalar=cst, in1=tmp[:].rearrange("p (h c) -> p h c", h=4),
                                op0=ALU.mult, op1=ALU.mult), reads=[pB, tmpB], writes=[stgB])
                        else:
                            rt, rtB = (rK, rKB) if kind == "ropeK" else (rQ, rQB)
                            S.op("act", lambda e, pt=pt: e.activation(out=tmp[:], in_=pt[:], func=AF.Copy),
                                 reads=[pB], writes=[tmpB])
                            tv = tmp[:].rearrange("p (h c two) -> p h c two", h=4, two=2)
                            t2v = tmp2[:].rearrange("p (h c two) -> p h c two", h=4, two=2)
                            sv = stg[:, :, t, :].rearrange("p h (c two) -> p h c two", two=2)
                            for h in range(4):
                                a1, a2 = tv[:, h, :, 0], tv[:, h, :, 1]
                                cs, sn = rt[:, t, 0, :], rt[:, t, 1, :]
                                b1, b2 = t2v[:, h, :, 0], t2v[:, h, :, 1]
                                S.op("dve", lambda e, a1=a1, cs=cs, b1=b1: e.tensor_tensor(out=b1, in0=a1, in1=cs, op=ALU.mult),
                                     reads=[tmpB, rtB], writes=[tmp2B])
                                S.op("dve", lambda e, a2=a2, sn=sn, b2=b2: e.tensor_tensor(out=b2, in0=a2, in1=sn, op=ALU.mult),
                                     reads=[tmpB, rtB], writes=[tmp2B])
                                S.op("dve", lambda e, b1=b1, b2=b2, o=sv[:, h, :, 0]: e.tensor_tensor(out=o, in0=b1, in1=b2, op=ALU.subtract),
                                     reads=[tmp2B], writes=[stgB])
                                S.op("dve", lambda e, a1=a1, sn=sn, b1=b1: e.tensor_tensor(out=b1, in0=a1, in1=sn, op=ALU.mult),
                                     reads=[tmpB, rtB, stgB], writes=[tmp2B])
                                S.op("dve", lambda e, a2=a2, cs=cs, b2=b2: e.tensor_tensor(out=b2, in0=a2, in1=cs, op=ALU.mult),
                                     reads=[tmpB, rtB], writes=[tmp2B])
                                S.op("dve", lambda e, b1=b1, b2=b2, o=sv[:, h, :, 1]: e.tensor_tensor(out=o, in0=b1, in1=b2, op=ALU.add),
                                     reads=[tmp2B], writes=[stgB])
                    ts0 = t0 if scr.shape[2] == NT_ALL else 0
                    S.dma("sp", scr[g4 * 4:(g4 + 1) * 4, :, ts0:ts0 + nt, :].rearrange("h p t c -> p h t c"),
                          stg[:], stgB, reads=[stgB], writes=[scrB[id(scr)]])
            S.barrier()

        seg_pass(xf[0:256], 2, 0, False, 1, "c")
        for f in range(3):
            seg_pass(xf[256 + f * 1024:256 + (f + 1) * 1024], 8, 2 + f * 8, False, 0, "f%d" % f, fsel=f)
        seg_pass(xo, 8, OWN0, True, 0, "o")

        yT_stack = ExitStack()
        yT, yTB = sb("yT", [128, 32, 1024], BF16, yT_stack)
        with ExitStack() as ph:
            NTA = NT_ALL
            zt = [sb("zt%d" % d, [128, NTA, 128], F32, ph) for d in range(2)]
            kt = [sb("kt%d" % d, [128, NTA, 128], BF16, ph) for d in range(2)]
            kd = [sb("kd%d" % d, [128, NTA, 128], BF16, ph) for d in range(2)]
            vt, vtB = sb("vt", [128, NTA, 128], BF16, ph)
            qt, qtB = sb("qt", [128, 8, 128], BF16, ph)
            gt_, gtB = sb("gt", [128, 8, 128], BF16, ph)
            gg, ggB = sb("gg", [128, 8, 128], F32, ph)
            lbt, lbtB = sb("lbt", [128, 2, 2, 128], F32, ph)
            lb, lbB = sb("lb", [128, 2, 128], F32, ph)
            oml, omlB = sb("oml", [128, 2, 128], F32, ph)
            hgn, hgnB = sb("hgn", [128, 128], F32, ph)
            S.dma("sp", hgn[:], hgain, hgnB, writes=[hgnB])
            dd = [sb("dd%d" % d, [128, NTA, 2], F32, ph) for d in range(2)]
            Sst = [sb("Sst%d" % d, [128, 128], F32, ph) for d in range(2)]
            Ssn = [sb("Ssn%d" % d, [128, 16, 128], BF16, ph) for d in range(2)]
            qe = [sb("qe%d" % d, [128, 128], F32, ph) for d in range(2)]
            ke = [sb("ke%d" % d, [128, 128], F32, ph) for d in range(2)]
            ee, eeB = sb("ee", [128, 512], F32, ph)
            qT = [sb("qT%d" % d, [128, 8, 128], BF16, ph) for d in range(2)]
            kT, kTB = sb("kT", [128, 128], BF16, ph)
            scT = [sb("scT%d" % d, [128, 8, 128], BF16, ph) for d in range(2)]
            osb, osbB = sb("osb", [128, 128], F32, ph)
            ysb, ysbB = sb("ysb", [128, 128], F32, ph)
            stt, sttB = sb("stt", [128, 8], F32, ph)
            bst, bstB = sb("bst", [128, 6], F32, ph)
            lgc, lgcB = sb("lgc", [128, 2], F32, ph)
            TRI_INC = [(m_le, m_leB), (m_ge, m_geB)]
            TRI_DEC = [(m_gt, m_gtB), (m_lt, m_ltB)]

            ee2, ee2B = sb("ee2", [128, 512], F32, ph)
            qe4, qe4B = sb("qe4", [128, 4, 128], F32, ph)
            ke4, ke4B = sb("ke4", [128, 4, 128], F32, ph)
            kT4, kT4B = sb("kT4", [128, 4, 128], BF16, ph)
            oall, oallB = sb("oall", [128, 8, 128], F32, ph)
            stall, stallB = sb("stall", [128, 3, 8], F32, ph)
            bst8, bst8B = sb("bst8", [128, 8, 6], F32, ph)
            mv8, mv8B = sb("mv8", [128, 8, 2], F32, ph)

            def bc3(ap2d, n):
                return ap2d.rearrange("p (o c) -> p o c", o=1).to_broadcast([128, n, 128])

            def loads_z(kind, h):
                if kind == "hg":
                    S.dma("sp", zt[0][0][:], ZF[h], zt[0][1], reads=[scrB[id(ZF)]], writes=[zt[0][1]])
                    S.dma("sp", zt[1][0][:, 0:2], ZB[h][:, 0:2], zt[1][1], reads=[scrB[id(ZB)]], writes=[zt[1][1]])
                    S.dma("sp", zt[1][0][:, 2:OWN0], ZF[h][:, 2:OWN0], zt[1][1], reads=[scrB[id(ZF)]], writes=[zt[1][1]])
                    S.dma("sp", zt[1][0][:, OWN0:NT_ALL], ZB[h][:, OWN0:NT_ALL], zt[1][1], reads=[scrB[id(ZB)]], writes=[zt[1][1]])
                else:
                    S.dma("sp", kt[0][0][:], RK[h], kt[0][1], reads=[scrB[id(RK)]], writes=[kt[0][1]])

            def scan_head(kind, h, nxt):
                hg = kind == "hg"
                if hg:
                    S.dma("sp", vt[:], VH[h], vtB, reads=[scrB[id(VH)]], writes=[vtB])
                    S.dma("sp", qt[:], HQ[h], qtB, reads=[scrB[id(HQ)]], writes=[qtB])
                    S.dma("sp", gt_[:], HG[h], gtB, reads=[scrB[id(HG)]], writes=[gtB])
                    S.dma("sp", lbt[:], lbl[:, :, :, h * 128:(h + 1) * 128], lbtB, writes=[lbtB])
                    S.op("dve", lambda e: e.tensor_tensor(out=lb[:], in0=lbt[:, 0], in1=lbt[:, 1], op=ALU.subtract),
                         reads=[lbtB], writes=[lbB])
                    S.op("act", lambda e: e.activation(out=lb[:], in_=lb[:], func=AF.Sigmoid), reads=[lbB], writes=[lbB])
                    S.op("dve", lambda e: e.tensor_scalar(out=oml[:], in0=lb[:], scalar1=-1.0, scalar2=1.0,
                                                          op0=ALU.mult, op1=ALU.add), reads=[lbB], writes=[omlB])
                    for d in range(2):
                        z, zB = zt[d]
                        k_, kB_ = kt[d]
                        S.op("act", lambda e, z=z: e.activation(out=z[:], in_=z[:], func=AF.Sigmoid), reads=[zB], writes=[zB])
                        S.op("dve", lambda e, z=z, d=d: e.tensor_tensor(
                            out=z[:], in0=z[:], in1=oml[:, d:d + 1, :].to_broadcast([128, NTA, 128]), op=ALU.mult),
                            reads=[zB, omlB], writes=[zB])
                        S.op("dve", lambda e, z=z, d=d: e.tensor_tensor(
                            out=z[:], in0=z[:], in1=lb[:, d:d + 1, :].to_broadcast([128, NTA, 128]), op=ALU.add),
                            reads=[zB, lbB], writes=[zB])
                        S.op("dve", lambda e, z=z, k_=k_: e.tensor_scalar(out=k_[:], in0=z[:], scalar1=-1.0, scalar2=1.0,
                                                                         op0=ALU.mult, op1=ALU.add), reads=[zB], writes=[kB_])
                        S.op("pool", lambda e, k_=k_, d=d: e.tensor_tensor(
                            out=k_[:], in0=k_[:], in1=mk[:, :, d:d + 1].to_broadcast([128, NTA, 128]), op=ALU.mult),
                            reads=[kB_, mkB], writes=[kB_])
                        S.op("act", lambda e, z=z: e.activation(out=z[:], in_=z[:], func=AF.Ln), reads=[zB, kB_], writes=[zB])
                else:
                    S.dma("sp", vt[:], RV[h], vtB, reads=[scrB[id(RV)]], writes=[vtB])
                    S.dma("sp", qt[:], RQ[h], qtB, reads=[scrB[id(RQ)]], writes=[qtB])
                    S.dma("sp", gt_[:], RG[h], gtB, reads=[scrB[id(RG)]], writes=[gtB])
                    for d in range(2):
                        z, zB = zt[d]
                        S.op("dve", lambda e, d=d: e.tensor_copy(out=lgc[:, d:d + 1], in_=lg[:, d, h:h + 1]), reads=[lgB], writes=[lgcB])
                        S.op("pool", lambda e, z=z: e.memset(z[:], 0.0), writes=[zB])
                        S.op("act", lambda e, z=z, d=d: e.activation(out=z[:], in_=z[:], func=AF.Identity, scale=1.0, bias=lgc[:, d:d + 1]),
                             reads=[zB, lgcB], writes=[zB])
                    S.op("pool", lambda e: e.tensor_tensor(
                        out=kt[1][0][:], in0=kt[0][0][:], in1=mk[:, :, 1:2].to_broadcast([128, NTA, 128]), op=ALU.mult),
                        reads=[kt[0][1], mkB], writes=[kt[1][1]])
                    S.op("pool", lambda e: e.tensor_tensor(
                        out=kt[0][0][:], in0=kt[0][0][:], in1=mk[:, :, 0:1].to_broadcast([128, NTA, 128]), op=ALU.mult),
                        reads=[kt[0][1], mkB], writes=[kt[0][1]])
                if hg:
                    S.op("dve", lambda e: e.tensor_tensor(out=gg[:], in0=gt_[:], in1=bc3(hgn[:], 8), op=ALU.mult),
                         reads=[gtB, hgnB], writes=[ggB])
                else:
                    S.op("dve", lambda e: e.tensor_copy(out=gg[:], in_=gt_[:]), reads=[gtB], writes=[ggB])
                for d in range(2):
                    z, zB = zt[d]
                    k_, kB_ = kt[d]
                    kdt, kdB = kd[d]
                    ddt, ddB = dd[d]
                    tri_d, tri_dB = TRI_DEC[d]
                    tri_i, tri_iB = TRI_INC[d]
                    pc, pcB = nb()
                    pcv = pc[:, 0:NTA * 2].rearrange("p (t c) -> p t c", c=2)
                    for t in range(NTA):
                        mm(pcv[:, t, :], z[:, t, :], selm[:, t, d, :], pcB, [zB, selmB], True, True)
                    S.op("act", lambda e, pcv=pcv, ddt=ddt: e.activation(out=ddt[:], in_=pcv, func=AF.Exp), reads=[pcB], writes=[ddB])
                    for t4 in range(0, NTA, 4):
                        n4 = min(4, NTA - t4)
                        pt, pB = nb()
                        for q in range(n4):
                            mm(pt[:, q * 128:(q + 1) * 128], tri_d[:], z[:, t4 + q, :], pB, [tri_dB, zB], True, True)
                        S.op("act", lambda e, pt=pt, n4=n4: e.activation(out=ee[:, 0:n4 * 128], in_=pt[:, 0:n4 * 128], func=AF.Exp),
                             reads=[pB], writes=[eeB])
                        S.op("dve", lambda e, t4=t4, n4=n4, k_=k_, kdt=kdt: e.tensor_tensor(
                            out=kdt[:, t4:t4 + n4, :], in0=k_[:, t4:t4 + n4, :],
                            in1=ee[:, 0:n4 * 128].rearrange("p (t c) -> p t c", c=128), op=ALU.mult),
                            reads=[kB_, eeB], writes=[kdB])
                    qTt, qTB = qT[d]
                    scTt, scTB = scT[d]
                    for t4 in range(0, 8, 4):
                        ta = OWN0 + t4
                        pt, pB = nb()
                        for q in range(4):
                            mm(pt[:, q * 128:(q + 1) * 128], tri_i[:], z[:, ta + q, :], pB, [tri_iB, zB], True, True)
                        S.op("act", lambda e, pt=pt: e.activation(out=ee[:], in_=pt[:], func=AF.Exp), reads=[pB], writes=[eeB])
                        S.op("act", lambda e, pt=pt: e.activation(out=ee2[:], in_=pt[:], func=AF.Exp, scale=-1.0), reads=[pB], writes=[ee2B])
                        S.op("dve", lambda e, t4=t4: e.tensor_tensor(
                            out=qe4[:], in0=qt[:, t4:t4 + 4, :], in1=ee[:].rearrange("p (t c) -> p t c", c=128), op=ALU.mult),
                            reads=[qtB, eeB], writes=[qe4B])
                        S.op("dve", lambda e, ta=ta, k_=k_: e.tensor_tensor(
                            out=ke4[:], in0=k_[:, ta:ta + 4, :], in1=ee2[:].rearrange("p (t c) -> p t c", c=128), op=ALU.mult),
                            reads=[kB_, ee2B], writes=[ke4B])
                        p2, p2B = nb()
                        p2k, p2kB = nb()
                        for q in range(4):
                            tr(p2[:, q * 128:(q + 1) * 128], qe4[:, q, :], ident[:], p2B, [qe4B, identB])
                        for q in range(4):
                            tr(p2k[:, q * 128:(q + 1) * 128], ke4[:, q, :], ident[:], p2kB, [ke4B, identB])
                        S.op("act", lambda e, p2=p2, t4=t4, qTt=qTt: e.activation(
                            out=qTt[:, t4:t4 + 4, :], in_=p2[:].rearrange("p (t c) -> p t c", c=128), func=AF.Copy),
                            reads=[p2B], writes=[qTB])
                        S.op("act", lambda e, p2k=p2k: e.activation(
                            out=kT4[:], in_=p2k[:].rearrange("p (t c) -> p t c", c=128), func=AF.Copy),
                            reads=[p2kB], writes=[kT4B])
                        p3, p3B = nb()
                        for q in range(4):
                            mm(p3[:, q * 128:(q + 1) * 128], kT4[:, q, :], qTt[:, t4 + q, :], p3B, [kT4B, qTB], True, True)
                        S.op("dve", lambda e, p3=p3, t4=t4, scTt=scTt, tri_i=tri_i: e.tensor_tensor(
                            out=scTt[:, t4:t4 + 4, :], in0=p3[:].rearrange("p (t c) -> p t c", c=128),
                            in1=bc3(tri_i[:], 4), op=ALU.mult), reads=[p3B, tri_iB], writes=[scTB])
                orders = []
                for d in range(2):
                    St, StB = Sst[d]
                    S.op("pool", lambda e, St=St: e.memset(St[:], 0.0), writes=[StB])
                    if d == 0:
                        orders.append([(t, c) for t in range(NTA) for c in (0, 1)])
                    else:
                        tl = [1, 0] + list(range(25, 1, -1)) + list(range(33, 25, -1))
                        orders.append([(t, c) for t in tl for c in (1, 0)])
                for i0 in range(0, NTA * 2, 4):
                    pend = []
                    for d in range(2):
                        kdt, kdB = kd[d]
                        grp = orders[d][i0:i0 + 4]
                        pt, pB = nb()
                        for q, (t, c) in enumerate(grp):
                            mm(pt[:, q * 128:(q + 1) * 128], kdt[c * 64:(c + 1) * 64, t, :], vt[c * 64:(c + 1) * 64, t, :],
                               pB, [kdB, vtB], True, True)
                        pend.append((grp, pt, pB))
                    for q in range(4):
                        for d in range(2):
                            grp, pt, pB = pend[d]
                            t, c = grp[q]
                            ddt, ddB = dd[d]
                            St, StB = Sst[d]
                            Snt, SnB = Ssn[d]
                            if t >= OWN0:
                                ci = (t - OWN0) * 2 + c
                                S.op("act", lambda e, ci=ci, Snt=Snt, St=St: e.activation(out=Snt[:, ci, :], in_=St[:], func=AF.Copy),
                                     reads=[StB], writes=[SnB])
                            S.op("dve", lambda e, pt=pt, q=q, t=t, c=c, St=St, ddt=ddt: e.scalar_tensor_tensor(
                                out=St[:], in0=St[:], scalar=ddt[:, t, c:c + 1], in1=pt[:, q * 128:(q + 1) * 128],
                                op0=ALU.mult, op1=ALU.add), reads=[StB, ddB, pB, SnB], writes=[StB])
                if nxt is not None:
                    loads_z(*nxt)
                for t in range(8):
                    ta = OWN0 + t
                    po, poB = nb()
                    ov = po[:, 0:128]
                    for d in range(2):
                        qTt, qTB = qT[d]
                        scTt, scTB = scT[d]
                        Snt, SnB = Ssn[d]
                        mm(ov, scTt[:, t, :], vt[:, ta, :], poB, [scTB, vtB], d == 0, False)
                        for c in range(2):
                            mm(po[c * 64:(c + 1) * 64, 0:128], qTt[:, t, c * 64:(c + 1) * 64], Snt[:, t * 2 + c, :], poB,
                               [qTB, SnB], False, d == 1 and c == 1)
                    S.op("act", lambda e, ov=ov, t=t: e.activation(out=oall[:, t, :], in_=ov, func=AF.Copy), reads=[poB], writes=[oallB])
                    if hg:
                        S.op("act", lambda e, t=t: e.activation(out=osb[:], in_=oall[:, t, :], func=AF.Square, accum_out=stall[:, 0, t:t + 1]),
                             reads=[oallB], writes=[osbB, stallB])
                    else:
                        S.op("dve", lambda e, t=t: e.bn_stats(out=bst8[:, t, :], in_=oall[:, t, :]), reads=[oallB], writes=[bst8B])
                        S.op("dve", lambda e, t=t: e.bn_aggr(out=mv8[:, t, :], in_=bst8[:, t, :]), reads=[bst8B], writes=[mv8B])
                if hg:
                    S.op("dve", lambda e: e.tensor_scalar(out=stall[:, 1, :], in0=stall[:, 0, :], scalar1=1.0 / 128, scalar2=EPS,
                                                          op0=ALU.mult, op1=ALU.add), reads=[stallB], writes=[stallB])
                else:
                    S.op("dve", lambda e: e.tensor_scalar(out=stall[:, 1, :], in0=mv8[:, :, 1], scalar1=EPS, scalar2=None,
                                                          op0=ALU.add), reads=[mv8B], writes=[stallB])
                S.op("act", lambda e: e.activation(out=stall[:, 1, :], in_=stall[:, 1, :], func=AF.Ln), reads=[stallB], writes=[stallB])
                S.op("act", lambda e: e.activation(out=stall[:, 1, :], in_=stall[:, 1, :], func=AF.Exp, scale=-0.5), reads=[stallB], writes=[stallB])
                for t in range(8):
                    if hg:
                        S.op("dve", lambda e, t=t: e.scalar_tensor_tensor(
                            out=ysb[:], in0=oall[:, t, :], scalar=stall[:, 1, t:t + 1], in1=gg[:, t, :], op0=ALU.mult, op1=ALU.mult),
                            reads=[oallB, stallB, ggB], writes=[ysbB])
                    else:
                        S.op("dve", lambda e, t=t: e.tensor_scalar(out=osb[:], in0=oall[:, t, :], scalar1=mv8[:, t, 0:1],
                                                                   scalar2=stall[:, 1, t:t + 1], op0=ALU.subtract, op1=ALU.mult),
                             reads=[oallB, mv8B, stallB], writes=[osbB])
                        S.op("dve", lambda e, t=t: e.tensor_tensor(out=ysb[:], in0=osb[:], in1=gg[:, t, :], op=ALU.mult),
                             reads=[osbB, ggB], writes=[ysbB])
                    py, pyB = nb()
                    tr(py[:, 0:128], ysb[:], ident[:], pyB, [ysbB, identB])
                    hh = h if hg else 16 + h
                    S.op("act", lambda e, py=py, t=t, hh=hh: e.activation(out=yT[:, hh, t * 128:(t + 1) * 128], in_=py[:, 0:128], func=AF.Copy),
                         reads=[pyB], writes=[yTB])

            heads = [("hg", h) for h in range(16)] + [("ret", h) for h in range(16)]
            loads_z(*heads[0])
            for i, (kind, h) in enumerate(heads):
                scan_head(kind, h, heads[i + 1] if i + 1 < len(heads) else None)
        S.barrier()
        if DEBUG:
            S.dma("sp", YT, yT[:], yTB, reads=[yTB], writes=[])

        with ExitStack() as ph:
            mst, mstB = sb("mst", [128, 1024], BF16, ph)
            wbb = [sb("wbb%d" % i, [128, 2, 16, 256], BF16, ph) for i in range(2)]
            gsb = [sb("gsb%d" % i, [128, 1024], BF16, ph) for i in range(2)]
            t1, t1B = sb("t1", [128, 512], F32, ph)
            t2, t2B = sb("t2", [128, 512], F32, ph)
            wbhv = w_bh.rearrange("(kc p) n -> p kc n", p=128)
            wbrv = w_br.rearrange("(kc p) n -> p kc n", p=128)

            def ldw(g):
                wt, wB = wbb[g % 2]
                S.dma("pool", wt[:, 0], wbhv[:, :, g * 256:(g + 1) * 256], wB, writes=[wB])
                S.dma("pool", wt[:, 1], wbrv[:, :, g * 256:(g + 1) * 256], wB, writes=[wB])
            ldw(0)
            for g in range(16):
                wt, wB = wbb[g % 2]
                if g + 1 < 16:
                    ldw(g + 1)
                for n in range(2):
                    nn = g * 2 + n
                    S.dma("sp", gsb[0][0][:], GT[nn], gsb[0][1], reads=[scrB[id(GT)]], writes=[gsb[0][1]])
                    S.dma("sp", gsb[1][0][:], GT[32 + nn], gsb[1][1], reads=[scrB[id(GT)]], writes=[gsb[1][1]])
                    for half in range(2):
                        ps = []
                        for br in range(2):
                            pt, pB = nb()
                            for kc in range(16):
                                mm(pt[:], wt[:, br, kc, n * 128:(n + 1) * 128], yT[:, br * 16 + kc, half * 512:(half + 1) * 512],
                                   pB, [wB, yTB], kc == 0, kc == 15)
                            ps.append((pt, pB))
                        S.op("dve", lambda e, pt=ps[0][0], half=half: e.tensor_tensor(
                            out=t1[:], in0=pt[:], in1=gsb[0][0][:, half * 512:(half + 1) * 512], op=ALU.mult),
                            reads=[ps[0][1], gsb[0][1]], writes=[t1B])
                        S.op("dve", lambda e, pt=ps[1][0], half=half: e.tensor_tensor(
                            out=t2[:], in0=pt[:], in1=gsb[1][0][:, half * 512:(half + 1) * 512], op=ALU.mult),
                            reads=[ps[1][1], gsb[1][1]], writes=[t2B])
                        S.op("pool", lambda e, nn=nn, half=half: e.tensor_tensor(
                            out=mst[:, half * 512:(half + 1) * 512], in0=t1[:], in1=t2[:], op=ALU.add),
                            reads=[t1B, t2B], writes=[mstB])
                    S.dma("sp", MT[nn], mst[:], mstB, reads=[mstB], writes=[scrB[id(MT)]])
        S.barrier()
        yT_stack.close()

        with ExitStack() as ph:
            mT, mTB = sb("mT", [128, 32, 1024], BF16, ph)
            S.dma("sp", mT[:], MT.rearrange("n p t -> p n t"), mTB, reads=[scrB[id(MT)]], writes=[mTB])
            g1bc, g1bcB = sb("g1bc", [128, D], F32, ph)
            build_bc(g1bc, g1bcB, lambda kc: modT[:, 64 + kc, 0:1], modTB)
            wob = [sb("wob%d" % i, [128, 32, 512], BF16, ph) for i in range(2)]
            xp, xpB = sb("xp", [128, 512], F32, ph)
            xq, xqB = sb("xq", [128, 512], F32, ph)
            wov = w_o.rearrange("(kc p) n -> p kc n", p=128)
            S.dma("pool", wob[0][0][:], wov[:, :, 0:512], wob[0][1], writes=[wob[0][1]])
            for cg in range(8):
                wt, wB = wob[cg % 2]
                if cg + 1 < 8:
                    S.dma("pool", wob[(cg + 1) % 2][0][:], wov[:, :, (cg + 1) * 512:(cg + 2) * 512], wob[(cg + 1) % 2][1],
                          writes=[wob[(cg + 1) % 2][1]])
                for t in range(8):
                    pt, pB = nb()
                    for kc in range(32):
                        mm(pt[:], mT[:, kc, t * 128:(t + 1) * 128], wt[:, kc, :], pB, [wB, mTB], kc == 0, kc == 31)
                    S.dma("sp", xp[:], xo[t * 128:(t + 1) * 128, cg * 512:(cg + 1) * 512], xpB, writes=[xpB])
                    S.op("dve", lambda e, pt=pt, cg=cg: e.tensor_tensor(out=xq[:], in0=pt[:], in1=g1bc[:, cg * 512:(cg + 1) * 512], op=ALU.mult),
                         reads=[pB, g1bcB], writes=[xqB])
                    S.op("pool", lambda e: e.tensor_tensor(out=xp[:], in0=xq[:], in1=xp[:], op=ALU.add),
                         reads=[xqB, xpB], writes=[xpB])
                    S.dma("sp", X1[t * 128:(t + 1) * 128, cg * 512:(cg + 1) * 512], xp[:], xpB, reads=[xpB], writes=[scrB[id(X1)]])
        S.barrier()

        with ExitStack() as ph:
            acc, accB = sb("acc", [128, 4, D], F32, ph)
            h2T, h2TB = sb("h2T", [128, 32, 512], BF16, ph)
            st2, st2B = sb("st2", [128, 2], F32, ph)
            w1v = w1.rearrange("(kc p) n -> p kc n", p=128)
            w2v = w2.rearrange("(hc p) n -> p hc n", p=128)
            for half in range(2):
                with ExitStack() as ph1:
                    norm_T(ph1, X1[half * 512:(half + 1) * 512], 4, h2T, h2TB, lambda kc: G2[:, kc:kc + 1], G2B,
                           lambda kc: modT[:, 96 + kc, 0:1], modTB, "m%d" % half)
                S.barrier()
                NHB = 64
                with ExitStack() as phw:
                    w1b = [sb("w1b%d_%d" % (i, half), [128, 32, 256], BF16, phw) for i in range(2)]
                    w2b = [sb("w2b%d_%d" % (i, half), [128, 2, D], BF16, phw) for i in range(2)]
                    aT, aTB = sb("aT%d" % half, [128, 2, 512], BF16, phw)
                    rl, rlB = sb("rl%d" % half, [128, 512], F32, phw)

                    def ldw12(hb):
                        S.dma("pool", w1b[hb % 2][0][:], w1v[:, :, hb * 256:(hb + 1) * 256], w1b[hb % 2][1], writes=[w1b[hb % 2][1]])
                        S.dma("pool", w2b[hb % 2][0][:], w2v[:, hb * 2:(hb + 1) * 2, :], w2b[hb % 2][1], writes=[w2b[hb % 2][1]])
                    ldw12(0)
                    for hb in range(NHB):
                        if hb + 1 < NHB:
                            ldw12(hb + 1)
                        w1t, w1B = w1b[hb % 2]
                        w2t, w2B = w2b[hb % 2]
                        for hc in range(2):
                            pt, pB = nb()
                            for kc in range(32):
                                mm(pt[:], w1t[:, kc, hc * 128:(hc + 1) * 128], h2T[:, kc, :], pB, [w1B, h2TB], kc == 0, kc == 31)
                            S.op("act", lambda e, pt=pt: e.activation(out=rl[:], in_=pt[:], func=AF.Relu), reads=[pB], writes=[rlB])
                            S.op("pool", lambda e, hc=hc: e.tensor_tensor(out=aT[:, hc, :], in0=rl[:], in1=rl[:], op=ALU.mult),
                                 reads=[rlB], writes=[aTB])
                        for tt in range(4):
                            for cg in range(8):
                                pt, pB = nb()
                                for hc in range(2):
                                    mm(pt[:], aT[:, hc, tt * 128:(tt + 1) * 128], w2t[:, hc, cg * 512:(cg + 1) * 512], pB,
                                       [aTB, w2B], hc == 0, hc == 1)
                                if hb == 0:
                                    S.op("act", lambda e, pt=pt, tt=tt, cg=cg: e.activation(
                                        out=acc[:, tt, cg * 512:(cg + 1) * 512], in_=pt[:], func=AF.Copy), reads=[pB], writes=[accB])
                                else:
                                    S.op("dve", lambda e, pt=pt, tt=tt, cg=cg: e.tensor_tensor(
                                        out=acc[:, tt, cg * 512:(cg + 1) * 512], in0=pt[:], in1=acc[:, tt, cg * 512:(cg + 1) * 512], op=ALU.add),
                                        reads=[pB, accB], writes=[accB])
                S.barrier()
                with ExitStack() as ph2:
                    g2bc, g2bcB = sb("g2bc%d" % half, [128, D], F32, ph2)
                    build_bc(g2bc, g2bcB, lambda kc: modT[:, 160 + kc, 0:1], modTB)
                    fgt, fgtB = sb("fgt%d" % half, [128, D], F32, ph2)
                    S.dma("sp", fgt[:], fg_bc, fgtB, writes=[fgtB])
                    x1t, x1tB = sb("x1t%d" % half, [128, D], F32, ph2)
                    jk2, jk2B = sb("jk2%d" % half, [128, D], BF16, ph2)
                    for tt in range(4):
                        r0 = half * 512 + tt * 128
                        S.dma("sp", x1t[:], X1[r0:r0 + 128, :], x1tB, reads=[scrB[id(X1)]], writes=[x1tB])
                        S.op("dve", lambda e, tt=tt: e.tensor_tensor(out=acc[:, tt, :], in0=acc[:, tt, :], in1=g2bc[:], op=ALU.mult),
                             reads=[accB, g2bcB], writes=[accB])
                        S.op("pool", lambda e, tt=tt: e.tensor_tensor(out=acc[:, tt, :], in0=acc[:, tt, :], in1=x1t[:], op=ALU.add),
                             reads=[accB, x1tB], writes=[accB])
                        S.op("act", lambda e, tt=tt: e.activation(out=jk2[:], in_=acc[:, tt, :], func=AF.Square, accum_out=st2[:, 0:1]),
                             reads=[accB], writes=[jk2B, st2B])
                        S.op("dve", lambda e: e.tensor_scalar(out=st2[:, 1:2], in0=st2[:, 0:1], scalar1=1.0 / D, scalar2=EPS,
                                                              op0=ALU.mult, op1=ALU.add), reads=[st2B], writes=[st2B])
                        S.op("act", lambda e: e.activation(out=st2[:, 1:2], in_=st2[:, 1:2], func=AF.Ln), reads=[st2B], writes=[st2B])
                        S.op("act", lambda e: e.activation(out=st2[:, 1:2], in_=st2[:, 1:2], func=AF.Exp, scale=-0.5), reads=[st2B], writes=[st2B])
                        S.op("dve", lambda e, tt=tt: e.scalar_tensor_tensor(
                            out=acc[:, tt, :], in0=acc[:, tt, :], scalar=st2[:, 1:2], in1=fgt[:], op0=ALU.mult, op1=ALU.mult),
                            reads=[accB, st2B, fgtB], writes=[accB])
                        S.dma("sp", out[r0:r0 + 128, :], acc[:, tt, :], accB, reads=[accB], writes=[])
                S.barrier()
        S.barrier()
        S.emit()
    return nc


_NC_CACHE = {}


def _consts():
    p = np.arange(128)
    same = (p[:, None] // 64) == (p[None, :] // 64)
    le = (same & (p[:, None] <= p[None, :])).astype(np.float32)
    gt = (same & (p[:, None] > p[None, :])).astype(np.float32)
    ge = (same & (p[:, None] >= p[None, :])).astype(np.float32)
    lt = (same & (p[:, None] < p[None, :])).astype(np.float32)
    return dict(c_ident=np.eye(128, dtype=np.float32), c_le=le, c_gt=gt, c_ge=ge, c_lt=lt,
                c_ones=np.ones((128, 128), np.float32))


def _rope_tables(pos, scale):
    n_freq = 32
    inv_freq = (np.float32(10000.0) ** (-np.arange(n_freq, dtype=np.float32) / np.float32(n_freq))).astype(np.float32)
    row = (pos // 64).astype(np.float32)
    col = (pos % 64).astype(np.float32)
    ang = np.concatenate([row[:, None] * inv_freq, col[:, None] * inv_freq], axis=-1).astype(np.float32)
    return (np.cos(ang) * np.float32(scale)).astype(np.float32), (np.sin(ang) * np.float32(scale)).astype(np.float32)


def kernel(x, c, ctx, c_ctx, w_mod, b_mod, norm1_g, norm2_g, w_in, hg_lb_logits, hg_norm_g,
           ret_decay_logit, w_branch_hgrn, w_branch_ret, w_out, w_ff1, w_ff2, final_norm_g):
    f32 = lambda a: np.ascontiguousarray(np.asarray(a, dtype=np.float32))
    x, c, ctx, c_ctx = f32(x), f32(c), f32(ctx), f32(c_ctx)
    if "nc" not in _NC_CACHE:
        _NC_CACHE["nc"] = build_program()
    nc = _NC_CACHE["nc"]
    colT = lambda v: np.ascontiguousarray(f32(v).reshape(-1, 128).T)
    shared = dict(
        w_mod=f32(w_mod)[0], b_modT=colT(f32(b_mod)[0]), n1gT=colT(f32(norm1_g)[0]), n2gT=colT(f32(norm2_g)[0]),
        fg_bc=np.ascontiguousarray(np.broadcast_to(f32(final_norm_g)[None, :], (128, D))),
        w_in=f32(w_in)[0],
        lbl=np.ascontiguousarray(np.broadcast_to(f32(hg_lb_logits)[None], (128, 2, 2, 2048))),
        hgain=np.ascontiguousarray(np.broadcast_to(f32(hg_norm_g)[0][None, :], (128, 128))),
        rdl=np.ascontiguousarray(np.broadcast_to(f32(ret_decay_logit)[0][None], (128, 2, 16))),
        w_bh=f32(w_branch_hgrn)[0], w_br=f32(w_branch_ret)[0], w_o=f32(w_out)[0], w1=f32(w_ff1)[0], w2=f32(w_ff2)[0],
    )
    shared.update(_consts())
    sk = 128.0 ** -0.5
    in_maps = []
    for core in range(8):
        b, s = core // 4, core % 4
        others = [j for j in range(4) if j != s]
        xf = np.concatenate([ctx[b]] + [x[b, j * 1024:(j + 1) * 1024] for j in others], axis=0)
        ccm = np.stack([c[b], c_ctx], axis=-1).reshape(32, 128, 2).transpose(1, 0, 2)
        cosK = np.empty((NT_ALL * 128, 64), np.float32)
        sinK = np.empty((NT_ALL * 128, 64), np.float32)
        cosK[:256] = sk
        sinK[:256] = 0.0
        for fi, j in enumerate(others):
            cs, sn = _rope_tables(np.arange(j * 1024, (j + 1) * 1024), sk)
            cosK[256 + fi * 1024:256 + (fi + 1) * 1024] = cs
            sinK[256 + fi * 1024:256 + (fi + 1) * 1024] = sn
        cs, sn = _rope_tables(np.arange(s * 1024, (s + 1) * 1024), sk)
        cosK[3328:] = cs
        sinK[3328:] = sn
        rK = np.stack([cosK.reshape(NT_ALL, 128, 64), sinK.reshape(NT_ALL, 128, 64)], axis=2).transpose(1, 0, 2, 3)
        cq, sq = _rope_tables(np.arange(s * 1024, (s + 1) * 1024), 1.0)
        rQ = np.stack([cq.reshape(8, 128, 64), sq.reshape(8, 128, 64)], axis=2).transpose(1, 0, 2, 3)
        mkv = np.ones((NT_ALL, 2), np.float32)
        for fi, j in enumerate(others):
            mkv[2 + fi * 8:2 + (fi + 1) * 8, 0] = 1.0 if j < s else 0.0
            mkv[2 + fi * 8:2 + (fi + 1) * 8, 1] = 1.0 if j > s else 0.0
        mk = np.broadcast_to(mkv[None], (128, NT_ALL, 2))
        sel = np.zeros((128, 2), np.float32)
        sel[:64, 0] = 1.0
        sel[64:, 1] = 1.0
        selm = sel[:, None, None, :] * mkv[None, :, :, None]
        wzs = np.stack([shared["w_in"][:, 2048:4096] if j < s else shared["w_in"][:, 4096:6144] for j in others], axis=0)
        m = dict(shared)
        m["wz"] = np.ascontiguousarray(wzs)
        m.update(xo=np.ascontiguousarray(x[b, s * 1024:(s + 1) * 1024]), xf=np.ascontiguousarray(xf),
                 cc=np.ascontiguousarray(ccm), ropeK=np.ascontiguousarray(rK), ropeQ=np.ascontiguousarray(rQ),
                 mk=np.ascontiguousarray(mk), selm=np.ascontiguousarray(selm.astype(np.float32)))
        in_maps.append(m)
    res = run_bass_kernel_spmd(nc, in_maps, core_ids=list(range(8)))
    outp = np.empty((2, 4096, D), np.float32)
    for core in range(8):
        b, s = core // 4, core % 4
        outp[b, s * 1024:(s + 1) * 1024] = res.results[core]["out"]
    return outp
```

```python
import numpy as np
from contextlib import ExitStack
import ml_dtypes
import concourse.bass as bass
import concourse.mybir as mybir
from concourse.bass_utils import run_bass_kernel_spmd

F32 = mybir.dt.float32
BF16 = mybir.dt.bfloat16
AF = mybir.ActivationFunctionType
ALU = mybir.AluOpType
AX = mybir.AxisListType

D = 4096
NT_ALL = 34
OWN0 = 26
EPS = 1e-6
DEBUG = False


class Buf:
    __slots__ = ("name", "w", "r", "dsem", "dcnt")

    def __init__(self, name):
        self.name = name
        self.w = None
        self.r = []
        self.dsem = None
        self.dcnt = 0


class Sched:
    ENG = ("pe", "act", "dve", "pool", "sp")

    def __init__(self, nc, es):
        self.nc = nc
        self.es = es
        self.streams = {e: [] for e in self.ENG}
        self.sems = {}
        self.cnt = {e: 0 for e in self.ENG}
        self.known = {e: {} for e in self.ENG}
        for e in self.ENG:
            self.sems[e] = es.enter_context(nc.semaphore("s_" + e))
        self.nd = 0
        self.dbufs = []

    def _dsem(self, buf):
        if buf.dsem is None:
            key = "d%d" % self.nd
            self.nd += 1
            self.sems[key] = self.es.enter_context(self.nc.semaphore(key))
            buf.dsem = key
            self.dbufs.append(buf)
        return buf.dsem

    def _need(self, eng, need):
        kn = self.known[eng]
        for k, v in need.items():
            if k == eng and v > self.cnt[eng]:
                continue
            if kn.get(k, 0) < v:
                kn[k] = v
                sem = self.sems[k]
                self.streams[eng].append(lambda e, sem=sem, v=v: e.wait_ge(sem, v))

    def _waits(self, eng, reads, writes):
        need = {}
        for b in reads:
            if b.w is not None:
                k, v = b.w
                need[k] = max(need.get(k, 0), v)
        for b in writes:
            if b.w is not None:
                k, v = b.w
                need[k] = max(need.get(k, 0), v)
            for (k, v) in b.r:
                need[k] = max(need.get(k, 0), v)
        self._need(eng, need)

    def _commit(self, ev, reads, writes):
        for b in reads:
            if len(b.r) > 64:
                m = {}
                for (k, v) in b.r:
                    m[k] = max(m.get(k, 0), v)
                b.r = list(m.items())
            b.r.append(ev)
        for b in writes:
            b.w = ev
            b.r = []

    def op(self, eng, fn, reads=(), writes=()):
        self._waits(eng, reads, writes)
        self.cnt[eng] += 1
        ev = (eng, self.cnt[eng])
        sem = self.sems[eng]
        self.streams[eng].append(lambda e, fn=fn, sem=sem: fn(e).then_inc(sem, 1))
        self._commit(ev, reads, writes)

    def op_noinc(self, eng, fn, reads=(), writes=()):
        self._waits(eng, reads, writes)
        ev = (eng, self.cnt[eng] + 1)
        self.streams[eng].append(lambda e, fn=fn: fn(e))
        self._commit(ev, reads, writes)

    def dma(self, q, out_ap, in_ap, sb, reads=(), writes=()):
        self._waits(q, reads, writes)
        key = self._dsem(sb)
        sb.dcnt += 16
        ev = (key, sb.dcnt)
        sem = self.sems[key]
        self.streams[q].append(
            lambda e, o=out_ap, i=in_ap, sem=sem: e.dma_start(out=o, in_=i).then_inc(sem, 16))
        self._commit(ev, reads, writes)

    def barrier(self):
        need = {e: self.cnt[e] for e in self.ENG if self.cnt[e] > 0}
        for b in self.dbufs:
            if b.dcnt > 0:
                need[b.dsem] = b.dcnt
        for e in self.ENG:
            self._need(e, dict(need))

    def emit(self):
        nc = self.nc
        st = self.streams
        with nc.Block() as block:
            @block.tensor
            def _(e):
                for f in st["pe"]:
                    f(e)

            @block.scalar
            def _(e):
                for f in st["act"]:
                    f(e)

            @block.vector
            def _(e):
                for f in st["dve"]:
                    f(e)

            @block.gpsimd
            def _(e):
                for f in st["pool"]:
                    f(e)

            @block.sync
            def _(e):
                for f in st["sp"]:
                    f(e)


def build_program():
    nc = bass.Bass("TRN2", target_bir_lowering=False)

    def din(name, shape, dt=F32):
        return nc.dram_tensor(name, list(shape), dt, kind="ExternalInput").ap()

    xo = din("xo", [1024, D])
    xf = din("xf", [3328, D])
    cc = din("cc", [128, 32, 2])
    w_mod = din("w_mod", [D, 6 * D])
    b_modT = din("b_modT", [128, 192])
    n1gT = din("n1gT", [128, 32])
    n2gT = din("n2gT", [128, 32])
    fg_bc = din("fg_bc", [128, D])
    w_in = din("w_in", [D, 26624])
    wz = din("wz", [3, D, 2048])
    lbl = din("lbl", [128, 2, 2, 2048])
    hgain = din("hgain", [128, 128])
    rdl = din("rdl", [128, 2, 16])
    w_bh = din("w_bh", [2048, D])
    w_br = din("w_br", [2048, D])
    w_o = din("w_o", [D, D])
    w1 = din("w1", [D, 4 * D])
    w2 = din("w2", [4 * D, D])
    c_ident = din("c_ident", [128, 128])
    c_le = din("c_le", [128, 128])
    c_gt = din("c_gt", [128, 128])
    c_ge = din("c_ge", [128, 128])
    c_lt = din("c_lt", [128, 128])
    c_ones = din("c_ones", [128, 128])
    ropeK = din("ropeK", [128, NT_ALL, 2, 64])
    ropeQ = din("ropeQ", [128, 8, 2, 64])
    mk_d = din("mk", [128, NT_ALL, 2])
    selm_d = din("selm", [128, NT_ALL, 2, 2])
    out = nc.dram_tensor("out", [1024, D], F32, kind="ExternalOutput").ap()

    def dscr(name, shape, dt):
        if DEBUG and name in ("X1", "MT", "YT", "MODT"):
            return nc.dram_tensor(name, list(shape), dt, kind="ExternalOutput").ap()
        return nc.dram_tensor(name, list(shape), dt).ap()

    ZF = dscr("ZF", [16, 128, NT_ALL, 128], F32)
    ZB = dscr("ZB", [16, 128, NT_ALL, 128], F32)
    VH = dscr("VH", [16, 128, NT_ALL, 128], BF16)
    RK = dscr("RK", [16, 128, NT_ALL, 128], BF16)
    RV = dscr("RV", [16, 128, NT_ALL, 128], BF16)
    HQ = dscr("HQ", [16, 128, 8, 128], BF16)
    HG = dscr("HG", [16, 128, 8, 128], BF16)
    RQ = dscr("RQ", [16, 128, 8, 128], BF16)
    RG = dscr("RG", [16, 128, 8, 128], BF16)
    GT = dscr("GT", [64, 128, 1024], BF16)
    X1 = dscr("X1", [1024, D], F32)
    MT = dscr("MT", [32, 128, 1024], BF16)
    if DEBUG:
        YT = dscr("YT", [128, 32, 1024], BF16)
        MODT = dscr("MODT", [128, 192, 2], F32)
    scrB = {id(t): Buf("scr%d" % i) for i, t in enumerate([ZF, ZB, VH, RK, RV, HQ, HG, RQ, RG, GT, X1, MT])}

    with ExitStack() as es:
        S = Sched(nc, es)

        sbn = [0]

        def sb(name, shape, dt, stack=None):
            sbn[0] += 1
            t = (stack or es).enter_context(nc.sbuf_tensor("sb%d_%s" % (sbn[0], name), list(shape), dt))
            return t, Buf(name)

        banks = []
        for i in range(8):
            t = es.enter_context(nc.psum_tensor("pb%d" % i, [128, 512], F32))
            banks.append((t, Buf("pb%d" % i)))
        bank_i = [0]

        def nb():
            b = banks[bank_i[0] % 8]
            bank_i[0] += 1
            return b

        def mm(outap, lhsT, rhs, pbuf, reads, first, last):
            fn = lambda e: e.matmul(outap, lhsT=lhsT, rhs=rhs, start=first, stop=last)
            if last:
                S.op("pe", fn, reads=reads, writes=[pbuf])
            else:
                S.op_noinc("pe", fn, reads=reads, writes=[pbuf] if first else [])

        def tr(outap, inap, ident, pbuf, reads):
            S.op("pe", lambda e: e.transpose(outap, inap, ident), reads=reads, writes=[pbuf])

        ident, identB = sb("ident", [128, 128], F32)
        m_le, m_leB = sb("m_le", [128, 128], F32)
        m_gt, m_gtB = sb("m_gt", [128, 128], F32)
        m_ge, m_geB = sb("m_ge", [128, 128], F32)
        m_lt, m_ltB = sb("m_lt", [128, 128], F32)
        ones, onesB = sb("ones", [128, 128], F32)
        for t, b, src in ((ident, identB, c_ident), (m_le, m_leB, c_le), (m_gt, m_gtB, c_gt),
                          (m_ge, m_geB, c_ge), (m_lt, m_ltB, c_lt), (ones, onesB, c_ones)):
            S.dma("sp", t[:], src, b, writes=[b])
        modT, modTB = sb("modT", [128, 192, 2], F32)
        G1, G1B = sb("G1", [128, 32, 2], F32)
        G2, G2B = sb("G2", [128, 32], F32)
        n1g, n1gB = sb("n1g", [128, 32], F32)
        n2g, n2gB = sb("n2g", [128, 32], F32)
        bmod, bmodB = sb("bmod", [128, 192], F32)
        S.dma("sp", n1g[:], n1gT, n1gB, writes=[n1gB])
        S.dma("sp", n2g[:], n2gT, n2gB, writes=[n2gB])
        S.dma("sp", bmod[:], b_modT, bmodB, writes=[bmodB])
        mk, mkB = sb("mk", [128, NT_ALL, 2], F32)
        selm, selmB = sb("selm", [128, NT_ALL, 2, 2], F32)
        S.dma("sp", mk[:], mk_d, mkB, writes=[mkB])
        S.dma("sp", selm[:], selm_d, selmB, writes=[selmB])
        lg, lgB = sb("lg", [128, 2, 16], F32)
        S.dma("sp", lg[:], rdl, lgB, writes=[lgB])
        S.op("act", lambda e: e.activation(out=lg[:], in_=lg[:], func=AF.Sigmoid), reads=[lgB], writes=[lgB])
        S.op("act", lambda e: e.activation(out=lg[:], in_=lg[:], func=AF.Ln), reads=[lgB], writes=[lgB])

        with ExitStack() as ph:
            ccs, ccsB = sb("ccs", [128, 32, 2], F32, ph)
            ccb, ccbB = sb("ccb", [128, 32, 2], BF16, ph)
            sg0, sg0B = sb("sg0", [128, 32, 2], F32, ph)
            S.dma("sp", ccs[:], cc, ccsB, writes=[ccsB])
            S.op("act", lambda e: e.activation(out=sg0[:], in_=ccs[:], func=AF.Sigmoid), reads=[ccsB], writes=[sg0B])
            S.op("dve", lambda e: e.tensor_tensor(out=ccb[:], in0=ccs[:], in1=sg0[:], op=ALU.mult),
                 reads=[ccsB, sg0B], writes=[ccbB])
            wm = [sb("wm%d" % i, [128, 32, 512], BF16, ph) for i in range(2)]
            wv = w_mod.rearrange("(kc p) n -> p kc n", p=128)
            pm, pmB = nb()
            pmv = pm[:, 0:384].rearrange("p (n j) -> p n j", j=2)
            NB = 48
            S.dma("pool", wm[0][0][:], wv[:, :, 0:512], wm[0][1], writes=[wm[0][1]])
            for g in range(NB):
                wt, wB = wm[g % 2]
                if g + 1 < NB:
                    S.dma("pool", wm[(g + 1) % 2][0][:], wv[:, :, (g + 1) * 512:(g + 2) * 512],
                          wm[(g + 1) % 2][1], writes=[wm[(g + 1) % 2][1]])
                for n in range(4):
                    nn = g * 4 + n
                    for kc in range(32):
                        mm(pmv[:, nn, :], wt[:, kc, n * 128:(n + 1) * 128], ccb[:, kc, :], pmB,
                           [wB, ccbB], kc == 0, kc == 31)
            for j in range(2):
                S.op("dve", lambda e, j=j: e.tensor_tensor(out=modT[:, :, j], in0=pmv[:, :, j], in1=bmod[:], op=ALU.add),
                     reads=[pmB, bmodB], writes=[modTB])
            for j in range(2):
                S.op("dve", lambda e, j=j: e.scalar_tensor_tensor(
                    out=G1[:, :, j], in0=modT[:, 32:64, j], scalar=1.0, in1=n1g[:], op0=ALU.add, op1=ALU.mult),
                    reads=[modTB, n1gB], writes=[G1B])
            S.op("dve", lambda e: e.scalar_tensor_tensor(
                out=G2[:], in0=modT[:, 128:160, 0], scalar=1.0, in1=n2g[:], op0=ALU.add, op1=ALU.mult),
                reads=[modTB, n2gB], writes=[G2B])
        S.barrier()
        if DEBUG:
            S.dma("sp", MODT, modT[:], modTB, reads=[modTB], writes=[])

        def build_bc(dst, dstB, colap_fn, colB):
            with ExitStack() as ph2:
                dg, dgB = sb("dg_" + dstB.name, [128, 128], F32, ph2)
                for kc in range(32):
                    S.op("dve", lambda e, kc=kc: e.tensor_scalar(
                        out=dg[:], in0=ident[:], scalar1=colap_fn(kc), scalar2=None, op0=ALU.mult),
                        reads=[identB, colB], writes=[dgB])
                    pt, pB = nb()
                    mm(pt[:, 0:128], ones[:], dg[:], pB, [onesB, dgB], True, True)
                    S.op("act", lambda e, kc=kc, pt=pt: e.activation(
                        out=dst[:, kc * 128:(kc + 1) * 128], in_=pt[:, 0:128], func=AF.Copy),
                        reads=[pB], writes=[dstB])
            S.barrier()

        def norm_T(ph, xsrc, nt, hT, hTB, Gcol, GB, SHcol, SHB, tag):
            xts = [sb("xt%d_%s" % (i, tag), [128, D], F32, ph) for i in range(2)]
            jk, jkB = sb("jk_" + tag, [128, D], BF16, ph)
            sts = [sb("st%d_%s" % (i, tag), [128, 2], F32, ph) for i in range(2)]
            for t in range(nt):
                xt, xtB = xts[t % 2]
                st, stB = sts[t % 2]
                S.dma("sp", xt[:], xsrc[t * 128:(t + 1) * 128, :], xtB, writes=[xtB])
                S.op("act", lambda e, xt=xt, st=st: e.activation(out=jk[:], in_=xt[:], func=AF.Square, accum_out=st[:, 0:1]),
                     reads=[xtB], writes=[jkB, stB])
                S.op("dve", lambda e, st=st: e.tensor_scalar(out=st[:, 1:2], in0=st[:, 0:1], scalar1=1.0 / D, scalar2=EPS,
                                                      op0=ALU.mult, op1=ALU.add), reads=[stB], writes=[stB])
                S.op("act", lambda e, st=st: e.activation(out=st[:, 1:2], in_=st[:, 1:2], func=AF.Ln), reads=[stB], writes=[stB])
                S.op("act", lambda e, st=st: e.activation(out=st[:, 1:2], in_=st[:, 1:2], func=AF.Exp, scale=-0.5), reads=[stB], writes=[stB])
                S.op("dve", lambda e, xt=xt, st=st: e.tensor_scalar(out=xt[:], in0=xt[:], scalar1=st[:, 1:2], scalar2=None,
                                                      op0=ALU.mult), reads=[stB, xtB], writes=[xtB])
                for k4 in range(8):
                    pt, pB = nb()
                    for q in range(4):
                        kc = k4 * 4 + q
                        tr(pt[:, q * 128:(q + 1) * 128], xt[:, kc * 128:(kc + 1) * 128], ident[:], pB, [xtB, identB])
                    for q in range(4):
                        kc = k4 * 4 + q
                        S.op("act", lambda e, kc=kc, q=q, pt=pt, t=t: e.activation(
                            out=hT[:, kc, t * 128:(t + 1) * 128], in_=pt[:, q * 128:(q + 1) * 128],
                            func=AF.Identity, scale=Gcol(kc), bias=SHcol(kc)),
                            reads=[pB, GB, SHB], writes=[hTB])

        FAMS = [("vh", 0, "copy", VH, BF16), ("zf", 2048, "z", ZF, F32), ("zb", 4096, "z", ZB, F32),
                ("rk", 6144, "ropeK", RK, BF16), ("rv", 8192, "copy", RV, BF16),
                ("hq", 10240, "siluc", HQ, BF16), ("hg", 12288, "silu", HG, BF16),
                ("rq", 14336, "ropeQ", RQ, BF16), ("rg", 16384, "silu", RG, BF16)]
        winv = w_in.rearrange("(kc p) n -> p kc n", p=128)

        def seg_pass(xsrc, nt, t0, own, modj, tag, fsel=None):
            with ExitStack() as ph:
                hT, hTB = sb("hT_" + tag, [128, 32, nt * 128], BF16, ph)
                wb = [sb("wi%d_%s" % (i, tag), [128, 32, 512], BF16, ph) for i in range(2)]
                S.dma("pool", wb[0][0][:], winv[:, :, 0:512], wb[0][1], writes=[wb[0][1]])
                with ExitStack() as ph1:
                    norm_T(ph1, xsrc, nt, hT, hTB, lambda kc: G1[:, kc, modj:modj + 1], G1B,
                           lambda kc: modT[:, kc, modj:modj + 1], modTB, tag)
                S.barrier()
                stg32, stg32B = sb("s32_" + tag, [128, 4, nt, 128], F32, ph)
                stg16, stg16B = sb("s16_" + tag, [128, 4, nt, 128], BF16, ph)
                tmp, tmpB = sb("tmp_" + tag, [128, 512], F32, ph)
                tmp2, tmp2B = sb("tmp2_" + tag, [128, 512], F32, ph)
                rK, rKB = sb("rK_" + tag, [128, nt, 2, 64], F32, ph)
                S.dma("sp", rK[:], ropeK[:, t0:t0 + nt], rKB, writes=[rKB])
                if own:
                    rQ, rQB = sb("rQ_" + tag, [128, 8, 2, 64], F32, ph)
                    S.dma("sp", rQ[:], ropeQ, rQB, writes=[rQB])
                    gst, gstB = sb("gst_" + tag, [128, 1024], BF16, ph)
                fams = FAMS if own else FAMS[:5]
                groups = []
                for (fn_, off, kind, scr, dt) in fams:
                    if fsel is not None and fn_ == "zb":
                        continue
                    for g4 in range(4):
                        if fsel is not None and fn_ == "zf":
                            groups.append((g4 * 512, kind, scr, dt, g4, wz[fsel].rearrange("(kc p) n -> p kc n", p=128)))
                        else:
                            groups.append((off + g4 * 512, kind, scr, dt, g4, winv))
                if own:
                    for gg in range(16):
                        groups.append((18432 + gg * 512, "gate", GT, BF16, gg, winv))
                ng = len(groups)
                assert groups[0][0] == 0
                for gi, (coff, kind, scr, dt, g4, wsrc) in enumerate(groups):
                    wt, wB = wb[gi % 2]
                    if gi + 1 < ng:
                        c2 = groups[gi + 1][0]
                        S.dma("pool", wb[(gi + 1) % 2][0][:], groups[gi + 1][5][:, :, c2:c2 + 512], wb[(gi + 1) % 2][1],
                              writes=[wb[(gi + 1) % 2][1]])
                    if kind == "gate":
                        for n in range(4):
                            for half in range(2):
                                pt, pB = nb()
                                for kc in range(32):
                                    mm(pt[:], wt[:, kc, n * 128:(n + 1) * 128], hT[:, kc, half * 512:(half + 1) * 512],
                                       pB, [wB, hTB], kc == 0, kc == 31)
                                S.op("act", lambda e, pt=pt, half=half: e.activation(
                                    out=gst[:, half * 512:(half + 1) * 512], in_=pt[:], func=AF.Sigmoid),
                                    reads=[pB], writes=[gstB])
                            S.dma("sp", GT[g4 * 4 + n], gst[:], gstB, reads=[gstB], writes=[scrB[id(GT)]])
                        continue
                    stg, stgB = (stg32, stg32B) if dt == F32 else (stg16, stg16B)
                    for t in range(nt):
                        pt, pB = nb()
                        for kc in range(32):
                            mm(pt[:], hT[:, kc, t * 128:(t + 1) * 128], wt[:, kc, :], pB, [wB, hTB], kc == 0, kc == 31)
                        pv = pt[:].rearrange("p (h c) -> p h c", h=4)
                        if kind in ("copy", "z"):
                            S.op("act", lambda e, pv=pv, t=t, stg=stg: e.activation(out=stg[:, :, t, :], in_=pv, func=AF.Copy),
                                 reads=[pB], writes=[stgB])
                        elif kind in ("silu", "siluc"):
                            S.op("act", lambda e, pt=pt: e.activation(out=tmp[:], in_=pt[:], func=AF.Sigmoid),
                                 reads=[pB], writes=[tmpB])
                            cst = (128.0 ** -0.5) if kind == "siluc" else 1.0
                            S.op("dve", lambda e, pv=pv, t=t, cst=cst, stg=stg: e.scalar_tensor_tensor(
                                out=stg[:, :, t, :], in0=pv, scalar=cst, in1=tmp[:].rearrange("p (h c) -> p h c", h=4),
                                op0=ALU.mult, op1=ALU.mult), reads=[pB, tmpB], writes=[stgB])
                        else:
                            rt, rtB = (rK, rKB) if kind == "ropeK" else (rQ, rQB)
                            S.op("act", lambda e, pt=pt: e.activation(out=tmp[:], in_=pt[:], func=AF.Copy),
                                 reads=[pB], writes=[tmpB])
                            tv = tmp[:].rearrange("p (h c two) -> p h c two", h=4, two=2)
                            t2v = tmp2[:].rearrange("p (h c two) -> p h c two", h=4, two=2)
                            sv = stg[:, :, t, :].rearrange("p h (c two) -> p h c two", two=2)
                            for h in range(4):
                                a1, a2 = tv[:, h, :, 0], tv[:, h, :, 1]
                                cs, sn = rt[:, t, 0, :], rt[:, t, 1, :]
                                b1, b2 = t2v[:, h, :, 0], t2v[:, h, :, 1]
                                S.op("dve", lambda e, a1=a1, cs=cs, b1=b1: e.tensor_tensor(out=b1, in0=a1, in1=cs, op=ALU.mult),
                                     reads=[tmpB, rtB], writes=[tmp2B])
                                S.op("dve", lambda e, a2=a2, sn=sn, b2=b2: e.tensor_tensor(out=b2, in0=a2, in1=sn, op=ALU.mult),
                                     reads=[tmpB, rtB], writes=[tmp2B])
                                S.op("dve", lambda e, b1=b1, b2=b2, o=sv[:, h, :, 0]: e.tensor_tensor(out=o, in0=b1, in1=b2, op=ALU.subtract),
                                     reads=[tmp2B], writes=[stgB])
                                S.op("dve", lambda e, a1=a1, sn=sn, b1=b1: e.tensor_tensor(out=b1, in0=a1, in1=sn, op=ALU.mult),
                                     reads=[tmpB, rtB, stgB], writes=[tmp2B])
                                S.op("dve", lambda e, a2=a2, cs=cs, b2=b2: e.tensor_tensor(out=b2, in0=a2, in1=cs, op=ALU.mult),
                                     reads=[tmpB, rtB], writes=[tmp2B])
                                S.op("dve", lambda e, b1=b1, b2=b2, o=sv[:, h, :, 1]: e.tensor_tensor(out=o, in0=b1, in1=b2, op=ALU.add),
                                     reads=[tmp2B], writes=[stgB])
                    ts0 = t0 if scr.shape[2] == NT_ALL else 0
                    S.dma("sp", scr[g4 * 4:(g4 + 1) * 4, :, ts0:ts0 + nt, :].rearrange("h p t c -> p h t c"),
                          stg[:], stgB, reads=[stgB], writes=[scrB[id(scr)]])
            S.barrier()

        seg_pass(xf[0:256], 2, 0, False, 1, "c")
        for f in range(3):
            seg_pass(xf[256 + f * 1024:256 + (f + 1) * 1024], 8, 2 + f * 8, False, 0, "f%d" % f, fsel=f)
        seg_pass(xo, 8, OWN0, True, 0, "o")

        yT_stack = ExitStack()
        yT, yTB = sb("yT", [128, 32, 1024], BF16, yT_stack)
        with ExitStack() as ph:
            NTA = NT_ALL
            zt = [sb("zt%d" % d, [128, NTA, 128], F32, ph) for d in range(2)]
            kt = [sb("kt%d" % d, [128, NTA, 128], BF16, ph) for d in range(2)]
            kd = [sb("kd%d" % d, [128, NTA, 128], BF16, ph) for d in range(2)]
            vt, vtB = sb("vt", [128, NTA, 128], BF16, ph)
            qt, qtB = sb("qt", [128, 8, 128], BF16, ph)
            gt_, gtB = sb("gt", [128, 8, 128], BF16, ph)
            gg, ggB = sb("gg", [128, 8, 128], F32, ph)
            lbt, lbtB = sb("lbt", [128, 2, 2, 128], F32, ph)
            lb, lbB = sb("lb", [128, 2, 128], F32, ph)
            oml, omlB = sb("oml", [128, 2, 128], F32, ph)
            hgn, hgnB = sb("hgn", [128, 128], F32, ph)
            S.dma("sp", hgn[:], hgain, hgnB, writes=[hgnB])
            dd = [sb("dd%d" % d, [128, NTA, 2], F32, ph) for d in range(2)]
            Sst = [sb("Sst%d" % d, [128, 128], F32, ph) for d in range(2)]
            Ssn = [sb("Ssn%d" % d, [128, 16, 128], BF16, ph) for d in range(2)]
            qe = [sb("qe%d" % d, [128, 128], F32, ph) for d in range(2)]
            ke = [sb("ke%d" % d, [128, 128], F32, ph) for d in range(2)]
            ee, eeB = sb("ee", [128, 512], F32, ph)
            qT = [sb("qT%d" % d, [128, 8, 128], BF16, ph) for d in range(2)]
            kT, kTB = sb("kT", [128, 128], BF16, ph)
            scT = [sb("scT%d" % d, [128, 8, 128], BF16, ph) for d in range(2)]
            osb, osbB = sb("osb", [128, 128], F32, ph)
            ysb, ysbB = sb("ysb", [128, 128], F32, ph)
            stt, sttB = sb("stt", [128, 8], F32, ph)
            bst, bstB = sb("bst", [128, 6], F32, ph)
            lgc, lgcB = sb("lgc", [128, 2], F32, ph)
            TRI_INC = [(m_le, m_leB), (m_ge, m_geB)]
            TRI_DEC = [(m_gt, m_gtB), (m_lt, m_ltB)]

            ee2, ee2B = sb("ee2", [128, 512], F32, ph)
            qe4, qe4B = sb("qe4", [128, 4, 128], F32, ph)
            ke4, ke4B = sb("ke4", [128, 4, 128], F32, ph)
            kT4, kT4B = sb("kT4", [128, 4, 128], BF16, ph)
            oall, oallB = sb("oall", [128, 8, 128], F32, ph)
            stall, stallB = sb("stall", [128, 3, 8], F32, ph)
            bst8, bst8B = sb("bst8", [128, 8, 6], F32, ph)
            mv8, mv8B = sb("mv8", [128, 8, 2], F32, ph)

            def bc3(ap2d, n):
                return ap2d.rearrange("p (o c) -> p o c", o=1).to_broadcast([128, n, 128])

            def loads_z(kind, h):
                if kind == "hg":
                    S.dma("sp", zt[0][0][:], ZF[h], zt[0][1], reads=[scrB[id(ZF)]], writes=[zt[0][1]])
                    S.dma("sp", zt[1][0][:, 0:2], ZB[h][:, 0:2], zt[1][1], reads=[scrB[id(ZB)]], writes=[zt[1][1]])
                    S.dma("sp", zt[1][0][:, 2:OWN0], ZF[h][:, 2:OWN0], zt[1][1], reads=[scrB[id(ZF)]], writes=[zt[1][1]])
                    S.dma("sp", zt[1][0][:, OWN0:NT_ALL], ZB[h][:, OWN0:NT_ALL], zt[1][1], reads=[scrB[id(ZB)]], writes=[zt[1][1]])
                else:
                    S.dma("sp", kt[0][0][:], RK[h], kt[0][1], reads=[scrB[id(RK)]], writes=[kt[0][1]])

            def scan_head(kind, h, nxt):
                hg = kind == "hg"
                if hg:
                    S.dma("sp", vt[:], VH[h], vtB, reads=[scrB[id(VH)]], writes=[vtB])
                    S.dma("sp", qt[:], HQ[h], qtB, reads=[scrB[id(HQ)]], writes=[qtB])
                    S.dma("sp", gt_[:], HG[h], gtB, reads=[scrB[id(HG)]], writes=[gtB])
                    S.dma("sp", lbt[:], lbl[:, :, :, h * 128:(h + 1) * 128], lbtB, writes=[lbtB])
                    S.op("dve", lambda e: e.tensor_tensor(out=lb[:], in0=lbt[:, 0], in1=lbt[:, 1], op=ALU.subtract),
                         reads=[lbtB], writes=[lbB])
                    S.op("act", lambda e: e.activation(out=lb[:], in_=lb[:], func=AF.Sigmoid), reads=[lbB], writes=[lbB])
                    S.op("dve", lambda e: e.tensor_scalar(out=oml[:], in0=lb[:], scalar1=-1.0, scalar2=1.0,
                                                          op0=ALU.mult, op1=ALU.add), reads=[lbB], writes=[omlB])
                    for d in range(2):
                        z, zB = zt[d]
                        k_, kB_ = kt[d]
                        S.op("act", lambda e, z=z: e.activation(out=z[:], in_=z[:], func=AF.Sigmoid), reads=[zB], writes=[zB])
                        S.op("dve", lambda e, z=z, d=d: e.tensor_tensor(
                            out=z[:], in0=z[:], in1=oml[:, d:d + 1, :].to_broadcast([128, NTA, 128]), op=ALU.mult),
                            reads=[zB, omlB], writes=[zB])
                        S.op("dve", lambda e, z=z, d=d: e.tensor_tensor(
                            out=z[:], in0=z[:], in1=lb[:, d:d + 1, :].to_broadcast([128, NTA, 128]), op=ALU.add),
                            reads=[zB, lbB], writes=[zB])
                        S.op("dve", lambda e, z=z, k_=k_: e.tensor_scalar(out=k_[:], in0=z[:], scalar1=-1.0, scalar2=1.0,
                                                                         op0=ALU.mult, op1=ALU.add), reads=[zB], writes=[kB_])
                        S.op("pool", lambda e, k_=k_, d=d: e.tensor_tensor(
                            out=k_[:], in0=k_[:], in1=mk[:, :, d:d + 1].to_broadcast([128, NTA, 128]), op=ALU.mult),
                            reads=[kB_, mkB], writes=[kB_])
                        S.op("act", lambda e, z=z: e.activation(out=z[:], in_=z[:], func=AF.Ln), reads=[zB, kB_], writes=[zB])
                else:
                    S.dma("sp", vt[:], RV[h], vtB, reads=[scrB[id(RV)]], writes=[vtB])
                    S.dma("sp", qt[:], RQ[h], qtB, reads=[scrB[id(RQ)]], writes=[qtB])
                    S.dma("sp", gt_[:], RG[h], gtB, reads=[scrB[id(RG)]], writes=[gtB])
                    for d in range(2):
                        z, zB = zt[d]
                        S.op("dve", lambda e, d=d: e.tensor_copy(out=lgc[:, d:d + 1], in_=lg[:, d, h:h + 1]), reads=[lgB], writes=[lgcB])
                        S.op("pool", lambda e, z=z: e.memset(z[:], 0.0), writes=[zB])
                        S.op("act", lambda e, z=z, d=d: e.activation(out=z[:], in_=z[:], func=AF.Identity, scale=1.0, bias=lgc[:, d:d + 1]),
                             reads=[zB, lgcB], writes=[zB])
                    S.op("pool", lambda e: e.tensor_tensor(
                        out=kt[1][0][:], in0=kt[0][0][:], in1=mk[:, :, 1:2].to_broadcast([128, NTA, 128]), op=ALU.mult),
                        reads=[kt[0][1], mkB], writes=[kt[1][1]])
                    S.op("pool", lambda e: e.tensor_tensor(
                        out=kt[0][0][:], in0=kt[0][0][:], in1=mk[:, :, 0:1].to_broadcast([128, NTA, 128]), op=ALU.mult),
                        reads=[kt[0][1], mkB], writes=[kt[0][1]])
                if hg:
                    S.op("dve", lambda e: e.tensor_tensor(out=gg[:], in0=gt_[:], in1=bc3(hgn[:], 8), op=ALU.mult),
                         reads=[gtB, hgnB], writes=[ggB])
                else:
                    S.op("dve", lambda e: e.tensor_copy(out=gg[:], in_=gt_[:]), reads=[gtB], writes=[ggB])
                for d in range(2):
                    z, zB = zt[d]
                    k_, kB_ = kt[d]
                    kdt, kdB = kd[d]
                    ddt, ddB = dd[d]
                    tri_d, tri_dB = TRI_DEC[d]
                    tri_i, tri_iB = TRI_INC[d]
                    pc, pcB = nb()
                    pcv = pc[:, 0:NTA * 2].rearrange("p (t c) -> p t c", c=2)
                    for t in range(NTA):
                        mm(pcv[:, t, :], z[:, t, :], selm[:, t, d, :], pcB, [zB, selmB], True, True)
                    S.op("act", lambda e, pcv=pcv, ddt=ddt: e.activation(out=ddt[:], in_=pcv, func=AF.Exp), reads=[pcB], writes=[ddB])
                    for t4 in range(0, NTA, 4):
                        n4 = min(4, NTA - t4)
                        pt, pB = nb()
                        for q in range(n4):
                            mm(pt[:, q * 128:(q + 1) * 128], tri_d[:], z[:, t4 + q, :], pB, [tri_dB, zB], True, True)
                        S.op("act", lambda e, pt=pt, n4=n4: e.activation(out=ee[:, 0:n4 * 128], in_=pt[:, 0:n4 * 128], func=AF.Exp),
                             reads=[pB], writes=[eeB])
                        S.op("dve", lambda e, t4=t4, n4=n4, k_=k_, kdt=kdt: e.tensor_tensor(
                            out=kdt[:, t4:t4 + n4, :], in0=k_[:, t4:t4 + n4, :],
                            in1=ee[:, 0:n4 * 128].rearrange("p (t c) -> p t c", c=128), op=ALU.mult),
                            reads=[kB_, eeB], writes=[kdB])
                    qTt, qTB = qT[d]
                    scTt, scTB = scT[d]
                    for t4 in range(0, 8, 4):
                        ta = OWN0 + t4
                        pt, pB = nb()
                        for q in range(4):
                            mm(pt[:, q * 128:(q + 1) * 128], tri_i[:], z[:, ta + q, :], pB, [tri_iB, zB], True, True)
                        S.op("act", lambda e, pt=pt: e.activation(out=ee[:], in_=pt[:], func=AF.Exp), reads=[pB], writes=[eeB])
                        S.op("act", lambda e, pt=pt: e.activation(out=ee2[:], in_=pt[:], func=AF.Exp, scale=-1.0), reads=[pB], writes=[ee2B])
                        S.op("dve", lambda e, t4=t4: e.tensor_tensor(
                            out=qe4[:], in0=qt[:, t4:t4 + 4, :], in1=ee[:].rearrange("p (t c) -> p t c", c=128), op=ALU.mult),
                            reads=[qtB, eeB], writes=[qe4B])
                        S.op("dve", lambda e, ta=ta, k_=k_: e.tensor_tensor(
                            out=ke4[:], in0=k_[:, ta:ta + 4, :], in1=ee2[:].rearrange("p (t c) -> p t c", c=128), op=ALU.mult),
                            reads=[kB_, ee2B], writes=[ke4B])
                        p2, p2B = nb()
                        p2k, p2kB = nb()
                        for q in range(4):
                            tr(p2[:, q * 128:(q + 1) * 128], qe4[:, q, :], ident[:], p2B, [qe4B, identB])
                        for q in range(4):
                            tr(p2k[:, q * 128:(q + 1) * 128], ke4[:, q, :], ident[:], p2kB, [ke4B, identB])
                        S.op("act", lambda e, p2=p2, t4=t4, qTt=qTt: e.activation(
                            out=qTt[:, t4:t4 + 4, :], in_=p2[:].rearrange("p (t c) -> p t c", c=128), func=AF.Copy),
                            reads=[p2B], writes=[qTB])
                        S.op("act", lambda e, p2k=p2k: e.activation(
                            out=kT4[:], in_=p2k[:].rearrange("p (t c) -> p t c", c=128), func=AF.Copy),
                            reads=[p2kB], writes=[kT4B])
                        p3, p3B = nb()
                        for q in range(4):
                            mm(p3[:, q * 128:(q + 1) * 128], kT4[:, q, :], qTt[:, t4 + q, :], p3B, [kT4B, qTB], True, True)
                        S.op("dve", lambda e, p3=p3, t4=t4, scTt=scTt, tri_i=tri_i: e.tensor_tensor(
                            out=scTt[:, t4:t4 + 4, :], in0=p3[:].rearrange("p (t c) -> p t c", c=128),
                            in1=bc3(tri_i[:], 4), op=ALU.mult), reads=[p3B, tri_iB], writes=[scTB])
                orders = []
                for d in range(2):
                    St, StB = Sst[d]
                    S.op("pool", lambda e, St=St: e.memset(St[:], 0.0), writes=[StB])
                    if d == 0:
                        orders.append([(t, c) for t in range(NTA) for c in (0, 1)])
                    else:
                        tl = [1, 0] + list(range(25, 1, -1)) + list(range(33, 25, -1))
                        orders.append([(t, c) for t in tl for c in (1, 0)])
                for i0 in range(0, NTA * 2, 4):
                    pend = []
                    for d in range(2):
                        kdt, kdB = kd[d]
                        grp = orders[d][i0:i0 + 4]
                        pt, pB = nb()
                        for q, (t, c) in enumerate(grp):
                            mm(pt[:, q * 128:(q + 1) * 128], kdt[c * 64:(c + 1) * 64, t, :], vt[c * 64:(c + 1) * 64, t, :],
                               pB, [kdB, vtB], True, True)
                        pend.append((grp, pt, pB))
                    for q in range(4):
                        for d in range(2):
                            grp, pt, pB = pend[d]
                            t, c = grp[q]
                            ddt, ddB = dd[d]
                            St, StB = Sst[d]
                            Snt, SnB = Ssn[d]
                            if t >= OWN0:
                                ci = (t - OWN0) * 2 + c
                                S.op("act", lambda e, ci=ci, Snt=Snt, St=St: e.activation(out=Snt[:, ci, :], in_=St[:], func=AF.Copy),
                                     reads=[StB], writes=[SnB])
                            S.op("dve", lambda e, pt=pt, q=q, t=t, c=c, St=St, ddt=ddt: e.scalar_tensor_tensor(
                                out=St[:], in0=St[:], scalar=ddt[:, t, c:c + 1], in1=pt[:, q * 128:(q + 1) * 128],
                                op0=ALU.mult, op1=ALU.add), reads=[StB, ddB, pB, SnB], writes=[StB])
                if nxt is not None:
                    loads_z(*nxt)
                for t in range(8):
                    ta = OWN0 + t
                    po, poB = nb()
                    ov = po[:, 0:128]
                    for d in range(2):
                        qTt, qTB = qT[d]
                        scTt, scTB = scT[d]
                        Snt, SnB = Ssn[d]
                        mm(ov, scTt[:, t, :], vt[:, ta, :], poB, [scTB, vtB], d == 0, False)
                        for c in range(2):
                            mm(po[c * 64:(c + 1) * 64, 0:128], qTt[:, t, c * 64:(c + 1) * 64], Snt[:, t * 2 + c, :], poB,
                               [qTB, SnB], False, d == 1 and c == 1)
                    S.op("act", lambda e, ov=ov, t=t: e.activation(out=oall[:, t, :], in_=ov, func=AF.Copy), reads=[poB], writes=[oallB])
                    if hg:
                        S.op("act", lambda e, t=t: e.activation(out=osb[:], in_=oall[:, t, :], func=AF.Square, accum_out=stall[:, 0, t:t + 1]),
                             reads=[oallB], writes=[osbB, stallB])
                    else:
                        S.op("dve", lambda e, t=t: e.bn_stats(out=bst8[:, t, :], in_=oall[:, t, :]), reads=[oallB], writes=[bst8B])
                        S.op("dve", lambda e, t=t: e.bn_aggr(out=mv8[:, t, :], in_=bst8[:, t, :]), reads=[bst8B], writes=[mv8B])
                if hg:
                    S.op("dve", lambda e: e.tensor_scalar(out=stall[:, 1, :], in0=stall[:, 0, :], scalar1=1.0 / 128, scalar2=EPS,
                                                          op0=ALU.mult, op1=ALU.add), reads=[stallB], writes=[stallB])
                else:
                    S.op("dve", lambda e: e.tensor_scalar(out=stall[:, 1, :], in0=mv8[:, :, 1], scalar1=EPS, scalar2=None,
                                                          op0=ALU.add), reads=[mv8B], writes=[stallB])
                S.op("act", lambda e: e.activation(out=stall[:, 1, :], in_=stall[:, 1, :], func=AF.Ln), reads=[stallB], writes=[stallB])
                S.op("act", lambda e: e.activation(out=stall[:, 1, :], in_=stall[:, 1, :], func=AF.Exp, scale=-0.5), reads=[stallB], writes=[stallB])
                for t in range(8):
                    if hg:
                        S.op("dve", lambda e, t=t: e.scalar_tensor_tensor(
                            out=ysb[:], in0=oall[:, t, :], scalar=stall[:, 1, t:t + 1], in1=gg[:, t, :], op0=ALU.mult, op1=ALU.mult),
                            reads=[oallB, stallB, ggB], writes=[ysbB])
                    else:
                        S.op("dve", lambda e, t=t: e.tensor_scalar(out=osb[:], in0=oall[:, t, :], scalar1=mv8[:, t, 0:1],
                                                                   scalar2=stall[:, 1, t:t + 1], op0=ALU.subtract, op1=ALU.mult),
                             reads=[oallB, mv8B, stallB], writes=[osbB])
                        S.op("dve", lambda e, t=t: e.tensor_tensor(out=ysb[:], in0=osb[:], in1=gg[:, t, :], op=ALU.mult),
                             reads=[osbB, ggB], writes=[ysbB])
                    py, pyB = nb()
                    tr(py[:, 0:128], ysb[:], ident[:], pyB, [ysbB, identB])
                    hh = h if hg else 16 + h
                    S.op("act", lambda e, py=py, t=t, hh=hh: e.activation(out=yT[:, hh, t * 128:(t + 1) * 128], in_=py[:, 0:128], func=AF.Copy),
                         reads=[pyB], writes=[yTB])

            heads = [("hg", h) for h in range(16)] + [("ret", h) for h in range(16)]
            loads_z(*heads[0])
            for i, (kind, h) in enumerate(heads):
                scan_head(kind, h, heads[i + 1] if i + 1 < len(heads) else None)
        S.barrier()
        if DEBUG:
            S.dma("sp", YT, yT[:], yTB, reads=[yTB], writes=[])

        with ExitStack() as ph:
            mst, mstB = sb("mst", [128, 1024], BF16, ph)
            wbb = [sb("wbb%d" % i, [128, 2, 16, 256], BF16, ph) for i in range(2)]
            gsb = [sb("gsb%d" % i, [128, 1024], BF16, ph) for i in range(2)]
            t1, t1B = sb("t1", [128, 512], F32, ph)
            t2, t2B = sb("t2", [128, 512], F32, ph)
            wbhv = w_bh.rearrange("(kc p) n -> p kc n", p=128)
            wbrv = w_br.rearrange("(kc p) n -> p kc n", p=128)

            def ldw(g):
                wt, wB = wbb[g % 2]
                S.dma("pool", wt[:, 0], wbhv[:, :, g * 256:(g + 1) * 256], wB, writes=[wB])
                S.dma("pool", wt[:, 1], wbrv[:, :, g * 256:(g + 1) * 256], wB, writes=[wB])
            ldw(0)
            for g in range(16):
                wt, wB = wbb[g % 2]
                if g + 1 < 16:
                    ldw(g + 1)
                for n in range(2):
                    nn = g * 2 + n
                    S.dma("sp", gsb[0][0][:], GT[nn], gsb[0][1], reads=[scrB[id(GT)]], writes=[gsb[0][1]])
                    S.dma("sp", gsb[1][0][:], GT[32 + nn], gsb[1][1], reads=[scrB[id(GT)]], writes=[gsb[1][1]])
                    for half in range(2):
                        ps = []
                        for br in range(2):
                            pt, pB = nb()
                            for kc in range(16):
                                mm(pt[:], wt[:, br, kc, n * 128:(n + 1) * 128], yT[:, br * 16 + kc, half * 512:(half + 1) * 512],
                                   pB, [wB, yTB], kc == 0, kc == 15)
                            ps.append((pt, pB))
                        S.op("dve", lambda e, pt=ps[0][0], half=half: e.tensor_tensor(
                            out=t1[:], in0=pt[:], in1=gsb[0][0][:, half * 512:(half + 1) * 512], op=ALU.mult),
                            reads=[ps[0][1], gsb[0][1]], writes=[t1B])
                        S.op("dve", lambda e, pt=ps[1][0], half=half: e.tensor_tensor(
                            out=t2[:], in0=pt[:], in1=gsb[1][0][:, half * 512:(half + 1) * 512], op=ALU.mult),
                            reads=[ps[1][1], gsb[1][1]], writes=[t2B])
                        S.op("pool", lambda e, nn=nn, half=half: e.tensor_tensor(
                            out=mst[:, half * 512:(half + 1) * 512], in0=t1[:], in1=t2[:], op=ALU.add),
                            reads=[t1B, t2B], writes=[mstB])
                    S.dma("sp", MT[nn], mst[:], mstB, reads=[mstB], writes=[scrB[id(MT)]])
        S.barrier()
        yT_stack.close()

        with ExitStack() as ph:
            mT, mTB = sb("mT", [128, 32, 1024], BF16, ph)
            S.dma("sp", mT[:], MT.rearrange("n p t -> p n t"), mTB, reads=[scrB[id(MT)]], writes=[mTB])
            g1bc, g1bcB = sb("g1bc", [128, D], F32, ph)
            build_bc(g1bc, g1bcB, lambda kc: modT[:, 64 + kc, 0:1], modTB)
            wob = [sb("wob%d" % i, [128, 32, 512], BF16, ph) for i in range(2)]
            xp, xpB = sb("xp", [128, 512], F32, ph)
            xq, xqB = sb("xq", [128, 512], F32, ph)
            wov = w_o.rearrange("(kc p) n -> p kc n", p=128)
            S.dma("pool", wob[0][0][:], wov[:, :, 0:512], wob[0][1], writes=[wob[0][1]])
            for cg in range(8):
                wt, wB = wob[cg % 2]
                if cg + 1 < 8:
                    S.dma("pool", wob[(cg + 1) % 2][0][:], wov[:, :, (cg + 1) * 512:(cg + 2) * 512], wob[(cg + 1) % 2][1],
                          writes=[wob[(cg + 1) % 2][1]])
                for t in range(8):
                    pt, pB = nb()
                    for kc in range(32):
                        mm(pt[:], mT[:, kc, t * 128:(t + 1) * 128], wt[:, kc, :], pB, [wB, mTB], kc == 0, kc == 31)
                    S.dma("sp", xp[:], xo[t * 128:(t + 1) * 128, cg * 512:(cg + 1) * 512], xpB, writes=[xpB])
                    S.op("dve", lambda e, pt=pt, cg=cg: e.tensor_tensor(out=xq[:], in0=pt[:], in1=g1bc[:, cg * 512:(cg + 1) * 512], op=ALU.mult),
                         reads=[pB, g1bcB], writes=[xqB])
                    S.op("pool", lambda e: e.tensor_tensor(out=xp[:], in0=xq[:], in1=xp[:], op=ALU.add),
                         reads=[xqB, xpB], writes=[xpB])
                    S.dma("sp", X1[t * 128:(t + 1) * 128, cg * 512:(cg + 1) * 512], xp[:], xpB, reads=[xpB], writes=[scrB[id(X1)]])
        S.barrier()

        with ExitStack() as ph:
            acc, accB = sb("acc", [128, 4, D], F32, ph)
            accR = [[Buf("acc_%d_%d" % (tt, cg)) for cg in range(8)] for tt in range(4)]
            accD = [Buf("accD%d" % tt) for tt in range(4)]
            h2T, h2TB = sb("h2T", [128, 32, 512], BF16, ph)
            st2, st2B = sb("st2", [128, 2], F32, ph)
            w1v = w1.rearrange("(kc p) n -> p kc n", p=128)
            w2v = w2.rearrange("(hc p) n -> p hc n", p=128)
            for half in range(2):
                with ExitStack() as ph1:
                    norm_T(ph1, X1[half * 512:(half + 1) * 512], 4, h2T, h2TB, lambda kc: G2[:, kc:kc + 1], G2B,
                           lambda kc: modT[:, 96 + kc, 0:1], modTB, "m%d" % half)
                S.barrier()
                NHB = 64
                with ExitStack() as phw:
                    w1b = [sb("w1b%d_%d" % (i, half), [128, 32, 256], BF16, phw) for i in range(2)]
                    w2b = [sb("w2b%d_%d" % (i, half), [128, 2, D], BF16, phw) for i in range(2)]
                    aT, aTB = sb("aT%d" % half, [128, 2, 512], BF16, phw)
                    rl, rlB = sb("rl%d" % half, [128, 512], F32, phw)

                    def ldw12(hb):
                        S.dma("pool", w1b[hb % 2][0][:], w1v[:, :, hb * 256:(hb + 1) * 256], w1b[hb % 2][1], writes=[w1b[hb % 2][1]])
                        S.dma("pool", w2b[hb % 2][0][:], w2v[:, hb * 2:(hb + 1) * 2, :], w2b[hb % 2][1], writes=[w2b[hb % 2][1]])
                    ldw12(0)
                    for hb in range(NHB):
                        if hb + 1 < NHB:
                            ldw12(hb + 1)
                        w1t, w1B = w1b[hb % 2]
                        w2t, w2B = w2b[hb % 2]
                        for hc in range(2):
                            pt, pB = nb()
                            for kc in range(32):
                                mm(pt[:], w1t[:, kc, hc * 128:(hc + 1) * 128], h2T[:, kc, :], pB, [w1B, h2TB], kc == 0, kc == 31)
                            S.op("act", lambda e, pt=pt: e.activation(out=rl[:], in_=pt[:], func=AF.Relu), reads=[pB], writes=[rlB])
                            S.op("pool", lambda e, hc=hc: e.tensor_tensor(out=aT[:, hc, :], in0=rl[:], in1=rl[:], op=ALU.mult),
                                 reads=[rlB], writes=[aTB])
                        for tt in range(4):
                            for cg in range(8):
                                pt, pB = nb()
                                for hc in range(2):
                                    mm(pt[:], aT[:, hc, tt * 128:(tt + 1) * 128], w2t[:, hc, cg * 512:(cg + 1) * 512], pB,
                                       [aTB, w2B], hc == 0, hc == 1)
                                if hb == 0:
                                    S.op("act", lambda e, pt=pt, tt=tt, cg=cg: e.activation(
                                        out=acc[:, tt, cg * 512:(cg + 1) * 512], in_=pt[:], func=AF.Copy), reads=[pB], writes=[accR[tt][cg]])
                                else:
                                    S.op("dve", lambda e, pt=pt, tt=tt, cg=cg: e.tensor_tensor(
                                        out=acc[:, tt, cg * 512:(cg + 1) * 512], in0=pt[:], in1=acc[:, tt, cg * 512:(cg + 1) * 512], op=ALU.add),
                                        reads=[pB, accR[tt][cg]], writes=[accR[tt][cg]])
                S.barrier()
                with ExitStack() as ph2:
                    g2bc, g2bcB = sb("g2bc%d" % half, [128, D], F32, ph2)
                    build_bc(g2bc, g2bcB, lambda kc: modT[:, 160 + kc, 0:1], modTB)
                    fgt, fgtB = sb("fgt%d" % half, [128, D], F32, ph2)
                    S.dma("sp", fgt[:], fg_bc, fgtB, writes=[fgtB])
                    x1t, x1tB = sb("x1t%d" % half, [128, D], F32, ph2)
                    jk2, jk2B = sb("jk2%d" % half, [128, D], BF16, ph2)
                    for tt in range(4):
                        r0 = half * 512 + tt * 128
                        S.dma("sp", x1t[:], X1[r0:r0 + 128, :], x1tB, reads=[scrB[id(X1)]], writes=[x1tB])
                        S.op("dve", lambda e, tt=tt: e.tensor_tensor(out=acc[:, tt, :], in0=acc[:, tt, :], in1=g2bc[:], op=ALU.mult),
                             reads=accR[tt] + [g2bcB], writes=accR[tt])
                        S.op("pool", lambda e, tt=tt: e.tensor_tensor(out=acc[:, tt, :], in0=acc[:, tt, :], in1=x1t[:], op=ALU.add),
                             reads=accR[tt] + [x1tB], writes=accR[tt])
                        S.op("act", lambda e, tt=tt: e.activation(out=jk2[:], in_=acc[:, tt, :], func=AF.Square, accum_out=st2[:, 0:1]),
                             reads=accR[tt], writes=[jk2B, st2B])
                        S.op("dve", lambda e: e.tensor_scalar(out=st2[:, 1:2], in0=st2[:, 0:1], scalar1=1.0 / D, scalar2=EPS,
                                                              op0=ALU.mult, op1=ALU.add), reads=[st2B], writes=[st2B])
                        S.op("act", lambda e: e.activation(out=st2[:, 1:2], in_=st2[:, 1:2], func=AF.Ln), reads=[st2B], writes=[st2B])
                        S.op("act", lambda e: e.activation(out=st2[:, 1:2], in_=st2[:, 1:2], func=AF.Exp, scale=-0.5), reads=[st2B], writes=[st2B])
                        S.op("dve", lambda e, tt=tt: e.scalar_tensor_tensor(
                            out=acc[:, tt, :], in0=acc[:, tt, :], scalar=st2[:, 1:2], in1=fgt[:], op0=ALU.mult, op1=ALU.mult),
                            reads=accR[tt] + [st2B, fgtB], writes=accR[tt])
                        S.dma("sp", out[r0:r0 + 128, :], acc[:, tt, :], accD[tt], reads=accR[tt], writes=[])
                S.barrier()
        S.barrier()
        S.emit()
    return nc


_NC_CACHE = {}


def _consts():
    p = np.arange(128)
    same = (p[:, None] // 64) == (p[None, :] // 64)
    le = (same & (p[:, None] <= p[None, :])).astype(np.float32)
    gt = (same & (p[:, None] > p[None, :])).astype(np.float32)
    ge = (same & (p[:, None] >= p[None, :])).astype(np.float32)
    lt = (same & (p[:, None] < p[None, :])).astype(np.float32)
    return dict(c_ident=np.eye(128, dtype=np.float32), c_le=le, c_gt=gt, c_ge=ge, c_lt=lt,
                c_ones=np.ones((128, 128), np.float32))


def _rope_tables(pos, scale):
    n_freq = 32
    inv_freq = (np.float32(10000.0) ** (-np.arange(n_freq, dtype=np.float32) / np.float32(n_freq))).astype(np.float32)
    row = (pos // 64).astype(np.float32)
    col = (pos % 64).astype(np.float32)
    ang = np.concatenate([row[:, None] * inv_freq, col[:, None] * inv_freq], axis=-1).astype(np.float32)
    return (np.cos(ang) * np.float32(scale)).astype(np.float32), (np.sin(ang) * np.float32(scale)).astype(np.float32)


def kernel(x, c, ctx, c_ctx, w_mod, b_mod, norm1_g, norm2_g, w_in, hg_lb_logits, hg_norm_g,
           ret_decay_logit, w_branch_hgrn, w_branch_ret, w_out, w_ff1, w_ff2, final_norm_g):
    f32 = lambda a: np.ascontiguousarray(np.asarray(a, dtype=np.float32))
    x, c, ctx, c_ctx = f32(x), f32(c), f32(ctx), f32(c_ctx)
    if "nc" not in _NC_CACHE:
        _NC_CACHE["nc"] = build_program()
    nc = _NC_CACHE["nc"]
    colT = lambda v: np.ascontiguousarray(f32(v).reshape(-1, 128).T)
    shared = dict(
        w_mod=f32(w_mod)[0], b_modT=colT(f32(b_mod)[0]), n1gT=colT(f32(norm1_g)[0]), n2gT=colT(f32(norm2_g)[0]),
        fg_bc=np.ascontiguousarray(np.broadcast_to(f32(final_norm_g)[None, :], (128, D))),
        w_in=f32(w_in)[0],
        lbl=np.ascontiguousarray(np.broadcast_to(f32(hg_lb_logits)[None], (128, 2, 2, 2048))),
        hgain=np.ascontiguousarray(np.broadcast_to(f32(hg_norm_g)[0][None, :], (128, 128))),
        rdl=np.ascontiguousarray(np.broadcast_to(f32(ret_decay_logit)[0][None], (128, 2, 16))),
        w_bh=f32(w_branch_hgrn)[0], w_br=f32(w_branch_ret)[0], w_o=f32(w_out)[0], w1=f32(w_ff1)[0], w2=f32(w_ff2)[0],
    )
    shared.update(_consts())
    sk = 128.0 ** -0.5
    in_maps = []
    for core in range(8):
        b, s = core // 4, core % 4
        others = [j for j in range(4) if j != s]
        xf = np.concatenate([ctx[b]] + [x[b, j * 1024:(j + 1) * 1024] for j in others], axis=0)
        ccm = np.stack([c[b], c_ctx], axis=-1).reshape(32, 128, 2).transpose(1, 0, 2)
        cosK = np.empty((NT_ALL * 128, 64), np.float32)
        sinK = np.empty((NT_ALL * 128, 64), np.float32)
        cosK[:256] = sk
        sinK[:256] = 0.0
        for fi, j in enumerate(others):
            cs, sn = _rope_tables(np.arange(j * 1024, (j + 1) * 1024), sk)
            cosK[256 + fi * 1024:256 + (fi + 1) * 1024] = cs
            sinK[256 + fi * 1024:256 + (fi + 1) * 1024] = sn
        cs, sn = _rope_tables(np.arange(s * 1024, (s + 1) * 1024), sk)
        cosK[3328:] = cs
        sinK[3328:] = sn
        rK = np.stack([cosK.reshape(NT_ALL, 128, 64), sinK.reshape(NT_ALL, 128, 64)], axis=2).transpose(1, 0, 2, 3)
        cq, sq = _rope_tables(np.arange(s * 1024, (s + 1) * 1024), 1.0)
        rQ = np.stack([cq.reshape(8, 128, 64), sq.reshape(8, 128, 64)], axis=2).transpose(1, 0, 2, 3)
        mkv = np.ones((NT_ALL, 2), np.float32)
        for fi, j in enumerate(others):
            mkv[2 + fi * 8:2 + (fi + 1) * 8, 0] = 1.0 if j < s else 0.0
            mkv[2 + fi * 8:2 + (fi + 1) * 8, 1] = 1.0 if j > s else 0.0
        mk = np.broadcast_to(mkv[None], (128, NT_ALL, 2))
        sel = np.zeros((128, 2), np.float32)
        sel[:64, 0] = 1.0
        sel[64:, 1] = 1.0
        selm = sel[:, None, None, :] * mkv[None, :, :, None]
        wzs = np.stack([shared["w_in"][:, 2048:4096] if j < s else shared["w_in"][:, 4096:6144] for j in others], axis=0)
        m = dict(shared)
        m["wz"] = np.ascontiguousarray(wzs)
        m.update(xo=np.ascontiguousarray(x[b, s * 1024:(s + 1) * 1024]), xf=np.ascontiguousarray(xf),
                 cc=np.ascontiguousarray(ccm), ropeK=np.ascontiguousarray(rK), ropeQ=np.ascontiguousarray(rQ),
                 mk=np.ascontiguousarray(mk), selm=np.ascontiguousarray(selm.astype(np.float32)))
        in_maps.append(m)
    res = run_bass_kernel_spmd(nc, in_maps, core_ids=list(range(8)))
    outp = np.empty((2, 4096, D), np.float32)
    for core in range(8):
        b, s = core // 4, core % 4
        outp[b, s * 1024:(s + 1) * 1024] = res.results[core]["out"]
    return outp
```

```python
import numpy as np
from contextlib import ExitStack
import ml_dtypes
import concourse.bass as bass
import concourse.mybir as mybir
from concourse.bass_utils import run_bass_kernel_spmd

F32 = mybir.dt.float32
BF16 = mybir.dt.bfloat16
AF = mybir.ActivationFunctionType
ALU = mybir.AluOpType
AX = mybir.AxisListType

D = 4096
NT_ALL = 34
OWN0 = 26
EPS = 1e-6
DEBUG = False


class Buf:
    __slots__ = ("name", "w", "r", "dsem", "dcnt")

    def __init__(self, name):
        self.name = name
        self.w = None
        self.r = []
        self.dsem = None
        self.dcnt = 0


class Sched:
    ENG = ("pe", "act", "dve", "pool", "sp")

    def __init__(self, nc, es):
        self.nc = nc
        self.es = es
        self.streams = {e: [] for e in self.ENG}
        self.sems = {}
        self.cnt = {e: 0 for e in self.ENG}
        self.known = {e: {} for e in self.ENG}
        for e in self.ENG:
            self.sems[e] = es.enter_context(nc.semaphore("s_" + e))
        self.nd = 0
        self.dbufs = []

    def _dsem(self, buf):
        if buf.dsem is None:
            key = "d%d" % self.nd
            self.nd += 1
            self.sems[key] = self.es.enter_context(self.nc.semaphore(key))
            buf.dsem = key
            self.dbufs.append(buf)
        return buf.dsem

    def _need(self, eng, need):
        kn = self.known[eng]
        for k, v in need.items():
            if k == eng and v > self.cnt[eng]:
                continue
            if kn.get(k, 0) < v:
                kn[k] = v
                sem = self.sems[k]
                self.streams[eng].append(lambda e, sem=sem, v=v: e.wait_ge(sem, v))

    def _waits(self, eng, reads, writes):
        need = {}
        for b in reads:
            if b.w is not None:
                k, v = b.w
                need[k] = max(need.get(k, 0), v)
        for b in writes:
            if b.w is not None:
                k, v = b.w
                need[k] = max(need.get(k, 0), v)
            for (k, v) in b.r:
                need[k] = max(need.get(k, 0), v)
        self._need(eng, need)

    def _commit(self, ev, reads, writes):
        for b in reads:
            if len(b.r) > 64:
                m = {}
                for (k, v) in b.r:
                    m[k] = max(m.get(k, 0), v)
                b.r = list(m.items())
            b.r.append(ev)
        for b in writes:
            b.w = ev
            b.r = []

    def op(self, eng, fn, reads=(), writes=()):
        self._waits(eng, reads, writes)
        self.cnt[eng] += 1
        ev = (eng, self.cnt[eng])
        sem = self.sems[eng]
        self.streams[eng].append(lambda e, fn=fn, sem=sem: fn(e).then_inc(sem, 1))
        self._commit(ev, reads, writes)

    def op_noinc(self, eng, fn, reads=(), writes=()):
        self._waits(eng, reads, writes)
        ev = (eng, self.cnt[eng] + 1)
        self.streams[eng].append(lambda e, fn=fn: fn(e))
        self._commit(ev, reads, writes)

    def dma(self, q, out_ap, in_ap, sb, reads=(), writes=()):
        self._waits(q, reads, writes)
        key = self._dsem(sb)
        sb.dcnt += 16
        ev = (key, sb.dcnt)
        sem = self.sems[key]
        self.streams[q].append(
            lambda e, o=out_ap, i=in_ap, sem=sem: e.dma_start(out=o, in_=i).then_inc(sem, 16))
        self._commit(ev, reads, writes)

    def barrier(self):
        need = {e: self.cnt[e] for e in self.ENG if self.cnt[e] > 0}
        for b in self.dbufs:
            if b.dcnt > 0:
                need[b.dsem] = b.dcnt
        for e in self.ENG:
            self._need(e, dict(need))

    def emit(self):
        nc = self.nc
        st = self.streams
        with nc.Block() as block:
            @block.tensor
            def _(e):
                for f in st["pe"]:
                    f(e)

            @block.scalar
            def _(e):
                for f in st["act"]:
                    f(e)

            @block.vector
            def _(e):
                for f in st["dve"]:
                    f(e)

            @block.gpsimd
            def _(e):
                for f in st["pool"]:
                    f(e)

            @block.sync
            def _(e):
                for f in st["sp"]:
                    f(e)


def build_program():
    nc = bass.Bass("TRN2", target_bir_lowering=False)

    def din(name, shape, dt=F32):
        return nc.dram_tensor(name, list(shape), dt, kind="ExternalInput").ap()

    xo = din("xo", [1024, D])
    xf = din("xf", [3328, D])
    cc = din("cc", [128, 32, 2])
    w_mod = din("w_mod", [D, 6 * D])
    b_modT = din("b_modT", [128, 192])
    n1gT = din("n1gT", [128, 32])
    n2gT = din("n2gT", [128, 32])
    fg_bc = din("fg_bc", [128, D])
    w_in = din("w_in", [D, 26624])
    wz = din("wz", [3, D, 2048])
    lbl = din("lbl", [128, 2, 2, 2048])
    hgain = din("hgain", [128, 128])
    rdl = din("rdl", [128, 2, 16])
    w_bh = din("w_bh", [2048, D])
    w_br = din("w_br", [2048, D])
    w_o = din("w_o", [D, D])
    w1 = din("w1", [D, 4 * D])
    w2 = din("w2", [4 * D, D])
    c_ident = din("c_ident", [128, 128])
    c_le = din("c_le", [128, 128])
    c_gt = din("c_gt", [128, 128])
    c_ge = din("c_ge", [128, 128])
    c_lt = din("c_lt", [128, 128])
    c_ones = din("c_ones", [128, 128])
    ropeK = din("ropeK", [128, NT_ALL, 2, 64])
    ropeQ = din("ropeQ", [128, 8, 2, 64])
    mk_d = din("mk", [128, NT_ALL, 2])
    selm_d = din("selm", [128, NT_ALL, 2, 2])
    out = nc.dram_tensor("out", [1024, D], F32, kind="ExternalOutput").ap()

    def dscr(name, shape, dt):
        if DEBUG and name in ("X1", "MT", "YT", "MODT"):
            return nc.dram_tensor(name, list(shape), dt, kind="ExternalOutput").ap()
        return nc.dram_tensor(name, list(shape), dt).ap()

    ZF = dscr("ZF", [16, 128, NT_ALL, 128], F32)
    ZB = dscr("ZB", [16, 128, NT_ALL, 128], F32)
    VH = dscr("VH", [16, 128, NT_ALL, 128], BF16)
    RK = dscr("RK", [16, 128, NT_ALL, 128], BF16)
    RV = dscr("RV", [16, 128, NT_ALL, 128], BF16)
    HQ = dscr("HQ", [16, 128, 8, 128], BF16)
    HG = dscr("HG", [16, 128, 8, 128], BF16)
    RQ = dscr("RQ", [16, 128, 8, 128], BF16)
    RG = dscr("RG", [16, 128, 8, 128], BF16)
    GT = dscr("GT", [64, 128, 1024], BF16)
    X1 = dscr("X1", [1024, D], F32)
    MT = dscr("MT", [32, 128, 1024], BF16)
    if DEBUG:
        YT = dscr("YT", [128, 32, 1024], BF16)
        MODT = dscr("MODT", [128, 192, 2], F32)
    scrB = {id(t): Buf("scr%d" % i) for i, t in enumerate([ZF, ZB, VH, RK, RV, HQ, HG, RQ, RG, GT, X1, MT])}

    with ExitStack() as es:
        S = Sched(nc, es)

        sbn = [0]

        def sb(name, shape, dt, stack=None):
            sbn[0] += 1
            t = (stack or es).enter_context(nc.sbuf_tensor("sb%d_%s" % (sbn[0], name), list(shape), dt))
            return t, Buf(name)

        banks = []
        for i in range(8):
            t = es.enter_context(nc.psum_tensor("pb%d" % i, [128, 512], F32))
            banks.append((t, Buf("pb%d" % i)))
        bank_i = [0]

        def nb():
            b = banks[bank_i[0] % 8]
            bank_i[0] += 1
            return b

        def mm(outap, lhsT, rhs, pbuf, reads, first, last):
            fn = lambda e: e.matmul(outap, lhsT=lhsT, rhs=rhs, start=first, stop=last)
            if last:
                S.op("pe", fn, reads=reads, writes=[pbuf])
            else:
                S.op_noinc("pe", fn, reads=reads, writes=[pbuf] if first else [])

        def tr(outap, inap, ident, pbuf, reads):
            S.op("pe", lambda e: e.transpose(outap, inap, ident), reads=reads, writes=[pbuf])

        ident, identB = sb("ident", [128, 128], F32)
        m_le, m_leB = sb("m_le", [128, 128], F32)
        m_gt, m_gtB = sb("m_gt", [128, 128], F32)
        m_ge, m_geB = sb("m_ge", [128, 128], F32)
        m_lt, m_ltB = sb("m_lt", [128, 128], F32)
        ones, onesB = sb("ones", [128, 128], F32)
        for t, b, src in ((ident, identB, c_ident), (m_le, m_leB, c_le), (m_gt, m_gtB, c_gt),
                          (m_ge, m_geB, c_ge), (m_lt, m_ltB, c_lt), (ones, onesB, c_ones)):
            S.dma("sp", t[:], src, b, writes=[b])
        modT, modTB = sb("modT", [128, 192, 2], F32)
        G1, G1B = sb("G1", [128, 32, 2], F32)
        G2, G2B = sb("G2", [128, 32], F32)
        n1g, n1gB = sb("n1g", [128, 32], F32)
        n2g, n2gB = sb("n2g", [128, 32], F32)
        bmod, bmodB = sb("bmod", [128, 192], F32)
        S.dma("sp", n1g[:], n1gT, n1gB, writes=[n1gB])
        S.dma("sp", n2g[:], n2gT, n2gB, writes=[n2gB])
        S.dma("sp", bmod[:], b_modT, bmodB, writes=[bmodB])
        mk, mkB = sb("mk", [128, NT_ALL, 2], F32)
        selm, selmB = sb("selm", [128, NT_ALL, 2, 2], F32)
        S.dma("sp", mk[:], mk_d, mkB, writes=[mkB])
        S.dma("sp", selm[:], selm_d, selmB, writes=[selmB])
        lg, lgB = sb("lg", [128, 2, 16], F32)
        S.dma("sp", lg[:], rdl, lgB, writes=[lgB])
        S.op("act", lambda e: e.activation(out=lg[:], in_=lg[:], func=AF.Sigmoid), reads=[lgB], writes=[lgB])
        S.op("act", lambda e: e.activation(out=lg[:], in_=lg[:], func=AF.Ln), reads=[lgB], writes=[lgB])

        with ExitStack() as ph:
            ccs, ccsB = sb("ccs", [128, 32, 2], F32, ph)
            ccb, ccbB = sb("ccb", [128, 32, 2], BF16, ph)
            sg0, sg0B = sb("sg0", [128, 32, 2], F32, ph)
            S.dma("sp", ccs[:], cc, ccsB, writes=[ccsB])
            S.op("act", lambda e: e.activation(out=sg0[:], in_=ccs[:], func=AF.Sigmoid), reads=[ccsB], writes=[sg0B])
            S.op("dve", lambda e: e.tensor_tensor(out=ccb[:], in0=ccs[:], in1=sg0[:], op=ALU.mult),
                 reads=[ccsB, sg0B], writes=[ccbB])
            wm = [sb("wm%d" % i, [128, 32, 512], BF16, ph) for i in range(2)]
            wv = w_mod.rearrange("(kc p) n -> p kc n", p=128)
            pm, pmB = nb()
            pmv = pm[:, 0:384].rearrange("p (n j) -> p n j", j=2)
            NB = 48
            S.dma("pool", wm[0][0][:], wv[:, :, 0:512], wm[0][1], writes=[wm[0][1]])
            for g in range(NB):
                wt, wB = wm[g % 2]
                if g + 1 < NB:
                    S.dma("pool", wm[(g + 1) % 2][0][:], wv[:, :, (g + 1) * 512:(g + 2) * 512],
                          wm[(g + 1) % 2][1], writes=[wm[(g + 1) % 2][1]])
                for n in range(4):
                    nn = g * 4 + n
                    for kc in range(32):
                        mm(pmv[:, nn, :], wt[:, kc, n * 128:(n + 1) * 128], ccb[:, kc, :], pmB,
                           [wB, ccbB], kc == 0, kc == 31)
            for j in range(2):
                S.op("dve", lambda e, j=j: e.tensor_tensor(out=modT[:, :, j], in0=pmv[:, :, j], in1=bmod[:], op=ALU.add),
                     reads=[pmB, bmodB], writes=[modTB])
            for j in range(2):
                S.op("dve", lambda e, j=j: e.scalar_tensor_tensor(
                    out=G1[:, :, j], in0=modT[:, 32:64, j], scalar=1.0, in1=n1g[:], op0=ALU.add, op1=ALU.mult),
                    reads=[modTB, n1gB], writes=[G1B])
            S.op("dve", lambda e: e.scalar_tensor_tensor(
                out=G2[:], in0=modT[:, 128:160, 0], scalar=1.0, in1=n2g[:], op0=ALU.add, op1=ALU.mult),
                reads=[modTB, n2gB], writes=[G2B])
        S.barrier()
        if DEBUG:
            S.dma("sp", MODT, modT[:], modTB, reads=[modTB], writes=[])

        def build_bc(dst, dstB, colap_fn, colB):
            with ExitStack() as ph2:
                dg, dgB = sb("dg_" + dstB.name, [128, 128], F32, ph2)
                for kc in range(32):
                    S.op("dve", lambda e, kc=kc: e.tensor_scalar(
                        out=dg[:], in0=ident[:], scalar1=colap_fn(kc), scalar2=None, op0=ALU.mult),
                        reads=[identB, colB], writes=[dgB])
                    pt, pB = nb()
                    mm(pt[:, 0:128], ones[:], dg[:], pB, [onesB, dgB], True, True)
                    S.op("act", lambda e, kc=kc, pt=pt: e.activation(
                        out=dst[:, kc * 128:(kc + 1) * 128], in_=pt[:, 0:128], func=AF.Copy),
                        reads=[pB], writes=[dstB])
            S.barrier()

        def norm_T(ph, xsrc, nt, hT, hTB, Gcol, GB, SHcol, SHB, tag):
            xts = [sb("xt%d_%s" % (i, tag), [128, D], F32, ph) for i in range(2)]
            jk, jkB = sb("jk_" + tag, [128, D], BF16, ph)
            sts = [sb("st%d_%s" % (i, tag), [128, 2], F32, ph) for i in range(2)]
            for t in range(nt):
                xt, xtB = xts[t % 2]
                st, stB = sts[t % 2]
                S.dma("sp", xt[:], xsrc[t * 128:(t + 1) * 128, :], xtB, writes=[xtB])
                S.op("act", lambda e, xt=xt, st=st: e.activation(out=jk[:], in_=xt[:], func=AF.Square, accum_out=st[:, 0:1]),
                     reads=[xtB], writes=[jkB, stB])
                S.op("dve", lambda e, st=st: e.tensor_scalar(out=st[:, 1:2], in0=st[:, 0:1], scalar1=1.0 / D, scalar2=EPS,
                                                      op0=ALU.mult, op1=ALU.add), reads=[stB], writes=[stB])
                S.op("act", lambda e, st=st: e.activation(out=st[:, 1:2], in_=st[:, 1:2], func=AF.Ln), reads=[stB], writes=[stB])
                S.op("act", lambda e, st=st: e.activation(out=st[:, 1:2], in_=st[:, 1:2], func=AF.Exp, scale=-0.5), reads=[stB], writes=[stB])
                S.op("dve", lambda e, xt=xt, st=st: e.tensor_scalar(out=xt[:], in0=xt[:], scalar1=st[:, 1:2], scalar2=None,
                                                      op0=ALU.mult), reads=[stB, xtB], writes=[xtB])
                for k4 in range(8):
                    pt, pB = nb()
                    for q in range(4):
                        kc = k4 * 4 + q
                        tr(pt[:, q * 128:(q + 1) * 128], xt[:, kc * 128:(kc + 1) * 128], ident[:], pB, [xtB, identB])
                    for q in range(4):
                        kc = k4 * 4 + q
                        S.op("act", lambda e, kc=kc, q=q, pt=pt, t=t: e.activation(
                            out=hT[:, kc, t * 128:(t + 1) * 128], in_=pt[:, q * 128:(q + 1) * 128],
                            func=AF.Identity, scale=Gcol(kc), bias=SHcol(kc)),
                            reads=[pB, GB, SHB], writes=[hTB])

        FAMS = [("vh", 0, "copy", VH, BF16), ("zf", 2048, "z", ZF, F32), ("zb", 4096, "z", ZB, F32),
                ("rk", 6144, "ropeK", RK, BF16), ("rv", 8192, "copy", RV, BF16),
                ("hq", 10240, "siluc", HQ, BF16), ("hg", 12288, "silu", HG, BF16),
                ("rq", 14336, "ropeQ", RQ, BF16), ("rg", 16384, "silu", RG, BF16)]
        winv = w_in.rearrange("(kc p) n -> p kc n", p=128)

        def seg_pass(xsrc, nt, t0, own, modj, tag, fsel=None):
            with ExitStack() as ph:
                hT, hTB = sb("hT_" + tag, [128, 32, nt * 128], BF16, ph)
                wb = [sb("wi%d_%s" % (i, tag), [128, 32, 512], BF16, ph) for i in range(2)]
                S.dma("pool", wb[0][0][:], winv[:, :, 0:512], wb[0][1], writes=[wb[0][1]])
                with ExitStack() as ph1:
                    norm_T(ph1, xsrc, nt, hT, hTB, lambda kc: G1[:, kc, modj:modj + 1], G1B,
                           lambda kc: modT[:, kc, modj:modj + 1], modTB, tag)
                S.barrier()
                stg32, stg32B = sb("s32_" + tag, [128, 4, nt, 128], F32, ph)
                stg16, stg16B = sb("s16_" + tag, [128, 4, nt, 128], BF16, ph)
                tmp, tmpB = sb("tmp_" + tag, [128, 512], F32, ph)
                tmp2, tmp2B = sb("tmp2_" + tag, [128, 512], F32, ph)
                rK, rKB = sb("rK_" + tag, [128, nt, 2, 64], F32, ph)
                S.dma("sp", rK[:], ropeK[:, t0:t0 + nt], rKB, writes=[rKB])
                if own:
                    rQ, rQB = sb("rQ_" + tag, [128, 8, 2, 64], F32, ph)
                    S.dma("sp", rQ[:], ropeQ, rQB, writes=[rQB])
                    gst, gstB = sb("gst_" + tag, [128, 1024], BF16, ph)
                fams = FAMS if own else FAMS[:5]
                groups = []
                for (fn_, off, kind, scr, dt) in fams:
                    if fsel is not None and fn_ == "zb":
                        continue
                    for g4 in range(4):
                        if fsel is not None and fn_ == "zf":
                            groups.append((g4 * 512, kind, scr, dt, g4, wz[fsel].rearrange("(kc p) n -> p kc n", p=128)))
                        else:
                            groups.append((off + g4 * 512, kind, scr, dt, g4, winv))
                if own:
                    for gg in range(16):
                        groups.append((18432 + gg * 512, "gate", GT, BF16, gg, winv))
                ng = len(groups)
                assert groups[0][0] == 0
                for gi, (coff, kind, scr, dt, g4, wsrc) in enumerate(groups):
                    wt, wB = wb[gi % 2]
                    if gi + 1 < ng:
                        c2 = groups[gi + 1][0]
                        S.dma("pool", wb[(gi + 1) % 2][0][:], groups[gi + 1][5][:, :, c2:c2 + 512], wb[(gi + 1) % 2][1],
                              writes=[wb[(gi + 1) % 2][1]])
                    if kind == "gate":
                        for n in range(4):
                            for half in range(2):
                                pt, pB = nb()
                                for kc in range(32):
                                    mm(pt[:], wt[:, kc, n * 128:(n + 1) * 128], hT[:, kc, half * 512:(half + 1) * 512],
                                       pB, [wB, hTB], kc == 0, kc == 31)
                                S.op("act", lambda e, pt=pt, half=half: e.activation(
                                    out=gst[:, half * 512:(half + 1) * 512], in_=pt[:], func=AF.Sigmoid),
                                    reads=[pB], writes=[gstB])
                            S.dma("sp", GT[g4 * 4 + n], gst[:], gstB, reads=[gstB], writes=[scrB[id(GT)]])
                        continue
                    stg, stgB = (stg32, stg32B) if dt == F32 else (stg16, stg16B)
                    for t in range(nt):
                        pt, pB = nb()
                        for kc in range(32):
                            mm(pt[:], hT[:, kc, t * 128:(t + 1) * 128], wt[:, kc, :], pB, [wB, hTB], kc == 0, kc == 31)
                        pv = pt[:].rearrange("p (h c) -> p h c", h=4)
                        if kind in ("copy", "z"):
                            S.op("act", lambda e, pv=pv, t=t, stg=stg: e.activation(out=stg[:, :, t, :], in_=pv, func=AF.Copy),
                                 reads=[pB], writes=[stgB])
                        elif kind in ("silu", "siluc"):
                            S.op("act", lambda e, pt=pt: e.activation(out=tmp[:], in_=pt[:], func=AF.Sigmoid),
                                 reads=[pB], writes=[tmpB])
                            cst = (128.0 ** -0.5) if kind == "siluc" else 1.0
                            S.op("dve", lambda e, pv=pv, t=t, cst=cst, stg=stg: e.scalar_tensor_tensor(
                                out=stg[:, :, t, :], in0=pv, scalar=cst, in1=tmp[:].rearrange("p (h c) -> p h c", h=4),
                                op0=ALU.mult, op1=ALU.mult), reads=[pB, tmpB], writes=[stgB])
                        else:
                            rt, rtB = (rK, rKB) if kind == "ropeK" else (rQ, rQB)
                            S.op("act", lambda e, pt=pt: e.activation(out=tmp[:], in_=pt[:], func=AF.Copy),
                                 reads=[pB], writes=[tmpB])
                            tv = tmp[:].rearrange("p (h c two) -> p h c two", h=4, two=2)
                            t2v = tmp2[:].rearrange("p (h c two) -> p h c two", h=4, two=2)
                            sv = stg[:, :, t, :].rearrange("p h (c two) -> p h c two", two=2)
                            for h in range(4):
                                a1, a2 = tv[:, h, :, 0], tv[:, h, :, 1]
                                cs, sn = rt[:, t, 0, :], rt[:, t, 1, :]
                                b1, b2 = t2v[:, h, :, 0], t2v[:, h, :, 1]
                                S.op("dve", lambda e, a1=a1, cs=cs, b1=b1: e.tensor_tensor(out=b1, in0=a1, in1=cs, op=ALU.mult),
                                     reads=[tmpB, rtB], writes=[tmp2B])
                                S.op("dve", lambda e, a2=a2, sn=sn, b2=b2: e.tensor_tensor(out=b2, in0=a2, in1=sn, op=ALU.mult),
                                     reads=[tmpB, rtB], writes=[tmp2B])
                                S.op("dve", lambda e, b1=b1, b2=b2, o=sv[:, h, :, 0]: e.tensor_tensor(out=o, in0=b1, in1=b2, op=ALU.subtract),
                                     reads=[tmp2B], writes=[stgB])
                                S.op("dve", lambda e, a1=a1, sn=sn, b1=b1: e.tensor_tensor(out=b1, in0=a1, in1=sn, op=ALU.mult),
                                     reads=[tmpB, rtB, stgB], writes=[tmp2B])
                                S.op("dve", lambda e, a2=a2, cs=cs, b2=b2: e.tensor_tensor(out=b2, in0=a2, in1=cs, op=ALU.mult),
                                     reads=[tmpB, rtB], writes=[tmp2B])
                                S.op("dve", lambda e, b1=b1, b2=b2, o=sv[:, h, :, 1]: e.tensor_tensor(out=o, in0=b1, in1=b2, op=ALU.add),
                                     reads=[tmp2B], writes=[stgB])
                    ts0 = t0 if scr.shape[2] == NT_ALL else 0
                    S.dma("sp", scr[g4 * 4:(g4 + 1) * 4, :, ts0:ts0 + nt, :].rearrange("h p t c -> p h t c"),
                          stg[:], stgB, reads=[stgB], writes=[scrB[id(scr)]])
            S.barrier()

        seg_pass(xf[0:256], 2, 0, False, 1, "c")
        for f in range(3):
            seg_pass(xf[256 + f * 1024:256 + (f + 1) * 1024], 8, 2 + f * 8, False, 0, "f%d" % f, fsel=f)
        seg_pass(xo, 8, OWN0, True, 0, "o")

        yT_stack = ExitStack()
        yT, yTB = sb("yT", [128, 32, 1024], BF16, yT_stack)
        with ExitStack() as ph:
            NTA = NT_ALL
            zt = [sb("zt%d" % d, [128, NTA, 128], F32, ph) for d in range(2)]
            kt = [sb("kt%d" % d, [128, NTA, 128], BF16, ph) for d in range(2)]
            kd = [sb("kd%d" % d, [128, NTA, 128], BF16, ph) for d in range(2)]
            vt, vtB = sb("vt", [128, NTA, 128], BF16, ph)
            qt, qtB = sb("qt", [128, 8, 128], BF16, ph)
            gt_, gtB = sb("gt", [128, 8, 128], BF16, ph)
            gg, ggB = sb("gg", [128, 8, 128], F32, ph)
            lbt, lbtB = sb("lbt", [128, 2, 2, 128], F32, ph)
            lb, lbB = sb("lb", [128, 2, 128], F32, ph)
            oml, omlB = sb("oml", [128, 2, 128], F32, ph)
            hgn, hgnB = sb("hgn", [128, 128], F32, ph)
            S.dma("sp", hgn[:], hgain, hgnB, writes=[hgnB])
            dd = [sb("dd%d" % d, [128, NTA, 2], F32, ph) for d in range(2)]
            Sst = [sb("Sst%d" % d, [128, 128], F32, ph) for d in range(2)]
            Ssn = [sb("Ssn%d" % d, [128, 16, 128], BF16, ph) for d in range(2)]
            qe = [sb("qe%d" % d, [128, 128], F32, ph) for d in range(2)]
            ke = [sb("ke%d" % d, [128, 128], F32, ph) for d in range(2)]
            ee, eeB = sb("ee", [128, 512], F32, ph)
            qT = [sb("qT%d" % d, [128, 8, 128], BF16, ph) for d in range(2)]
            kT, kTB = sb("kT", [128, 128], BF16, ph)
            scT = [sb("scT%d" % d, [128, 8, 128], BF16, ph) for d in range(2)]
            osb, osbB = sb("osb", [128, 128], F32, ph)
            ysb, ysbB = sb("ysb", [128, 128], F32, ph)
            stt, sttB = sb("stt", [128, 8], F32, ph)
            bst, bstB = sb("bst", [128, 6], F32, ph)
            lgc, lgcB = sb("lgc", [128, 2], F32, ph)
            TRI_INC = [(m_le, m_leB), (m_ge, m_geB)]
            TRI_DEC = [(m_gt, m_gtB), (m_lt, m_ltB)]

            ee2, ee2B = sb("ee2", [128, 512], F32, ph)
            qe4, qe4B = sb("qe4", [128, 4, 128], F32, ph)
            ke4, ke4B = sb("ke4", [128, 4, 128], F32, ph)
            kT4, kT4B = sb("kT4", [128, 4, 128], BF16, ph)
            oall, oallB = sb("oall", [128, 8, 128], F32, ph)
            stall, stallB = sb("stall", [128, 3, 8], F32, ph)
            bst8, bst8B = sb("bst8", [128, 8, 6], F32, ph)
            mv8, mv8B = sb("mv8", [128, 8, 2], F32, ph)

            def bc3(ap2d, n):
                return ap2d.rearrange("p (o c) -> p o c", o=1).to_broadcast([128, n, 128])

            def loads_z(kind, h):
                if kind == "hg":
                    S.dma("sp", zt[0][0][:], ZF[h], zt[0][1], reads=[scrB[id(ZF)]], writes=[zt[0][1]])
                    S.dma("sp", zt[1][0][:, 0:2], ZB[h][:, 0:2], zt[1][1], reads=[scrB[id(ZB)]], writes=[zt[1][1]])
                    S.dma("sp", zt[1][0][:, 2:OWN0], ZF[h][:, 2:OWN0], zt[1][1], reads=[scrB[id(ZF)]], writes=[zt[1][1]])
                    S.dma("sp", zt[1][0][:, OWN0:NT_ALL], ZB[h][:, OWN0:NT_ALL], zt[1][1], reads=[scrB[id(ZB)]], writes=[zt[1][1]])
                else:
                    S.dma("sp", kt[0][0][:], RK[h], kt[0][1], reads=[scrB[id(RK)]], writes=[kt[0][1]])

            def scan_head(kind, h, nxt):
                hg = kind == "hg"
                if hg:
                    S.dma("sp", vt[:], VH[h], vtB, reads=[scrB[id(VH)]], writes=[vtB])
                    S.dma("sp", qt[:], HQ[h], qtB, reads=[scrB[id(HQ)]], writes=[qtB])
                    S.dma("sp", gt_[:], HG[h], gtB, reads=[scrB[id(HG)]], writes=[gtB])
                    S.dma("sp", lbt[:], lbl[:, :, :, h * 128:(h + 1) * 128], lbtB, writes=[lbtB])
                    S.op("dve", lambda e: e.tensor_tensor(out=lb[:], in0=lbt[:, 0], in1=lbt[:, 1], op=ALU.subtract),
                         reads=[lbtB], writes=[lbB])
                    S.op("act", lambda e: e.activation(out=lb[:], in_=lb[:], func=AF.Sigmoid), reads=[lbB], writes=[lbB])
                    S.op("dve", lambda e: e.tensor_scalar(out=oml[:], in0=lb[:], scalar1=-1.0, scalar2=1.0,
                                                          op0=ALU.mult, op1=ALU.add), reads=[lbB], writes=[omlB])
                    for d in range(2):
                        z, zB = zt[d]
                        k_, kB_ = kt[d]
                        S.op("act", lambda e, z=z: e.activation(out=z[:], in_=z[:], func=AF.Sigmoid), reads=[zB], writes=[zB])
                        S.op("dve", lambda e, z=z, d=d: e.tensor_tensor(
                            out=z[:], in0=z[:], in1=oml[:, d:d + 1, :].to_broadcast([128, NTA, 128]), op=ALU.mult),
                            reads=[zB, omlB], writes=[zB])
                        S.op("dve", lambda e, z=z, d=d: e.tensor_tensor(
                            out=z[:], in0=z[:], in1=lb[:, d:d + 1, :].to_broadcast([128, NTA, 128]), op=ALU.add),
                            reads=[zB, lbB], writes=[zB])
                        S.op("dve", lambda e, z=z, k_=k_: e.tensor_scalar(out=k_[:], in0=z[:], scalar1=-1.0, scalar2=1.0,
                                                                         op0=ALU.mult, op1=ALU.add), reads=[zB], writes=[kB_])
                        S.op("pool", lambda e, k_=k_, d=d: e.tensor_tensor(
                            out=k_[:], in0=k_[:], in1=mk[:, :, d:d + 1].to_broadcast([128, NTA, 128]), op=ALU.mult),
                            reads=[kB_, mkB], writes=[kB_])
                        S.op("act", lambda e, z=z: e.activation(out=z[:], in_=z[:], func=AF.Ln), reads=[zB, kB_], writes=[zB])
                else:
                    S.dma("sp", vt[:], RV[h], vtB, reads=[scrB[id(RV)]], writes=[vtB])
                    S.dma("sp", qt[:], RQ[h], qtB, reads=[scrB[id(RQ)]], writes=[qtB])
                    S.dma("sp", gt_[:], RG[h], gtB, reads=[scrB[id(RG)]], writes=[gtB])
                    for d in range(2):
                        z, zB = zt[d]
                        S.op("dve", lambda e, d=d: e.tensor_copy(out=lgc[:, d:d + 1], in_=lg[:, d, h:h + 1]), reads=[lgB], writes=[lgcB])
                        S.op("pool", lambda e, z=z: e.memset(z[:], 0.0), writes=[zB])
                        S.op("act", lambda e, z=z, d=d: e.activation(out=z[:], in_=z[:], func=AF.Identity, scale=1.0, bias=lgc[:, d:d + 1]),
                             reads=[zB, lgcB], writes=[zB])
                    S.op("pool", lambda e: e.tensor_tensor(
                        out=kt[1][0][:], in0=kt[0][0][:], in1=mk[:, :, 1:2].to_broadcast([128, NTA, 128]), op=ALU.mult),
                        reads=[kt[0][1], mkB], writes=[kt[1][1]])
                    S.op("pool", lambda e: e.tensor_tensor(
                        out=kt[0][0][:], in0=kt[0][0][:], in1=mk[:, :, 0:1].to_broadcast([128, NTA, 128]), op=ALU.mult),
                        reads=[kt[0][1], mkB], writes=[kt[0][1]])
                if hg:
                    S.op("dve", lambda e: e.tensor_tensor(out=gg[:], in0=gt_[:], in1=bc3(hgn[:], 8), op=ALU.mult),
                         reads=[gtB, hgnB], writes=[ggB])
                else:
                    S.op("dve", lambda e: e.tensor_copy(out=gg[:], in_=gt_[:]), reads=[gtB], writes=[ggB])
                for d in range(2):
                    z, zB = zt[d]
                    k_, kB_ = kt[d]
                    kdt, kdB = kd[d]
                    ddt, ddB = dd[d]
                    tri_d, tri_dB = TRI_DEC[d]
                    tri_i, tri_iB = TRI_INC[d]
                    pc, pcB = nb()
                    pcv = pc[:, 0:NTA * 2].rearrange("p (t c) -> p t c", c=2)
                    for t in range(NTA):
                        mm(pcv[:, t, :], z[:, t, :], selm[:, t, d, :], pcB, [zB, selmB], True, True)
                    S.op("act", lambda e, pcv=pcv, ddt=ddt: e.activation(out=ddt[:], in_=pcv, func=AF.Exp), reads=[pcB], writes=[ddB])
                    for t4 in range(0, NTA, 4):
                        n4 = min(4, NTA - t4)
                        pt, pB = nb()
                        for q in range(n4):
                            mm(pt[:, q * 128:(q + 1) * 128], tri_d[:], z[:, t4 + q, :], pB, [tri_dB, zB], True, True)
                        eX, eXB = (ee, eeB) if (t4 // 4) % 2 == 0 else (ee2, ee2B)
                        S.op("act", lambda e, pt=pt, n4=n4, eX=eX: e.activation(out=eX[:, 0:n4 * 128], in_=pt[:, 0:n4 * 128], func=AF.Exp),
                             reads=[pB], writes=[eXB])
                        S.op("dve", lambda e, t4=t4, n4=n4, k_=k_, kdt=kdt, eX=eX: e.tensor_tensor(
                            out=kdt[:, t4:t4 + n4, :], in0=k_[:, t4:t4 + n4, :],
                            in1=eX[:, 0:n4 * 128].rearrange("p (t c) -> p t c", c=128), op=ALU.mult),
                            reads=[kB_, eXB], writes=[kdB])
                    qTt, qTB = qT[d]
                    scTt, scTB = scT[d]
                    for t4 in range(0, 8, 4):
                        ta = OWN0 + t4
                        pt, pB = nb()
                        for q in range(4):
                            mm(pt[:, q * 128:(q + 1) * 128], tri_i[:], z[:, ta + q, :], pB, [tri_iB, zB], True, True)
                        S.op("act", lambda e, pt=pt: e.activation(out=ee[:], in_=pt[:], func=AF.Exp), reads=[pB], writes=[eeB])
                        S.op("act", lambda e, pt=pt: e.activation(out=ee2[:], in_=pt[:], func=AF.Exp, scale=-1.0), reads=[pB], writes=[ee2B])
                        S.op("dve", lambda e, t4=t4: e.tensor_tensor(
                            out=qe4[:], in0=qt[:, t4:t4 + 4, :], in1=ee[:].rearrange("p (t c) -> p t c", c=128), op=ALU.mult),
                            reads=[qtB, eeB], writes=[qe4B])
                        S.op("dve", lambda e, ta=ta, k_=k_: e.tensor_tensor(
                            out=ke4[:], in0=k_[:, ta:ta + 4, :], in1=ee2[:].rearrange("p (t c) -> p t c", c=128), op=ALU.mult),
                            reads=[kB_, ee2B], writes=[ke4B])
                        p2, p2B = nb()
                        p2k, p2kB = nb()
                        for q in range(4):
                            tr(p2[:, q * 128:(q + 1) * 128], qe4[:, q, :], ident[:], p2B, [qe4B, identB])
                        for q in range(4):
                            tr(p2k[:, q * 128:(q + 1) * 128], ke4[:, q, :], ident[:], p2kB, [ke4B, identB])
                        S.op("act", lambda e, p2=p2, t4=t4, qTt=qTt: e.activation(
                            out=qTt[:, t4:t4 + 4, :], in_=p2[:].rearrange("p (t c) -> p t c", c=128), func=AF.Copy),
                            reads=[p2B], writes=[qTB])
                        S.op("act", lambda e, p2k=p2k: e.activation(
                            out=kT4[:], in_=p2k[:].rearrange("p (t c) -> p t c", c=128), func=AF.Copy),
                            reads=[p2kB], writes=[kT4B])
                        p3, p3B = nb()
                        for q in range(4):
                            mm(p3[:, q * 128:(q + 1) * 128], kT4[:, q, :], qTt[:, t4 + q, :], p3B, [kT4B, qTB], True, True)
                        S.op("dve", lambda e, p3=p3, t4=t4, scTt=scTt, tri_i=tri_i: e.tensor_tensor(
                            out=scTt[:, t4:t4 + 4, :], in0=p3[:].rearrange("p (t c) -> p t c", c=128),
                            in1=bc3(tri_i[:], 4), op=ALU.mult), reads=[p3B, tri_iB], writes=[scTB])
                orders = []
                for d in range(2):
                    St, StB = Sst[d]
                    S.op("pool", lambda e, St=St: e.memset(St[:], 0.0), writes=[StB])
                    if d == 0:
                        orders.append([(t, c) for t in range(NTA) for c in (0, 1)])
                    else:
                        tl = [1, 0] + list(range(25, 1, -1)) + list(range(33, 25, -1))
                        orders.append([(t, c) for t in tl for c in (1, 0)])
                for i0 in range(0, NTA * 2, 4):
                    pend = []
                    for d in range(2):
                        kdt, kdB = kd[d]
                        grp = orders[d][i0:i0 + 4]
                        pt, pB = nb()
                        for q, (t, c) in enumerate(grp):
                            mm(pt[:, q * 128:(q + 1) * 128], kdt[c * 64:(c + 1) * 64, t, :], vt[c * 64:(c + 1) * 64, t, :],
                               pB, [kdB, vtB], True, True)
                        pend.append((grp, pt, pB))
                    for q in range(4):
                        for d in range(2):
                            grp, pt, pB = pend[d]
                            t, c = grp[q]
                            ddt, ddB = dd[d]
                            St, StB = Sst[d]
                            Snt, SnB = Ssn[d]
                            if t >= OWN0:
                                ci = (t - OWN0) * 2 + c
                                S.op("act", lambda e, ci=ci, Snt=Snt, St=St: e.activation(out=Snt[:, ci, :], in_=St[:], func=AF.Copy),
                                     reads=[StB], writes=[SnB])
                            S.op("dve", lambda e, pt=pt, q=q, t=t, c=c, St=St, ddt=ddt: e.scalar_tensor_tensor(
                                out=St[:], in0=St[:], scalar=ddt[:, t, c:c + 1], in1=pt[:, q * 128:(q + 1) * 128],
                                op0=ALU.mult, op1=ALU.add), reads=[StB, ddB, pB, SnB], writes=[StB])
                if nxt is not None:
                    loads_z(*nxt)
                for t in range(8):
                    ta = OWN0 + t
                    po, poB = nb()
                    ov = po[:, 0:128]
                    for d in range(2):
                        qTt, qTB = qT[d]
                        scTt, scTB = scT[d]
                        Snt, SnB = Ssn[d]
                        mm(ov, scTt[:, t, :], vt[:, ta, :], poB, [scTB, vtB], d == 0, False)
                        for c in range(2):
                            mm(po[c * 64:(c + 1) * 64, 0:128], qTt[:, t, c * 64:(c + 1) * 64], Snt[:, t * 2 + c, :], poB,
                               [qTB, SnB], False, d == 1 and c == 1)
                    S.op("act", lambda e, ov=ov, t=t: e.activation(out=oall[:, t, :], in_=ov, func=AF.Copy), reads=[poB], writes=[oallB])
                    if hg:
                        S.op("act", lambda e, t=t: e.activation(out=osb[:], in_=oall[:, t, :], func=AF.Square, accum_out=stall[:, 0, t:t + 1]),
                             reads=[oallB], writes=[osbB, stallB])
                    else:
                        S.op("dve", lambda e, t=t: e.bn_stats(out=bst8[:, t, :], in_=oall[:, t, :]), reads=[oallB], writes=[bst8B])
                        S.op("dve", lambda e, t=t: e.bn_aggr(out=mv8[:, t, :], in_=bst8[:, t, :]), reads=[bst8B], writes=[mv8B])
                if hg:
                    S.op("dve", lambda e: e.tensor_scalar(out=stall[:, 1, :], in0=stall[:, 0, :], scalar1=1.0 / 128, scalar2=EPS,
                                                          op0=ALU.mult, op1=ALU.add), reads=[stallB], writes=[stallB])
                else:
                    S.op("dve", lambda e: e.tensor_scalar(out=stall[:, 1, :], in0=mv8[:, :, 1], scalar1=EPS, scalar2=None,
                                                          op0=ALU.add), reads=[mv8B], writes=[stallB])
                S.op("act", lambda e: e.activation(out=stall[:, 1, :], in_=stall[:, 1, :], func=AF.Ln), reads=[stallB], writes=[stallB])
                S.op("act", lambda e: e.activation(out=stall[:, 1, :], in_=stall[:, 1, :], func=AF.Exp, scale=-0.5), reads=[stallB], writes=[stallB])
                for t in range(8):
                    if hg:
                        S.op("dve", lambda e, t=t: e.scalar_tensor_tensor(
                            out=ysb[:], in0=oall[:, t, :], scalar=stall[:, 1, t:t + 1], in1=gg[:, t, :], op0=ALU.mult, op1=ALU.mult),
                            reads=[oallB, stallB, ggB], writes=[ysbB])
                    else:
                        S.op("dve", lambda e, t=t: e.tensor_scalar(out=osb[:], in0=oall[:, t, :], scalar1=mv8[:, t, 0:1],
                                                                   scalar2=stall[:, 1, t:t + 1], op0=ALU.subtract, op1=ALU.mult),
                             reads=[oallB, mv8B, stallB], writes=[osbB])
                        S.op("dve", lambda e, t=t: e.tensor_tensor(out=ysb[:], in0=osb[:], in1=gg[:, t, :], op=ALU.mult),
                             reads=[osbB, ggB], writes=[ysbB])
                    py, pyB = nb()
                    tr(py[:, 0:128], ysb[:], ident[:], pyB, [ysbB, identB])
                    hh = h if hg else 16 + h
                    S.op("act", lambda e, py=py, t=t, hh=hh: e.activation(out=yT[:, hh, t * 128:(t + 1) * 128], in_=py[:, 0:128], func=AF.Copy),
                         reads=[pyB], writes=[yTB])

            heads = [("hg", h) for h in range(16)] + [("ret", h) for h in range(16)]
            loads_z(*heads[0])
            for i, (kind, h) in enumerate(heads):
                scan_head(kind, h, heads[i + 1] if i + 1 < len(heads) else None)
        S.barrier()
        if DEBUG:
            S.dma("sp", YT, yT[:], yTB, reads=[yTB], writes=[])

        with ExitStack() as ph:
            mst, mstB = sb("mst", [128, 1024], BF16, ph)
            wbb = [sb("wbb%d" % i, [128, 2, 16, 256], BF16, ph) for i in range(2)]
            gsb = [sb("gsb%d" % i, [128, 1024], BF16, ph) for i in range(2)]
            t1, t1B = sb("t1", [128, 512], F32, ph)
            t2, t2B = sb("t2", [128, 512], F32, ph)
            wbhv = w_bh.rearrange("(kc p) n -> p kc n", p=128)
            wbrv = w_br.rearrange("(kc p) n -> p kc n", p=128)

            def ldw(g):
                wt, wB = wbb[g % 2]
                S.dma("pool", wt[:, 0], wbhv[:, :, g * 256:(g + 1) * 256], wB, writes=[wB])
                S.dma("pool", wt[:, 1], wbrv[:, :, g * 256:(g + 1) * 256], wB, writes=[wB])
            ldw(0)
            for g in range(16):
                wt, wB = wbb[g % 2]
                if g + 1 < 16:
                    ldw(g + 1)
                for n in range(2):
                    nn = g * 2 + n
                    S.dma("sp", gsb[0][0][:], GT[nn], gsb[0][1], reads=[scrB[id(GT)]], writes=[gsb[0][1]])
                    S.dma("sp", gsb[1][0][:], GT[32 + nn], gsb[1][1], reads=[scrB[id(GT)]], writes=[gsb[1][1]])
                    for half in range(2):
                        ps = []
                        for br in range(2):
                            pt, pB = nb()
                            for kc in range(16):
                                mm(pt[:], wt[:, br, kc, n * 128:(n + 1) * 128], yT[:, br * 16 + kc, half * 512:(half + 1) * 512],
                                   pB, [wB, yTB], kc == 0, kc == 15)
                            ps.append((pt, pB))
                        S.op("dve", lambda e, pt=ps[0][0], half=half: e.tensor_tensor(
                            out=t1[:], in0=pt[:], in1=gsb[0][0][:, half * 512:(half + 1) * 512], op=ALU.mult),
                            reads=[ps[0][1], gsb[0][1]], writes=[t1B])
                        S.op("dve", lambda e, pt=ps[1][0], half=half: e.tensor_tensor(
                            out=t2[:], in0=pt[:], in1=gsb[1][0][:, half * 512:(half + 1) * 512], op=ALU.mult),
                            reads=[ps[1][1], gsb[1][1]], writes=[t2B])
                        S.op("pool", lambda e, nn=nn, half=half: e.tensor_tensor(
                            out=mst[:, half * 512:(half + 1) * 512], in0=t1[:], in1=t2[:], op=ALU.add),
                            reads=[t1B, t2B], writes=[mstB])
                    S.dma("sp", MT[nn], mst[:], mstB, reads=[mstB], writes=[scrB[id(MT)]])
        S.barrier()
        yT_stack.close()

        with ExitStack() as ph:
            mT, mTB = sb("mT", [128, 32, 1024], BF16, ph)
            S.dma("sp", mT[:], MT.rearrange("n p t -> p n t"), mTB, reads=[scrB[id(MT)]], writes=[mTB])
            g1bc, g1bcB = sb("g1bc", [128, D], F32, ph)
            build_bc(g1bc, g1bcB, lambda kc: modT[:, 64 + kc, 0:1], modTB)
            wob = [sb("wob%d" % i, [128, 32, 512], BF16, ph) for i in range(2)]
            xp, xpB = sb("xp", [128, 512], F32, ph)
            xq, xqB = sb("xq", [128, 512], F32, ph)
            wov = w_o.rearrange("(kc p) n -> p kc n", p=128)
            S.dma("pool", wob[0][0][:], wov[:, :, 0:512], wob[0][1], writes=[wob[0][1]])
            for cg in range(8):
                wt, wB = wob[cg % 2]
                if cg + 1 < 8:
                    S.dma("pool", wob[(cg + 1) % 2][0][:], wov[:, :, (cg + 1) * 512:(cg + 2) * 512], wob[(cg + 1) % 2][1],
                          writes=[wob[(cg + 1) % 2][1]])
                for t in range(8):
                    pt, pB = nb()
                    for kc in range(32):
                        mm(pt[:], mT[:, kc, t * 128:(t + 1) * 128], wt[:, kc, :], pB, [wB, mTB], kc == 0, kc == 31)
                    S.dma("sp", xp[:], xo[t * 128:(t + 1) * 128, cg * 512:(cg + 1) * 512], xpB, writes=[xpB])
                    S.op("dve", lambda e, pt=pt, cg=cg: e.tensor_tensor(out=xq[:], in0=pt[:], in1=g1bc[:, cg * 512:(cg + 1) * 512], op=ALU.mult),
                         reads=[pB, g1bcB], writes=[xqB])
                    S.op("pool", lambda e: e.tensor_tensor(out=xp[:], in0=xq[:], in1=xp[:], op=ALU.add),
                         reads=[xqB, xpB], writes=[xpB])
                    S.dma("sp", X1[t * 128:(t + 1) * 128, cg * 512:(cg + 1) * 512], xp[:], xpB, reads=[xpB], writes=[scrB[id(X1)]])
        S.barrier()

        with ExitStack() as ph:
            acc, accB = sb("acc", [128, 4, D], F32, ph)
            accR = [[Buf("acc_%d_%d" % (tt, cg)) for cg in range(8)] for tt in range(4)]
            accD = [Buf("accD%d" % tt) for tt in range(4)]
            h2T, h2TB = sb("h2T", [128, 32, 512], BF16, ph)
            st2, st2B = sb("st2", [128, 2], F32, ph)
            w1v = w1.rearrange("(kc p) n -> p kc n", p=128)
            w2v = w2.rearrange("(hc p) n -> p hc n", p=128)
            for half in range(2):
                with ExitStack() as ph1:
                    norm_T(ph1, X1[half * 512:(half + 1) * 512], 4, h2T, h2TB, lambda kc: G2[:, kc:kc + 1], G2B,
                           lambda kc: modT[:, 96 + kc, 0:1], modTB, "m%d" % half)
                S.barrier()
                NHB = 64
                with ExitStack() as phw:
                    w1b = [sb("w1b%d_%d" % (i, half), [128, 32, 256], BF16, phw) for i in range(2)]
                    w2b = [sb("w2b%d_%d" % (i, half), [128, 2, D], BF16, phw) for i in range(2)]
                    aT, aTB = sb("aT%d" % half, [128, 2, 512], BF16, phw)
                    rl, rlB = sb("rl%d" % half, [128, 512], F32, phw)

                    def ldw12(hb):
                        S.dma("pool", w1b[hb % 2][0][:], w1v[:, :, hb * 256:(hb + 1) * 256], w1b[hb % 2][1], writes=[w1b[hb % 2][1]])
                        S.dma("pool", w2b[hb % 2][0][:], w2v[:, hb * 2:(hb + 1) * 2, :], w2b[hb % 2][1], writes=[w2b[hb % 2][1]])
                    ldw12(0)
                    for hb in range(NHB):
                        if hb + 1 < NHB:
                            ldw12(hb + 1)
                        w1t, w1B = w1b[hb % 2]
                        w2t, w2B = w2b[hb % 2]
                        for hc in range(2):
                            pt, pB = nb()
                            for kc in range(32):
                                mm(pt[:], w1t[:, kc, hc * 128:(hc + 1) * 128], h2T[:, kc, :], pB, [w1B, h2TB], kc == 0, kc == 31)
                            S.op("act", lambda e, pt=pt: e.activation(out=rl[:], in_=pt[:], func=AF.Relu), reads=[pB], writes=[rlB])
                            S.op("pool", lambda e, hc=hc: e.tensor_tensor(out=aT[:, hc, :], in0=rl[:], in1=rl[:], op=ALU.mult),
                                 reads=[rlB], writes=[aTB])
                        for tt in range(4):
                            for cg in range(8):
                                pt, pB = nb()
                                for hc in range(2):
                                    mm(pt[:], aT[:, hc, tt * 128:(tt + 1) * 128], w2t[:, hc, cg * 512:(cg + 1) * 512], pB,
                                       [aTB, w2B], hc == 0, hc == 1)
                                if hb == 0:
                                    S.op("act", lambda e, pt=pt, tt=tt, cg=cg: e.activation(
                                        out=acc[:, tt, cg * 512:(cg + 1) * 512], in_=pt[:], func=AF.Copy), reads=[pB], writes=[accR[tt][cg]])
                                else:
                                    S.op("dve", lambda e, pt=pt, tt=tt, cg=cg: e.tensor_tensor(
                                        out=acc[:, tt, cg * 512:(cg + 1) * 512], in0=pt[:], in1=acc[:, tt, cg * 512:(cg + 1) * 512], op=ALU.add),
                                        reads=[pB, accR[tt][cg]], writes=[accR[tt][cg]])
                S.barrier()
                with ExitStack() as ph2:
                    g2bc, g2bcB = sb("g2bc%d" % half, [128, D], F32, ph2)
                    build_bc(g2bc, g2bcB, lambda kc: modT[:, 160 + kc, 0:1], modTB)
                    fgt, fgtB = sb("fgt%d" % half, [128, D], F32, ph2)
                    S.dma("sp", fgt[:], fg_bc, fgtB, writes=[fgtB])
                    x1t, x1tB = sb("x1t%d" % half, [128, D], F32, ph2)
                    jk2, jk2B = sb("jk2%d" % half, [128, D], BF16, ph2)
                    for tt in range(4):
                        r0 = half * 512 + tt * 128
                        S.dma("sp", x1t[:], X1[r0:r0 + 128, :], x1tB, reads=[scrB[id(X1)]], writes=[x1tB])
                        S.op("dve", lambda e, tt=tt: e.tensor_tensor(out=acc[:, tt, :], in0=acc[:, tt, :], in1=g2bc[:], op=ALU.mult),
                             reads=accR[tt] + [g2bcB], writes=accR[tt])
                        S.op("pool", lambda e, tt=tt: e.tensor_tensor(out=acc[:, tt, :], in0=acc[:, tt, :], in1=x1t[:], op=ALU.add),
                             reads=accR[tt] + [x1tB], writes=accR[tt])
                        S.op("act", lambda e, tt=tt: e.activation(out=jk2[:], in_=acc[:, tt, :], func=AF.Square, accum_out=st2[:, 0:1]),
                             reads=accR[tt], writes=[jk2B, st2B])
                        S.op("dve", lambda e: e.tensor_scalar(out=st2[:, 1:2], in0=st2[:, 0:1], scalar1=1.0 / D, scalar2=EPS,
                                                              op0=ALU.mult, op1=ALU.add), reads=[st2B], writes=[st2B])
                        S.op("act", lambda e: e.activation(out=st2[:, 1:2], in_=st2[:, 1:2], func=AF.Ln), reads=[st2B], writes=[st2B])
                        S.op("act", lambda e: e.activation(out=st2[:, 1:2], in_=st2[:, 1:2], func=AF.Exp, scale=-0.5), reads=[st2B], writes=[st2B])
                        S.op("dve", lambda e, tt=tt: e.scalar_tensor_tensor(
                            out=acc[:, tt, :], in0=acc[:, tt, :], scalar=st2[:, 1:2], in1=fgt[:], op0=ALU.mult, op1=ALU.mult),
                            reads=accR[tt] + [st2B, fgtB], writes=accR[tt])
                        S.dma("sp", out[r0:r0 + 128, :], acc[:, tt, :], accD[tt], reads=accR[tt], writes=[])
                S.barrier()
        S.barrier()
        S.emit()
    return nc


_NC_CACHE = {}


def _consts():
    p = np.arange(128)
    same = (p[:, None] // 64) == (p[None, :] // 64)
    le = (same & (p[:, None] <= p[None, :])).astype(np.float32)
    gt = (same & (p[:, None] > p[None, :])).astype(np.float32)
    ge = (same & (p[:, None] >= p[None, :])).astype(np.float32)
    lt = (same & (p[:, None] < p[None, :])).astype(np.float32)
    return dict(c_ident=np.eye(128, dtype=np.float32), c_le=le, c_gt=gt, c_ge=ge, c_lt=lt,
                c_ones=np.ones((128, 128), np.float32))


def _rope_tables(pos, scale):
    n_freq = 32
    inv_freq = (np.float32(10000.0) ** (-np.arange(n_freq, dtype=np.float32) / np.float32(n_freq))).astype(np.float32)
    row = (pos // 64).astype(np.float32)
    col = (pos % 64).astype(np.float32)
    ang = np.concatenate([row[:, None] * inv_freq, col[:, None] * inv_freq], axis=-1).astype(np.float32)
    return (np.cos(ang) * np.float32(scale)).astype(np.float32), (np.sin(ang) * np.float32(scale)).astype(np.float32)


def kernel(x, c, ctx, c_ctx, w_mod, b_mod, norm1_g, norm2_g, w_in, hg_lb_logits, hg_norm_g,
           ret_decay_logit, w_branch_hgrn, w_branch_ret, w_out, w_ff1, w_ff2, final_norm_g):
    f32 = lambda a: np.ascontiguousarray(np.asarray(a, dtype=np.float32))
    x, c, ctx, c_ctx = f32(x), f32(c), f32(ctx), f32(c_ctx)
    if "nc" not in _NC_CACHE:
        _NC_CACHE["nc"] = build_program()
    nc = _NC_CACHE["nc"]
    colT = lambda v: np.ascontiguousarray(f32(v).reshape(-1, 128).T)
    shared = dict(
        w_mod=f32(w_mod)[0], b_modT=colT(f32(b_mod)[0]), n1gT=colT(f32(norm1_g)[0]), n2gT=colT(f32(norm2_g)[0]),
        fg_bc=np.ascontiguousarray(np.broadcast_to(f32(final_norm_g)[None, :], (128, D))),
        w_in=f32(w_in)[0],
        lbl=np.ascontiguousarray(np.broadcast_to(f32(hg_lb_logits)[None], (128, 2, 2, 2048))),
        hgain=np.ascontiguousarray(np.broadcast_to(f32(hg_norm_g)[0][None, :], (128, 128))),
        rdl=np.ascontiguousarray(np.broadcast_to(f32(ret_decay_logit)[0][None], (128, 2, 16))),
        w_bh=f32(w_branch_hgrn)[0], w_br=f32(w_branch_ret)[0], w_o=f32(w_out)[0], w1=f32(w_ff1)[0], w2=f32(w_ff2)[0],
    )
    shared.update(_consts())
    sk = 128.0 ** -0.5
    in_maps = []
    for core in range(8):
        b, s = core // 4, core % 4
        others = [j for j in range(4) if j != s]
        xf = np.concatenate([ctx[b]] + [x[b, j * 1024:(j + 1) * 1024] for j in others], axis=0)
        ccm = np.stack([c[b], c_ctx], axis=-1).reshape(32, 128, 2).transpose(1, 0, 2)
        cosK = np.empty((NT_ALL * 128, 64), np.float32)
        sinK = np.empty((NT_ALL * 128, 64), np.float32)
        cosK[:256] = sk
        sinK[:256] = 0.0
        for fi, j in enumerate(others):
            cs, sn = _rope_tables(np.arange(j * 1024, (j + 1) * 1024), sk)
            cosK[256 + fi * 1024:256 + (fi + 1) * 1024] = cs
            sinK[256 + fi * 1024:256 + (fi + 1) * 1024] = sn
        cs, sn = _rope_tables(np.arange(s * 1024, (s + 1) * 1024), sk)
        cosK[3328:] = cs
        sinK[3328:] = sn
        rK = np.stack([cosK.reshape(NT_ALL, 128, 64), sinK.reshape(NT_ALL, 128, 64)], axis=2).transpose(1, 0, 2, 3)
        cq, sq = _rope_tables(np.arange(s * 1024, (s + 1) * 1024), 1.0)
        rQ = np.stack([cq.reshape(8, 128, 64), sq.reshape(8, 128, 64)], axis=2).transpose(1, 0, 2, 3)
        mkv = np.ones((NT_ALL, 2), np.float32)
        for fi, j in enumerate(others):
            mkv[2 + fi * 8:2 + (fi + 1) * 8, 0] = 1.0 if j < s else 0.0
            mkv[2 + fi * 8:2 + (fi + 1) * 8, 1] = 1.0 if j > s else 0.0
        mk = np.broadcast_to(mkv[None], (128, NT_ALL, 2))
        sel = np.zeros((128, 2), np.float32)
        sel[:64, 0] = 1.0
        sel[64:, 1] = 1.0
        selm = sel[:, None, None, :] * mkv[None, :, :, None]
        wzs = np.stack([shared["w_in"][:, 2048:4096] if j < s else shared["w_in"][:, 4096:6144] for j in others], axis=0)
        m = dict(shared)
        m["wz"] = np.ascontiguousarray(wzs)
        m.update(xo=np.ascontiguousarray(x[b, s * 1024:(s + 1) * 1024]), xf=np.ascontiguousarray(xf),
                 cc=np.ascontiguousarray(ccm), ropeK=np.ascontiguousarray(rK), ropeQ=np.ascontiguousarray(rQ),
                 mk=np.ascontiguousarray(mk), selm=np.ascontiguousarray(selm.astype(np.float32)))
        in_maps.append(m)
    res = run_bass_kernel_spmd(nc, in_maps, core_ids=list(range(8)))
    outp = np.empty((2, 4096, D), np.float32)
    for core in range(8):
        b, s = core // 4, core % 4
        outp[b, s * 1024:(s + 1) * 1024] = res.results[core]["out"]
    return outp
```
